# Optimizing a Trainium2 kernel written in Bass

```python
import math
import jax, jax.numpy as jnp
from jax import lax
import numpy as np

D_MODEL = 1024
BATCH = 1
SEQ = 16384
DEPTH = 4
DEC_BATCH = 16
DEC_SEQ = 4096
PAST_LEN = 128

ATTN_GROUPS = ((128, 1), (512, 4), (2048, 16))
N_GROUPS = 3
HEADS_PER_GROUP = 8
HEAD_DIM = 64
N_ATTN_HEADS = N_GROUPS * HEADS_PER_GROUP
ATTN_QK_WIDTH = N_ATTN_HEADS * HEAD_DIM
ATTN_V_WIDTH = HEADS_PER_GROUP * HEAD_DIM
M_WIDTH = D_MODEL
M_HEADS = 4
M_HEAD_DIM = M_WIDTH // M_HEADS
M_CHUNK = 128
CONV_WIDTH = 5
FFN_HIDDEN = int(math.ceil(8 * D_MODEL / 3 / 256)) * 256
IN_SIZES = (ATTN_QK_WIDTH, ATTN_QK_WIDTH, ATTN_V_WIDTH, 2 * M_WIDTH, M_WIDTH, M_WIDTH, 4 * M_HEADS, 2 * D_MODEL)
IN_SPLITS = tuple(int(c) for c in np.cumsum(IN_SIZES)[:-1])
N_IN = int(sum(IN_SIZES))
EPS = 1e-6

kernel_name = "hybrid_dilated_attn_mlstm_encoder"


def alibi_slopes():
    h = np.arange(1, N_ATTN_HEADS + 1, dtype=np.float32)
    return (2.0 ** (-8.0 * h / N_ATTN_HEADS)).astype(np.float32).reshape(N_GROUPS, HEADS_PER_GROUP)


def rms_norm(x, g):
    xf = x.astype(jnp.float32)
    y = xf * lax.rsqrt(jnp.mean(xf * xf, axis=-1, keepdims=True) + EPS)
    return (y * g.astype(jnp.float32)).astype(x.dtype)


def dilated_window_attention(q, k, v, slopes, dilation, half_win):
    b, h, s, hd = q.shape
    sr = s // dilation
    blk = half_win
    nb = -(-sr // blk)
    extra = nb * blk - sr

    def to_residue(t):
        return t.reshape(b, h, sr, dilation, hd).swapaxes(2, 3)

    def windows(t):
        tp = jnp.pad(t, ((0, 0), (0, 0), (0, 0), (half_win, half_win + extra), (0, 0)))
        tp = tp.reshape(b, h, dilation, nb + 2, blk, hd)
        return jnp.concatenate([tp[:, :, :, :-2], tp[:, :, :, 1:-1], tp[:, :, :, 2:]], axis=-2)

    qb = jnp.pad(to_residue(q), ((0, 0), (0, 0), (0, 0), (0, extra), (0, 0))).reshape(b, h, dilation, nb, blk, hd)
    kw = windows(to_residue(k))
    vw = windows(to_residue(v))
    scores = jnp.einsum('bhcnqd,bhcnkd->bhcnqk', qb, kw) * (hd ** -0.5)
    qi = jnp.arange(blk)[:, None]
    ki = jnp.arange(3 * blk)[None, :]
    step = ki - half_win - qi
    key_idx = jnp.arange(nb)[:, None, None] * blk - half_win + ki[None]
    valid = (jnp.abs(step) <= half_win)[None] & (key_idx >= 0) & (key_idx < sr)
    alibi = -jnp.asarray(slopes)[:, None, None] * (jnp.abs(step) * dilation).astype(jnp.float32)[None]
    scores = jnp.where(valid, scores + alibi[:, None, None], -jnp.inf)
    lse = jax.nn.logsumexp(scores, axis=-1)
    out = jnp.einsum('bhcnqk,bhcnkd->bhcnqd', jnp.exp(scores - lse[..., None]), vw)
    out = out.reshape(b, h, dilation, nb * blk, hd)[:, :, :, :sr]
    lse = lse.reshape(b, h, dilation, nb * blk)[:, :, :, :sr]
    return out.swapaxes(2, 3).reshape(b, h, s, hd), lse.swapaxes(2, 3).reshape(b, h, s)


def attention_branch(aq, ak, av):
    b, s = aq.shape[0], aq.shape[1]
    slopes = alibi_slopes()
    v = av.astype(jnp.float32).transpose(0, 2, 1, 3)
    outs, lses = [], []
    for g, (window, dilation) in enumerate(ATTN_GROUPS):
        q = aq[:, :, g].astype(jnp.float32).transpose(0, 2, 1, 3)
        k = ak[:, :, g].astype(jnp.float32).transpose(0, 2, 1, 3)
        o, l = dilated_window_attention(q, k, v, slopes[g], dilation, window // (2 * dilation))
        outs.append(o)
        lses.append(l)
    wts = jax.nn.softmax(jnp.stack(lses, axis=0), axis=0)
    out = jnp.einsum('gbhs,gbhsd->bhsd', wts, jnp.stack(outs, axis=0))
    return out.transpose(0, 2, 1, 3).reshape(b, s, ATTN_V_WIDTH).astype(aq.dtype)


def mlstm_direction(q, k, v, log_i, log_f):
    b, h, s, d = q.shape
    nc = s // M_CHUNK

    def chunks(t):
        return jnp.moveaxis(t.reshape(b, h, nc, M_CHUNK, *t.shape[3:]), 2, 0)

    lower = jnp.tril(jnp.ones((M_CHUNK, M_CHUNK), dtype=bool))

    def step(carry, xs):
        c_prev, n_prev, m_prev = carry
        qc, kc, vc, li, lf = xs
        cum = jnp.cumsum(lf, axis=-1)
        dmat = jnp.where(lower, cum[..., :, None] - cum[..., None, :] + li[..., None, :], -jnp.inf)
        m_inter = cum + m_prev[..., None]
        m_t = jnp.maximum(m_inter, jnp.max(dmat, axis=-1))
        w_inter = jnp.exp(m_inter - m_t)
        sc = jnp.einsum('bhtd,bhsd->bhts', qc, kc) * jnp.exp(dmat - m_t[..., None])
        num = jnp.einsum('bhts,bhsd->bhtd', sc, vc) + w_inter[..., None] * jnp.einsum('bhtk,bhkv->bhtv', qc, c_prev)
        den = jnp.sum(sc, axis=-1) + w_inter * jnp.einsum('bhtk,bhk->bht', qc, n_prev)
        h_out = num / jnp.maximum(jnp.abs(den), jnp.exp(-m_t))[..., None]
        tot = cum[..., -1]
        a = tot[..., None] - cum + li
        m_new = jnp.maximum(tot + m_prev, jnp.max(a, axis=-1))
        w_k = jnp.exp(a - m_new[..., None])
        decay = jnp.exp(tot + m_prev - m_new)
        c_new = decay[..., None, None] * c_prev + jnp.einsum('bhs,bhsk,bhsv->bhkv', w_k, kc, vc)
        n_new = decay[..., None] * n_prev + jnp.einsum('bhs,bhsk->bhk', w_k, kc)
        return (c_new, n_new, m_new), h_out

    init = (jnp.zeros((b, h, d, d), jnp.float32), jnp.zeros((b, h, d), jnp.float32), jnp.zeros((b, h), jnp.float32))
    _, hs = lax.scan(step, init, (chunks(q), chunks(k), chunks(v), chunks(log_i), chunks(log_f)))
    return jnp.moveaxis(hs, 0, 2).reshape(b, h, s, d)


def mlstm_branch(mq, mk, mv, mo, mg, g_norm):
    b, s, _ = mq.shape

    def heads(t):
        return t.astype(jnp.float32).reshape(b, s, M_HEADS, M_HEAD_DIM).transpose(0, 2, 1, 3)

    qh = heads(mq) * (M_HEAD_DIM ** -0.5)
    kh = heads(mk)
    vh = heads(mv)
    g = mg.astype(jnp.float32).reshape(b, s, 4, M_HEADS).transpose(2, 0, 3, 1)
    h_fwd = mlstm_direction(qh, kh, vh, g[0], jax.nn.log_sigmoid(g[1]))
    rev = lambda t: jnp.flip(t, axis=2)
    h_bwd = rev(mlstm_direction(rev(qh), rev(kh), rev(vh), rev(g[2]), rev(jax.nn.log_sigmoid(g[3]))))
    hsum = h_fwd + h_bwd
    hsum = hsum * lax.rsqrt(jnp.mean(hsum * hsum, axis=-1, keepdims=True) + EPS)
    hsum = hsum.transpose(0, 2, 1, 3).reshape(b, s, M_WIDTH) * g_norm.astype(jnp.float32)
    return (jax.nn.sigmoid(mo.astype(jnp.float32)) * hsum).astype(mq.dtype)


def centred_depthwise_conv(x, w, bias):
    y = lax.conv_general_dilated(x, w[:, None, :], window_strides=(1,),
                                 padding=[(CONV_WIDTH // 2, CONV_WIDTH // 2)],
                                 dimension_numbers=('NWC', 'WIO', 'NWC'),
                                 feature_group_count=x.shape[-1])
    return y + bias


def encoder_trunk(x, norm_mix_pre, norm_mix_post, norm_ffn_pre, norm_ffn_post, w_in, b_mlstm_gates,
                  w_conv, b_conv, g_mlstm_norm, w_attn_proj, w_mlstm_proj, w_out, w_ffn_in, w_ffn_out):
    b, s, _ = x.shape
    for l in range(DEPTH):
        xn = rms_norm(x, norm_mix_pre[l])
        proj = xn @ w_in[l]
        aq, ak, av, mqk, mv, mo, mg, bg = jnp.split(proj, IN_SPLITS, axis=-1)
        aq = aq.reshape(b, s, N_GROUPS, HEADS_PER_GROUP, HEAD_DIM)
        ak = ak.reshape(b, s, N_GROUPS, HEADS_PER_GROUP, HEAD_DIM)
        av = av.reshape(b, s, HEADS_PER_GROUP, HEAD_DIM)
        attn_out = attention_branch(aq, ak, av)
        mqk = jax.nn.silu(centred_depthwise_conv(mqk, w_conv[l], b_conv[l]))
        mq, mk = jnp.split(mqk, 2, axis=-1)
        mlstm_out = mlstm_branch(mq, mk, mv, mo, mg + b_mlstm_gates[l], g_mlstm_norm[l])
        gate_a, gate_m = jnp.split(jax.nn.sigmoid(bg), 2, axis=-1)
        merged = gate_a * (attn_out @ w_attn_proj[l]) + gate_m * (mlstm_out @ w_mlstm_proj[l])
        x = x + rms_norm(merged @ w_out[l], norm_mix_post[l])
        xn = rms_norm(x, norm_ffn_pre[l])
        gate, up = jnp.split(xn @ w_ffn_in[l], 2, axis=-1)
        x = x + rms_norm((jax.nn.silu(gate) * up) @ w_ffn_out[l], norm_ffn_post[l])
    return x


def setup_inputs(seed: int = 0) -> dict:
    key = jax.random.key(seed)
    ks = jax.random.split(key, 18)
    nrm = jax.random.normal
    gain = lambda k, n: 1.0 + 0.02 * nrm(k, (DEPTH, n), jnp.float32)
    fbias = jax.random.uniform(ks[7], (DEPTH, 4, M_HEADS), jnp.float32, minval=3.0, maxval=6.0)
    ibias = 0.1 * nrm(ks[8], (DEPTH, 4, M_HEADS), jnp.float32)
    is_forget = jnp.array([False, True, False, True])[None, :, None]
    return {
        'x_prompt': nrm(ks[0], (BATCH, SEQ, D_MODEL), jnp.float32),
        'x_sample': nrm(ks[1], (DEC_BATCH, DEC_SEQ, D_MODEL), jnp.float32),
        'norm_mix_pre': gain(ks[2], D_MODEL),
        'norm_mix_post': gain(ks[3], D_MODEL),
        'norm_ffn_pre': gain(ks[4], D_MODEL),
        'norm_ffn_post': gain(ks[5], D_MODEL),
        'w_in': nrm(ks[6], (DEPTH, D_MODEL, N_IN), jnp.float32) * D_MODEL ** -0.5,
        'b_mlstm_gates': jnp.where(is_forget, fbias, ibias).reshape(DEPTH, 4 * M_HEADS),
        'w_conv': nrm(ks[9], (DEPTH, CONV_WIDTH, 2 * M_WIDTH), jnp.float32) * CONV_WIDTH ** -0.5,
        'b_conv': 0.02 * nrm(ks[10], (DEPTH, 2 * M_WIDTH), jnp.float32),
        'g_mlstm_norm': gain(ks[11], M_WIDTH),
        'w_attn_proj': nrm(ks[12], (DEPTH, ATTN_V_WIDTH, D_MODEL), jnp.float32) * ATTN_V_WIDTH ** -0.5,
        'w_mlstm_proj': nrm(ks[13], (DEPTH, M_WIDTH, D_MODEL), jnp.float32) * M_WIDTH ** -0.5,
        'w_out': nrm(ks[14], (DEPTH, D_MODEL, D_MODEL), jnp.float32) * D_MODEL ** -0.5,
        'w_ffn_in': nrm(ks[15], (DEPTH, D_MODEL, 2 * FFN_HIDDEN), jnp.float32) * D_MODEL ** -0.5,
        'w_ffn_out': nrm(ks[16], (DEPTH, FFN_HIDDEN, D_MODEL), jnp.float32) * FFN_HIDDEN ** -0.5,
    }


def reference(x_prompt, x_sample, norm_mix_pre, norm_mix_post, norm_ffn_pre, norm_ffn_post, w_in, b_mlstm_gates,
              w_conv, b_conv, g_mlstm_norm, w_attn_proj, w_mlstm_proj, w_out, w_ffn_in, w_ffn_out):
    y_prompt = encoder_trunk(x_prompt, norm_mix_pre, norm_mix_post, norm_ffn_pre, norm_ffn_post, w_in, b_mlstm_gates,
                             w_conv, b_conv, g_mlstm_norm, w_attn_proj, w_mlstm_proj, w_out, w_ffn_in, w_ffn_out)
    y_sample = encoder_trunk(x_sample, norm_mix_pre, norm_mix_post, norm_ffn_pre, norm_ffn_post, w_in, b_mlstm_gates,
                             w_conv, b_conv, g_mlstm_norm, w_attn_proj, w_mlstm_proj, w_out, w_ffn_in, w_ffn_out)
    return (y_prompt, y_sample)
```

```python
import math
from contextlib import ExitStack

import numpy as np
import concourse.bass as bass
import concourse.mybir as mybir
from concourse.bass_utils import run_bass_kernel_spmd

F32 = mybir.dt.float32
BF16 = mybir.dt.bfloat16
AF = mybir.ActivationFunctionType
ALU = mybir.AluOpType

ENGS = ("pe", "act", "dve", "pool", "sp")
D = 1024
NIN = 9744
FH = 2816
EPS = 1e-6
LN16 = math.log(16.0)


class Buf:
    __slots__ = ("name", "w", "r")

    def __init__(self, name=""):
        self.name = name
        self.w = []
        self.r = []


class Ins:
    __slots__ = ("eng", "fn", "deps", "inc", "val", "dma", "dsem", "dval", "idx")

    def __init__(self, eng, fn, dma):
        self.eng = eng
        self.fn = fn
        self.deps = []
        self.inc = False
        self.val = 0
        self.dma = dma
        self.dsem = -1
        self.dval = 0
        self.idx = -1


def _reduce(lst):
    best = {}
    for d in lst:
        key = (d.eng, d.dsem) if d.dma else (d.eng, -1)
        cur = best.get(key)
        if cur is None or (d.dval > cur.dval if d.dma else d.idx > cur.idx):
            best[key] = d
    return list(best.values())


class Prog:
    NDSEM = 8

    def __init__(self, nc):
        self.nc = nc
        self.lists = {e: [] for e in ENGS}
        self.dma_count = {e: 0 for e in ENGS}

    def op(self, eng, fn, reads=(), writes=(), pwrites=(), dma=False):
        ins = Ins(eng, fn, dma)
        deps = []
        for b in reads:
            deps.extend(b.w)
        for b in writes:
            deps.extend(b.w)
            deps.extend(b.r)
        for b in pwrites:
            deps.extend(b.r)
        lst = self.lists[eng]
        ins.idx = len(lst)
        if dma:
            n = self.dma_count[eng]
            self.dma_count[eng] = n + 1
            ins.dsem = n % self.NDSEM
            ins.dval = 16 * (n // self.NDSEM + 1)
        final = []
        for d in _reduce(deps):
            if (not d.dma) and d.eng == "pe" and eng == "pe" and not dma:
                continue
            if not d.dma:
                d.inc = True
            final.append(d)
        ins.deps = final
        lst.append(ins)
        for b in writes:
            b.w = [ins]
            b.r = []
        for b in pwrites:
            b.w = _reduce(b.w + [ins])
        for b in reads:
            b.r = _reduce(b.r + [ins])
        return ins

    def emit(self):
        nc = self.nc
        with ExitStack() as st:
            esem = {e: st.enter_context(nc.semaphore("s_" + e)) for e in ENGS}
            dsem = {e: [st.enter_context(nc.semaphore("d_%s%d" % (e, i))) for i in range(self.NDSEM)]
                    for e in ("sp", "act", "pool")}
            for e in ENGS:
                c = 0
                for ins in self.lists[e]:
                    if ins.inc and not ins.dma:
                        c += 1
                        ins.val = c
            block = st.enter_context(nc.Block())
            last = {}
            for e in ("sp", "act", "pool"):
                for ins in self.lists[e]:
                    if ins.dma:
                        last[(e, ins.dsem)] = ins.dval

            def run(ename, eng):
                waited = {}

                def wait(key, sem, val):
                    if waited.get(key, 0) < val:
                        eng.wait_ge(sem, val)
                        waited[key] = val

                for ins in self.lists[ename]:
                    for d in ins.deps:
                        if d.dma:
                            wait(("d", d.eng, d.dsem), dsem[d.eng][d.dsem], d.dval)
                        else:
                            wait(("e", d.eng), esem[d.eng], d.val)
                    if ins.dma:
                        if ins.dval > 16:
                            wait(("d", ename, ins.dsem), dsem[ename][ins.dsem], ins.dval - 16)
                        ins.fn(eng).then_inc(dsem[ename][ins.dsem], 16)
                    else:
                        h = ins.fn(eng)
                        if ins.inc:
                            h.then_inc(esem[ename], 1)
                if ename == "sp":
                    for (e, k), v in last.items():
                        wait(("d", e, k), dsem[e][k], v)

            block.tensor(lambda eng: run("pe", eng))
            block.scalar(lambda eng: run("act", eng))
            block.vector(lambda eng: run("dve", eng))
            block.gpsimd(lambda eng: run("pool", eng))
            block.sync(lambda eng: run("sp", eng))


class Arena:
    def __init__(self, t, n):
        self.t = t
        self.n = n
        self.off = 0
        self.bufs = []
        self.pending = []

    def reset(self):
        pend = list(self.pending)
        for b in self.bufs:
            pend.extend(b.w)
            pend.extend(b.r)
        self.pending = _reduce(pend)
        self.bufs = []
        self.off = 0

    def carve(self, n, name=""):
        assert self.off + n <= self.n, (name, self.off, n, self.n)
        ap = self.t[:, self.off:self.off + n]
        self.off += n
        b = Buf(name)
        b.r = list(self.pending)
        self.bufs.append(b)
        return ap, b


def alibi_slopes(ng):
    h = np.arange(1, ng * 8 + 1, dtype=np.float32)
    return (2.0 ** (-8.0 * h / (ng * 8))).astype(np.float32).reshape(ng, 8)


def make_consts(dils):
    ng = len(dils)
    k = np.arange(128)[:, None]
    q = np.arange(128)[None, :]
    ident = (k == q).astype(np.float32)
    mu = (k <= q).astype(np.float32)
    ml = (k >= q).astype(np.float32)
    ones = np.ones((128, 128), np.float32)
    sl = alibi_slopes(ng)
    mt = np.zeros((128, ng, 2, 8, 128), np.float32)
    for g, d in enumerate(dils):
        for j in range(2):
            step = 128 * j - 64 + k - q
            valid = np.abs(step) <= 64
            for hi, h in enumerate([0, 2, 4, 6, 1, 3, 5, 7]):
                mt[:, g, j, hi, :] = np.where(valid, np.exp(-sl[g, h] * d * np.abs(step).astype(np.float32)), 0.0)
    return np.concatenate([ident, mu, ml, ones, mt.reshape(128, -1)], axis=1).astype(np.float32)


def build(seqs, depth, dils, dbg=None, bound=None):
    nc = bass.Bass("TRN2", target_bir_lowering=False)
    P = Prog(nc)
    st = ExitStack()
    ng = len(dils)
    SMAX = max(seqs)
    L = depth

    def dram(name, shape, dt, kind="Internal"):
        return nc.dram_tensor(name, list(shape), dt, kind=kind).ap()

    xin = [dram("x%d" % i, [s, D], F32, "ExternalInput") for i, s in enumerate(seqs)]
    yout = [dram("y%d" % i, [s, D], F32, "ExternalOutput") for i, s in enumerate(seqs)]
    w_in = dram("w_in", [L, D, NIN], F32, "ExternalInput")
    w_ap = dram("w_attn_proj", [L, 512, D], F32, "ExternalInput")
    w_mp = dram("w_mlstm_proj", [L, D, D], F32, "ExternalInput")
    w_o = dram("w_out", [L, D, D], F32, "ExternalInput")
    w_f1 = dram("w_ffn_in", [L, D, 2 * FH], F32, "ExternalInput")
    w_f2 = dram("w_ffn_out", [L, FH, D], F32, "ExternalInput")
    gains_d = dram("gains", [L, 128, 32], F32, "ExternalInput")
    bconv_d = dram("bconv", [L, 128, 16], F32, "ExternalInput")
    wconv_d = dram("wconv", [L, 128, 80], F32, "ExternalInput")
    gnorm_d = dram("gnorm", [L, D], F32, "ExternalInput")
    gbias_d = dram("gbias", [L, 16], F32, "ExternalInput")
    NCONST = 4 * 128 + ng * 2 * 8 * 128
    const_d = dram("consts", [128, NCONST], F32, "ExternalInput")
    keep_d = dram("keep", [128, 1], F32, "ExternalInput")

    xT = dram("xT", [D, SMAX], F32)
    aqkT = dram("aqkT", [2 * ng * 512, SMAX], BF16)
    vaug_d = dram("vaug", [SMAX, 520], BF16)
    mpre = dram("mpre", [2048, SMAX], BF16)
    mqkT = dram("mqkT", [2048, SMAX], BF16)
    mvaug_d = dram("mvaug", [SMAX, 1028], BF16)
    mo_d = dram("mo", [SMAX, D], BF16)
    gates_d = dram("gates", [SMAX, 16], F32)
    bgT = dram("bgT", [2048, SMAX], BF16)
    og_d = [dram("og%d" % g, [SMAX, 520], F32) for g in range(ng)]
    h_d = [dram("hdir%d" % i, [SMAX, D], F32) for i in range(2)]
    B_xT, B_aqk, B_vaug, B_mpre, B_mqk, B_mvaug, B_mo, B_gates, B_bg = [Buf(n) for n in
        "xT aqk vaug mpre mqk mvaug mo gates bg".split()]
    B_og = [Buf("og%d" % g) for g in range(ng)]
    B_h = [Buf("h0"), Buf("h1")]
    B_in = Buf("in")
    B_out = Buf("out")

    def sb(name, shape, dt):
        return st.enter_context(nc.sbuf_tensor(name, list(shape), dt))

    cf = sb("cf", [128, 512], F32)
    cb = sb("cb", [128, NCONST], BF16)
    B_c = Buf("c")
    P.op("sp", lambda e: e.dma_start(out=cf[:], in_=const_d[:, 0:512]), writes=[B_c], dma=True)
    P.op("pool", lambda e: e.dma_start(out=cb[:], in_=const_d[:, :]), pwrites=[B_c], dma=True)
    ident_f, maskU_f, maskL_f, ones_f = (cf[:, i * 128:(i + 1) * 128] for i in range(4))
    keep = sb("keeps", [128, 1], F32)
    P.op("sp", lambda e: e.dma_start(out=keep[:], in_=keep_d[:, :]), pwrites=[B_c], dma=True)
    ident_b = cb[:, 0:128]
    ones_b = cb[:, 384:512]

    def mtab(g, j, h0, nh):
        o = 512 + ((g * 2 + j) * 8 + h0) * 128
        return cb[:, o:o + nh * 128]

    gains = sb("gainss", [128, 32], F32)
    bconv = sb("bconvs", [128, 16], F32)
    wconv = sb("wconvs", [128, 80], F32)
    gnorm = sb("gnorms", [128, D], F32)
    gbias = sb("gbiass", [128, 16], F32)
    B_par = Buf("par")

    NW = 3
    wt = [sb("wt%d" % i, [128, 4096], BF16) for i in range(NW)]
    B_wt = [Buf("wt%d" % i) for i in range(NW)]
    wctr = [0]
    _wsrc_buf = []

    def load_w(src3):
        i = wctr[0] % NW
        wctr[0] += 1
        nk, ncol = src3.shape[1], src3.shape[2]
        dst = wt[i][:, 0:nk * ncol].rearrange("p (k c) -> p k c", k=nk)
        P.op("pool", lambda e: e.dma_start(out=dst, in_=src3), reads=list(_wsrc_buf), writes=[B_wt[i]], dma=True)
        return dst, B_wt[i]

    def wview(w2, r0, nk, c0, ncol):
        return w2[r0:r0 + nk * 128, c0:c0 + ncol].rearrange("(k p) c -> p k c", p=128)


    NF = 17024
    NB = 34816
    fa = Arena(sb("fa", [128, NF], F32), NF)
    ba = Arena(sb("ba", [128, NB], BF16), NB)
    PS = []
    for i in range(8):
        t = st.enter_context(nc.psum_tensor("ps%d" % i, [128, 512], F32))
        PS.append((t, Buf("ps%d" % i)))
    prr = [0]

    def psn(lo=0, hi=8):
        i = lo + prr[0] % (hi - lo)
        prr[0] += 1
        return PS[i]

    def new_phase():
        fa.reset()
        ba.reset()

    def MM(o, l, r, s, t, rd, wr):
        P.op("pe", lambda e: e.matmul(o, lhsT=l, rhs=r, start=s, stop=t), reads=rd, writes=wr)

    def TR(o, i, idn, rd, wr):
        P.op("pe", lambda e: e.transpose(o, i, idn), reads=rd + [B_c], writes=wr)

    def ACT(o, i, f, rd, wr, bias=None, scale=None, accum=None, eng="act"):
        kw = {}
        if bias is not None:
            kw["bias"] = bias
        if scale is not None:
            kw["scale"] = scale
        if accum is not None:
            kw["accum_out"] = accum
        P.op("act", lambda e: e.activation(out=o, in_=i, func=f, **kw), reads=rd, writes=wr)

    def TT(eng, o, a, b, op, rd, wr):
        P.op(eng, lambda e: e.tensor_tensor(out=o, in0=a, in1=b, op=op), reads=rd, writes=wr)

    def TS(eng, o, a, s1, op0, rd, wr, s2=None, op1=None):
        if op1 is None:
            P.op(eng, lambda e: e.tensor_scalar(out=o, in0=a, scalar1=s1, scalar2=None, op0=op0), reads=rd, writes=wr)
        else:
            P.op(eng, lambda e: e.tensor_scalar(out=o, in0=a, scalar1=s1, scalar2=s2, op0=op0, op1=op1),
                 reads=rd, writes=wr)

    def STT(o, a, s, b, op0, op1, rd, wr):
        P.op("dve", lambda e: e.scalar_tensor_tensor(out=o, in0=a, scalar=s, in1=b, op0=op0, op1=op1),
             reads=rd, writes=wr)

    def CP(eng, o, i, rd, wr):
        P.op(eng, lambda e: e.tensor_copy(out=o, in_=i), reads=rd, writes=wr)

    def RCP(o, i, rd, wr):
        P.op("dve", lambda e: e.reciprocal(out=o, in_=i), reads=rd, writes=wr)

    def MS(eng, o, v, wr, partial=False):
        if partial:
            P.op(eng, lambda e: e.memset(o, v), pwrites=wr)
        else:
            P.op(eng, lambda e: e.memset(o, v), writes=wr)

    def srange(base, n, step):
        return slice(base, base + step * (n - 1) + 1, step)

    def LD(o, i, rd, wr, eng="sp"):
        P.op(eng, lambda e: e.dma_start(out=o, in_=i), reads=rd, writes=wr, dma=True)

    def STO(o, i, rd, pw, eng="sp"):
        P.op(eng, lambda e: e.dma_start(out=o, in_=i), reads=rd, pwrites=pw, dma=True)

    def rstd_from_ssq(ps_ap, psb, n, tmp, tb, out, ob, width):
        TS("dve", tmp, ps_ap, 1.0 / n, ALU.mult, [psb], [tb], EPS, ALU.add)
        ACT(tmp, tmp, AF.Sqrt, [tb], [tb])
        RCP(out, tmp, [tb], [ob])

    def fm_norm(x3, xb, gcol, sq3, sqb, xn3, xnb, rs, rsb, tmp, tb, src_for_sq=None):
        for k in range(8):
            ACT(sq3[:, k, :], x3[:, k, :], AF.Square, [xb], [sqb])
        pt, pb = psn()
        for k in range(8):
            MM(pt[:, :], ones_b, sq3[:, k, :], k == 0, k == 7, [sqb, B_c], [pb])
        rstd_from_ssq(pt[:, :], pb, float(D), tmp, tb, rs, rsb, 512)
        if xn3 is not None:
            for k in range(8):
                STT(xn3[:, k, :], x3[:, k, :], gains[:, gcol + k:gcol + k + 1], rs, ALU.mult, ALU.mult,
                    [xb, rsb, B_par], [xnb])

    B_wbf = Buf("wbf")
    wbf = {}
    for (wname, wsrc, K_, N_) in (("in", w_in, D, NIN), ("ap", w_ap, 512, D), ("mp", w_mp, D, D), ("o", w_o, D, D),
                                  ("f1", w_f1, D, 2 * FH), ("f2", w_f2, FH, D)):
        wdst = dram("wbf_" + wname, [L, K_, N_], BF16)
        wbf[wname] = wdst
        for l_ in range(L):
            for k0 in range(0, K_ // 128, 8):
                nk = min(8, K_ // 128 - k0)
                for c0 in range(0, N_, 512):
                    ncol = min(512, N_ - c0)
                    t3, tb = load_w(wview(wsrc[l_], k0 * 128, nk, c0, ncol))
                    STO(wview(wdst[l_], k0 * 128, nk, c0, ncol), t3, [tb], [B_wbf])
    w_in, w_ap, w_mp, w_o, w_f1, w_f2 = (wbf[n_] for n_ in ("in", "ap", "mp", "o", "f1", "f2"))
    _wsrc_buf.append(B_wbf)

    class _Stop(Exception):
        pass

    def chk(name):
        if dbg == name:
            raise _Stop()

    def _main_body():
        for si, S in enumerate(seqs):
            NT = S // 512
            new_phase()
            xa2 = [fa.carve(4096, "xa") for _ in range(2)]
            xs2 = [fa.carve(4096, "xs") for _ in range(2)]
            for it in range(NT):
                t0 = it * 512
                xa, xab = xa2[it % 2]
                xs, xsb = xs2[it % 2]
                xa3 = xa.rearrange("p (j c) -> p j c", j=4)
                xs3 = xs.rearrange("p (k t) -> p k t", k=8)
                LD(xa3, xin[si][t0:t0 + 512, :].rearrange("(j p) c -> p j c", p=128), [B_in], [xab])
                for k in range(8):
                    pt, pb = psn()
                    for j in range(4):
                        TR(pt[:, j * 128:(j + 1) * 128], xa3[:, j, k * 128:(k + 1) * 128], ident_f, [xab], [pb])
                    if k % 2:
                        CP("dve", xs3[:, k, :], pt[:, :], [pb], [xsb])
                    else:
                        ACT(xs3[:, k, :], pt[:, :], AF.Copy, [pb], [xsb])
                STO(xT.rearrange("(k p) s -> p k s", p=128)[:, :, t0:t0 + 512], xs3, [xsb], [B_xT])

            chk('pro')
            for l in range(L):
                LD(gains[:], gains_d[l], [], [B_par])
                LD(bconv[:], bconv_d[l], [], [B_par])
                LD(wconv[:], wconv_d[l], [], [B_par])
                LD(gnorm[:], gnorm_d[l:l + 1, :].partition_broadcast(128), [], [B_par])
                LD(gbias[:], gbias_d[l:l + 1, :].partition_broadcast(128), [], [B_par])
                wl = w_in[l]

                new_phase()
                xaA = [fa.carve(4096, "xaA") for _ in range(2)]
                rsA = fa.carve(512, "rsA")
                tmA = fa.carve(512, "tmA")
                gtA = fa.carve(64, "gtA")
                sqA = ba.carve(4096, "sqA")
                xnA = ba.carve(4096, "xnA")
                fmA = [ba.carve(2048, "fmA") for _ in range(2)]
                vaA = ba.carve(4 * 520, "vaA")
                mvA = ba.carve(4 * 1028, "mvA")
                moA = ba.carve(4096, "moA")
                MS("pool", vaA[0], 1.0, [vaA[1]])
                MS("pool", mvA[0], 1.0, [mvA[1]])
                fm_groups = []
                for c0 in range(0, 2 * ng * 512, 512):
                    fm_groups.append((c0, aqkT, c0, B_aqk))
                for c0 in range(0, 2048, 512):
                    fm_groups.append((3584 + c0, mpre, c0, B_mpre))
                for c0 in range(0, 2048, 512):
                    fm_groups.append((7696 + c0, bgT, c0, B_bg))
                fctr = 0
                for it in range(NT):
                    t0 = it * 512
                    xa, xab = xaA[it % 2]
                    x3 = xa.rearrange("p (k t) -> p k t", k=8)
                    LD(x3, xT.rearrange("(k p) s -> p k s", p=128)[:, :, t0:t0 + 512], [B_xT], [xab])
                    sq3 = sqA[0].rearrange("p (k t) -> p k t", k=8)
                    xn3 = xnA[0].rearrange("p (k t) -> p k t", k=8)
                    fm_norm(x3, xab, 0, sq3, sqA[1], xn3, xnA[1], rsA[0], rsA[1], tmA[0], tmA[1])
                    for (wc, dst, r0, dbuf) in fm_groups:
                        w3, wb = load_w(wview(wl, 0, 8, wc, 512))
                        fm, fmb = fmA[fctr % 2]
                        fctr += 1
                        fm3 = fm.rearrange("p (o t) -> p o t", o=4)
                        for oc in range(4):
                            pt, pb = psn()
                            for k in range(8):
                                MM(pt[:, :], w3[:, k, oc * 128:(oc + 1) * 128], xn3[:, k, :], k == 0, k == 7,
                                   [wb, xnA[1]], [pb])
                            if oc % 2:
                                CP("dve", fm3[:, oc, :], pt[:, :], [pb], [fmb])
                            else:
                                ACT(fm3[:, oc, :], pt[:, :], AF.Copy, [pb], [fmb])
                        STO(dst[r0:r0 + 512, t0:t0 + 512].rearrange("(o p) s -> p o s", p=128), fm3, [fmb], [dbuf])
                    va4 = vaA[0].rearrange("p (j h e) -> p j h e", j=4, h=8)
                    mv4 = mvA[0].rearrange("p (j h e) -> p j h e", j=4, h=4)
                    mo3 = moA[0].rearrange("p (j c) -> p j c", j=4)
                    gt3 = gtA[0].rearrange("p (j c) -> p j c", j=4)
                    tm_groups = [(3072, 512, "av", 0), (5632, 512, "mv", 0), (6144, 512, "mv", 1),
                                 (6656, 512, "mo", 0), (7168, 512, "mo", 1), (7680, 16, "mg", 0)]
                    for (wc, ncol, kind, half) in tm_groups:
                        w3, wb = load_w(wview(wl, 0, 8, wc, ncol))
                        for j in range(4):
                            pt, pb = psn()
                            for k in range(8):
                                MM(pt[:, 0:ncol], xn3[:, k, j * 128:(j + 1) * 128], w3[:, k, :], k == 0, k == 7,
                                   [wb, xnA[1]], [pb])
                            if kind == "av":
                                CP("dve", va4[:, j, :, 0:64], pt[:, :].rearrange("p (h e) -> p h e", h=8), [pb], [vaA[1]])
                            elif kind == "mv":
                                ACT(mv4[:, j, 2 * half:2 * half + 2, 0:256], pt[:, :].rearrange("p (h e) -> p h e", h=2),
                                    AF.Copy, [pb], [mvA[1]])
                            elif kind == "mo":
                                CP("dve", mo3[:, j, half * 512:(half + 1) * 512], pt[:, :], [pb], [moA[1]])
                            else:
                                TT("dve", gt3[:, j, :], pt[:, 0:16], gbias[:, :], ALU.add, [pb, B_par], [gtA[1]])
                    rows = lambda d_, w_: d_[t0:t0 + 512, :].rearrange("(j p) c -> p j c", p=128)
                    STO(rows(vaug_d, 520), vaA[0].rearrange("p (j c) -> p j c", j=4), [vaA[1]], [B_vaug])
                    STO(rows(mvaug_d, 1028), mvA[0].rearrange("p (j c) -> p j c", j=4), [mvA[1]], [B_mvaug])
                    STO(rows(mo_d, D), mo3, [moA[1]], [B_mo])
                    STO(rows(gates_d, 16), gt3, [gtA[1]], [B_gates])

                chk('A')
                new_phase()
                dmax = max(dils)
                qsB = ba.carve(4 * 128 * dmax, "qs")
                ksB = ba.carve(4 * 256 * dmax, "ks")
                vtB = [[ba.carve(520, "vt") for _ in range(2)] for _ in range(2)]
                ptB = [[ba.carve(512, "pt") for _ in range(2)] for _ in range(2)]
                exB = [fa.carve(512, "ex") for _ in range(2)]
                osB = [fa.carve(520, "os") for _ in range(2)]
                bctr = 0
                import os as _os
                _skip = set(_os.environ.get("PB_SKIP", "").split(","))
                for g, d in enumerate(dils):
                    if ("g%d" % g) in _skip:
                        continue
                    sr = S // d
                    nb = sr // 128
                    for b in range(nb):
                        q3 = qsB[0][:, 0:4 * 128 * d].rearrange("p (h t) -> p h t", h=4)
                        k3 = ksB[0][:, 0:4 * 256 * d].rearrange("p (h t) -> p h t", h=4)
                        LD(q3, aqkT[g * 512:(g + 1) * 512, d * 128 * b:d * 128 * (b + 1)].rearrange("(h p) t -> p h t", p=128),
                           [B_aqk], [qsB[1]])
                        klo = d * (128 * b - 64)
                        khi = d * (128 * b + 192)
                        clo, chi = max(klo, 0), min(khi, S)
                        if clo > klo:
                            MS("pool", k3[:, :, 0:clo - klo], 0.0, [ksB[1]], partial=True)
                        if chi < khi:
                            MS("pool", k3[:, :, chi - klo:khi - klo], 0.0, [ksB[1]], partial=True)
                        wrk = {"pwrites": [ksB[1]]}
                        ksrc = aqkT[ng * 512 + g * 512:ng * 512 + (g + 1) * 512, clo:chi].rearrange("(h p) t -> p h t", p=128)
                        kdst = k3[:, :, clo - klo:chi - klo]
                        P.op("sp", lambda e, o=kdst, i=ksrc: e.dma_start(out=o, in_=i), reads=[B_aqk], dma=True, **wrk)
                        for c in range(d):
                            vts = []
                            for j in range(2):
                                m = b + j
                                vt, vtb = vtB[j][bctr % 2]
                                base = c + d * (128 * m - 64)
                                if m == 0:
                                    MS("pool", vt[0:64, :], 0.0, [vtb], partial=True)
                                    STO(vt[64:128, :], vaug_d[srange(c, 64, d), :], [B_vaug], [vtb])
                                elif m == nb:
                                    MS("pool", vt[64:128, :], 0.0, [vtb], partial=True)
                                    STO(vt[0:64, :], vaug_d[srange(base, 64, d), :], [B_vaug], [vtb])
                                else:
                                    STO(vt[:, :], vaug_d[srange(base, 128, d), :], [B_vaug], [vtb])
                                    if bound and (128 * m * d) % bound == 0:
                                        rs_ = slice(0, 64) if j == 0 else slice(64, 128)
                                        TS("pool", vt[rs_, :], vt[rs_, :], keep[rs_, 0:1], ALU.mult, [vtb, B_c], [vtb])
                                vts.append((vt.rearrange("p (h e) -> p h e", h=8), vtb))
                            osb_t, osb = osB[bctr % 2]
                            if "mm" in _skip:
                                MS("pool", osb_t, 0.0, [osb])
                            for hg in range(2 if "mm" not in _skip else 0):
                                pts = []
                                for j in range(2):
                                    pt_, pb = psn(0, 4)
                                    for hh in range(4):
                                        h = 2 * hh + hg
                                        hp, pp = h // 2, (h % 2) * 64
                                        kk = k3[pp:pp + 64, hp, srange(c + 128 * j * d, 128, d)]
                                        qq = q3[pp:pp + 64, hp, srange(c, 128, d)]
                                        MM(pt_[:, hh * 128:(hh + 1) * 128], kk, qq, True, True, [ksB[1], qsB[1]], [pb])
                                    ex, exb = exB[j]
                                    ACT(ex, pt_[:, :], AF.Exp, [pb], [exb], scale=0.125)
                                    pT, pTb = ptB[j][hg]
                                    TT("dve", pT, ex, mtab(g, j, hg * 4, 4), ALU.mult, [exb, B_c], [pTb])
                                    pts.append((pT, pTb))
                                po, pob = psn(4, 8)
                                for hh in range(4):
                                    h = 2 * hh + hg
                                    for j in range(2):
                                        MM(po[:, hh * 65:(hh + 1) * 65], pts[j][0][:, hh * 128:(hh + 1) * 128],
                                           vts[j][0][:, h, :], j == 0, j == 1, [pts[j][1], vts[j][1]], [pob])
                                os3 = osb_t.rearrange("p (h e) -> p h e", h=8)
                                po3 = po[:, 0:260].rearrange("p (h e) -> p h e", h=4)
                                if hg == 0:
                                    ACT(os3[:, 0:8:2, :], po3, AF.Copy, [pob], [osb])
                                else:
                                    CP("dve", os3[:, 1:8:2, :], po3, [pob], [osb])
                            r0 = c + d * 128 * b
                            STO(og_d[g][srange(r0, 128, d), :], osb_t, [osb], [B_og[g]])
                            bctr += 1

                chk('B')
                new_phase()
                dgC = [ba.carve(640, "dg") for _ in range(2)]
                prC = [ba.carve(516, "pr") for _ in range(3)]
                scC = [ba.carve(512, "sc") for _ in range(3)]
                cctr = 0
                for fc in range(16):
                    dg, dgb = dgC[fc % 2]
                    dg3 = dg.rearrange("p (j c) -> p j c", j=5)
                    for j in range(5):
                        TS("dve", dg3[:, j, :], ident_b, wconv[:, fc * 5 + j:fc * 5 + j + 1], ALU.mult, [B_c, B_par], [dgb])
                    for it in range(NT):
                        t0 = it * 512
                        pr, prb = prC[cctr % 3]
                        sc, scb = scC[cctr % 3]
                        cctr += 1
                        lo, hi = t0 - 2, t0 + 514
                        clo, chi = max(lo, 0), min(hi, S)
                        if clo > lo:
                            MS("pool", pr[:, 0:2], 0.0, [prb], partial=True)
                        if chi < hi:
                            MS("pool", pr[:, 514:516], 0.0, [prb], partial=True)
                        STO(pr[:, clo - lo:chi - lo], mpre[fc * 128:(fc + 1) * 128, clo:chi], [B_mpre], [prb])
                        if bound and t0 > 0 and t0 % bound == 0:
                            TS("pool", pr[:, 0:2], pr[:, 0:2], keep[:, 0:1], ALU.mult, [prb, B_c], [prb])
                        if bound and t0 + 512 < S and (t0 + 512) % bound == 0:
                            TS("pool", pr[:, 514:516], pr[:, 514:516], keep[:, 0:1], ALU.mult, [prb, B_c], [prb])
                        pt, pb = psn()
                        for j in range(5):
                            MM(pt[:, :], dg3[:, j, :], pr[:, j:j + 512], j == 0, j == 4, [dgb, prb], [pb])
                        ACT(sc, pt[:, :], AF.Silu, [pb, B_par], [scb], bias=bconv[:, fc:fc + 1])
                        STO(mqkT[fc * 128:(fc + 1) * 128, t0:t0 + 512], sc, [scb], [B_mqk])

                chk('C')
                for dr in range(2):
                    new_phase()
                    maskf = maskU_f if dr == 0 else maskL_f
                    qTD = ba.carve(4096, "qTD")
                    kTD = ba.carve(4096, "kTD")
                    ktm = ba.carve(4096, "ktm")
                    mvD = ba.carve(4 * 1028, "mvD")
                    CbD = ba.carve(8 * 257, "Cb")
                    scD = [ba.carve(128, "scT") for _ in range(2)]
                    vtD = [ba.carve(257, "vtl") for _ in range(2)]
                    vpD = [ba.carve(257, "vpr") for _ in range(2)]
                    hst = fa.carve(4096, "hst")
                    CfD = fa.carve(8 * 257, "Cf")
                    gtD = fa.carve(64, "gtD")
                    gsm = {n: fa.carve(16, n) for n in ["e1", "sp", "tmp", "tmp2", "eb", "ebt", "ecum", "etot"]}
                    dnD = [fa.carve(2, "dn") for _ in range(2)]
                    MS("pool", CfD[0], 0.0, [CfD[1]])
                    MS("pool", CbD[0], 0.0, [CbD[1]])
                    Cf3 = CfD[0].rearrange("p (c e) -> p c e", c=8)
                    Cb3 = CbD[0].rearrange("p (c e) -> p c e", c=8)
                    order = list(range(NT)) if dr == 0 else list(range(NT - 1, -1, -1))
                    for it in order:
                        t0 = it * 512
                        if bound and ((dr == 0 and t0 > 0 and t0 % bound == 0) or
                                      (dr == 1 and t0 + 512 < S and (t0 + 512) % bound == 0)):
                            TS("dve", CfD[0], CfD[0], keep[:, 0:1], ALU.mult, [CfD[1], B_c], [CfD[1]])
                            TS("pool", CbD[0], CbD[0], keep[:, 0:1], ALU.mult, [CbD[1], B_c], [CbD[1]])
                        q3 = qTD[0].rearrange("p (c t) -> p c t", c=8)
                        k3 = kTD[0].rearrange("p (c t) -> p c t", c=8)
                        LD(q3, mqkT[0:1024, t0:t0 + 512].rearrange("(c p) t -> p c t", p=128), [B_mqk], [qTD[1]])
                        LD(k3, mqkT[1024:2048, t0:t0 + 512].rearrange("(c p) t -> p c t", p=128), [B_mqk], [kTD[1]])
                        mv3 = mvD[0].rearrange("p (j c) -> p j c", j=4)
                        LD(mv3, mvaug_d[t0:t0 + 512, :].rearrange("(j p) c -> p j c", p=128), [B_mvaug], [mvD[1]])
                        gt3 = gtD[0].rearrange("p (j c) -> p j c", j=4)
                        LD(gt3, gates_d[t0:t0 + 512, :].rearrange("(j p) c -> p j c", p=128), [B_gates], [gtD[1]])
                        g3 = {n: v[0].rearrange("p (j c) -> p j c", j=4) for n, v in gsm.items()}
                        gb_ = {n: v[1] for n, v in gsm.items()}
                        li = gt3[:, :, dr * 8:dr * 8 + 4]
                        gf = gt3[:, :, dr * 8 + 4:dr * 8 + 8]
                        ACT(g3["e1"], gf, AF.Exp, [gtD[1]], [gb_["e1"]], scale=-1.0)
                        ACT(g3["sp"], g3["e1"], AF.Ln, [gb_["e1"]], [gb_["sp"]], bias=1.0)
                        pg, pgb = PS[7]
                        for j in range(4):
                            MM(pg[:, j * 4:j * 4 + 4], maskf, g3["sp"][:, j, :], True, True, [B_c, gb_["sp"]], [pgb])
                        for j in range(4):
                            MM(pg[:, 16 + j * 4:16 + j * 4 + 4], ones_f, g3["sp"][:, j, :], True, True, [B_c, gb_["sp"]], [pgb])
                        ncum = pg[:, 0:16].rearrange("p (j c) -> p j c", j=4)
                        ntot = pg[:, 16:32].rearrange("p (j c) -> p j c", j=4)
                        TT("dve", g3["tmp"], li, ncum, ALU.add, [gtD[1], pgb], [gb_["tmp"]])
                        TT("dve", g3["tmp2"], g3["tmp"], ntot, ALU.subtract, [gb_["tmp"], pgb], [gb_["tmp2"]])
                        ACT(g3["eb"], g3["tmp"], AF.Exp, [gb_["tmp"]], [gb_["eb"]], bias=-LN16)
                        ACT(g3["ebt"], g3["tmp2"], AF.Exp, [gb_["tmp2"]], [gb_["ebt"]], bias=-LN16)
                        ACT(g3["ecum"], ncum, AF.Exp, [pgb], [gb_["ecum"]])
                        ACT(g3["etot"], ntot, AF.Exp, [pgb], [gb_["etot"]], scale=-1.0)
                        kt3 = ktm[0].rearrange("p (j c) -> p j c", j=4)
                        for j in range(4):
                            pk, pkb = PS[6]
                            pkb16 = pk[:, :].bitcast(BF16)
                            for c in range(8):
                                TR(pkb16[:, c * 128:(c + 1) * 128], k3[:, c, j * 128:(j + 1) * 128], ident_b, [kTD[1]], [pkb])
                            CP("dve", kt3[:, j, :], pkb16, [pkb], [ktm[1]])
                        h3 = hst[0].rearrange("p (j c) -> p j c", j=4)
                        jorder = list(range(4)) if dr == 0 else [3, 2, 1, 0]
                        for j in jorder:
                            tsl = slice(j * 128, (j + 1) * 128)
                            for h in range(4):
                                pS, pSb = PS[h % 2]
                                for c in range(2):
                                    MM(pS[:, 0:128], k3[:, 2 * h + c, tsl], q3[:, 2 * h + c, tsl], c == 0, c == 1,
                                       [kTD[1], qTD[1]], [pSb])
                                sc, scb = scD[h % 2]
                                TT("dve", sc, pS[:, 0:128], maskf, ALU.mult, [pSb, B_c], [scb])
                                vtl, vtlb = vtD[h % 2]
                                vpr, vprb = vpD[h % 2]
                                mvh = mv3[:, j, h * 257:(h + 1) * 257]
                                ACT(vtl, mvh, AF.Copy, [mvD[1], gb_["eb"]], [vtlb], scale=g3["eb"][:, j, h:h + 1])
                                TS("pool", vpr, mvh, g3["ebt"][:, j, h:h + 1], ALU.mult, [mvD[1], gb_["ebt"]], [vprb])
                                ph, phb = PS[2 + h % 2]
                                MM(ph[:, 0:257], sc, vtl, True, False, [scb, vtlb], [phb])
                                for c in range(2):
                                    MM(ph[:, 0:257], q3[:, 2 * h + c, tsl], Cb3[:, 2 * h + c, :], False, c == 1,
                                       [qTD[1], CbD[1]], [phb])
                                dn, dnb = dnD[h % 2]
                                ACT(dn[:, 0:1], ph[:, 256:257], AF.Abs, [phb], [dnb])
                                TS("dve", dn[:, 0:1], dn[:, 0:1], g3["ecum"][:, j, h:h + 1], ALU.max, [dnb, gb_["ecum"]], [dnb])
                                RCP(dn[:, 1:2], dn[:, 0:1], [dnb], [dnb])
                                ACT(h3[:, j, h * 256:(h + 1) * 256], ph[:, 0:256], AF.Copy, [phb, dnb], [hst[1]],
                                    scale=dn[:, 1:2])
                                for c in range(2):
                                    pc, pcb = PS[4 + c]
                                    MM(pc[:, 0:257], kt3[:, j, (2 * h + c) * 128:(2 * h + c + 1) * 128], vpr, True, True,
                                       [ktm[1], vprb], [pcb])
                                    STT(Cf3[:, 2 * h + c, :], Cf3[:, 2 * h + c, :], g3["etot"][:, j, h:h + 1], pc[:, 0:257],
                                        ALU.mult, ALU.add, [CfD[1], gb_["etot"], pcb], [CfD[1]])
                                    ACT(Cb3[:, 2 * h + c, :], Cf3[:, 2 * h + c, :], AF.Copy, [CfD[1]], [CbD[1]])
                        STO(h_d[dr][t0:t0 + 512, :].rearrange("(j p) c -> p j c", p=128), h3, [hst[1]], [B_h[dr]])

                chk('D')
                new_phase()
                xE = fa.carve(4096, "xE")
                yE = fa.carve(4096, "yE")
                hfE = fa.carve(1024, "hf")
                hbE = fa.carve(1024, "hb")
                sgE = fa.carve(1024, "sg")
                jkE = fa.carve(1024, "jk")
                ogE = [fa.carve(520, "og%d" % g) for g in range(ng)]
                smE = fa.carve(32, "smE")
                gaE = fa.carve(512, "ga")
                gmE = fa.carve(512, "gm")
                t1E = fa.carve(512, "t1")
                t2E = fa.carve(512, "t2")
                rsE = fa.carve(512, "rsE")
                tmE = fa.carve(512, "tmE")
                moE = ba.carve(1024, "moE")
                hnE = ba.carve(1024, "hnE")
                aoE = ba.carve(512, "aoE")
                mlT = ba.carve(4096, "mlT")
                atT = ba.carve(2048, "atT")
                bgE = [ba.carve(512, "bgE") for _ in range(2)]
                mgE = ba.carve(4096, "mgE")
                sqE = ba.carve(4096, "sqE")
                xnE = ba.carve(4096, "xnE")
                acE = ba.carve(22 * 512, "acE")
                last = (l == L - 1)
                for it in range(NT):
                    t0 = it * 512
                    x3 = xE[0].rearrange("p (k t) -> p k t", k=8)
                    y3 = yE[0].rearrange("p (k t) -> p k t", k=8)
                    LD(x3, xT.rearrange("(k p) s -> p k s", p=128)[:, :, t0:t0 + 512], [B_xT], [xE[1]])
                    ml3 = mlT[0].rearrange("p (k t) -> p k t", k=8)
                    at3 = atT[0].rearrange("p (k t) -> p k t", k=4)
                    for j in range(4):
                        r0 = t0 + j * 128
                        LD(hfE[0], h_d[0][r0:r0 + 128, :], [B_h[0]], [hfE[1]])
                        LD(hbE[0], h_d[1][r0:r0 + 128, :], [B_h[1]], [hbE[1]])
                        LD(moE[0], mo_d[r0:r0 + 128, :], [B_mo], [moE[1]])
                        TT("dve", hfE[0], hfE[0], hbE[0], ALU.add, [hfE[1], hbE[1]], [hfE[1]])
                        sm = smE[0]
                        for h in range(4):
                            ACT(jkE[0][:, h * 256:(h + 1) * 256], hfE[0][:, h * 256:(h + 1) * 256], AF.Square,
                                [hfE[1]], [jkE[1], smE[1]], accum=sm[:, h:h + 1])
                        rstd_from_ssq(sm[:, 0:4], smE[1], 256.0, sm[:, 4:8], smE[1], sm[:, 8:12], smE[1], 4)
                        ACT(sgE[0], moE[0], AF.Sigmoid, [moE[1]], [sgE[1]])
                        TT("pool", sgE[0], sgE[0], gnorm[:, :], ALU.mult, [sgE[1], B_par], [sgE[1]])
                        for h in range(4):
                            hs = slice(h * 256, (h + 1) * 256)
                            STT(hnE[0][:, hs], hfE[0][:, hs], sm[:, 8 + h:9 + h], sgE[0][:, hs], ALU.mult, ALU.mult,
                                [hfE[1], smE[1], sgE[1]], [hnE[1]])
                        pk, pkb = psn()
                        pk16 = pk[:, :].bitcast(BF16)
                        for k in range(8):
                            TR(pk16[:, k * 128:(k + 1) * 128], hnE[0][:, k * 128:(k + 1) * 128], ident_b, [hnE[1]], [pkb])
                        CP("dve", ml3[:, :, j * 128:(j + 1) * 128], pk16.rearrange("p (k t) -> p k t", k=8), [pkb], [mlT[1]])
                        for g in range(ng):
                            LD(ogE[g][0], og_d[g][r0:r0 + 128, :], [B_og[g]], [ogE[g][1]])
                        for g in range(1, ng):
                            TT("pool", ogE[0][0], ogE[0][0], ogE[g][0], ALU.add, [ogE[0][1], ogE[g][1]], [ogE[0][1]])
                        o3 = ogE[0][0].rearrange("p (h e) -> p h e", h=8)
                        RCP(sm[:, 16:24], o3[:, :, 64], [ogE[0][1]], [smE[1]])
                        ao3 = aoE[0].rearrange("p (h e) -> p h e", h=8)
                        for h in range(8):
                            TS("dve" if h % 2 else "pool", ao3[:, h, :], o3[:, h, 0:64], sm[:, 16 + h:17 + h], ALU.mult,
                               [ogE[0][1], smE[1]], [aoE[1]])
                        pk, pkb = psn()
                        pk16 = pk[:, :].bitcast(BF16)
                        for k in range(4):
                            TR(pk16[:, k * 128:(k + 1) * 128], aoE[0][:, k * 128:(k + 1) * 128], ident_b, [aoE[1]], [pkb])
                        CP("dve", at3[:, :, j * 128:(j + 1) * 128], pk16[:, 0:512].rearrange("p (k t) -> p k t", k=4),
                           [pkb], [atT[1]])
                    mg3 = mgE[0].rearrange("p (k t) -> p k t", k=8)
                    for cg in range(2):
                        wa3, wab = load_w(wview(w_ap[l], 0, 4, cg * 512, 512))
                        wm3, wmb = load_w(wview(w_mp[l], 0, 8, cg * 512, 512))
                        for oc in range(4):
                            fcx = cg * 4 + oc
                            LD(bgE[0][0], bgT[fcx * 128:(fcx + 1) * 128, t0:t0 + 512], [B_bg], [bgE[0][1]])
                            LD(bgE[1][0], bgT[1024 + fcx * 128:1024 + (fcx + 1) * 128, t0:t0 + 512], [B_bg], [bgE[1][1]])
                            pa, pab = psn()
                            for k in range(4):
                                MM(pa[:, :], wa3[:, k, oc * 128:(oc + 1) * 128], at3[:, k, :], k == 0, k == 3,
                                   [wab, atT[1]], [pab])
                            pm, pmb = psn()
                            for k in range(8):
                                MM(pm[:, :], wm3[:, k, oc * 128:(oc + 1) * 128], ml3[:, k, :], k == 0, k == 7,
                                   [wmb, mlT[1]], [pmb])
                            ACT(gaE[0], bgE[0][0], AF.Sigmoid, [bgE[0][1]], [gaE[1]])
                            ACT(gmE[0], bgE[1][0], AF.Sigmoid, [bgE[1][1]], [gmE[1]])
                            TT("dve", t1E[0], pa[:, :], gaE[0], ALU.mult, [pab, gaE[1]], [t1E[1]])
                            TT("dve", t2E[0], pm[:, :], gmE[0], ALU.mult, [pmb, gmE[1]], [t2E[1]])
                            TT("pool", mg3[:, fcx, :], t1E[0], t2E[0], ALU.add, [t1E[1], t2E[1]], [mgE[1]])

                    def proj_norm_res(w2, nkt, src3, srcb, gcol):
                        kgs = [(k0, min(8, nkt - k0)) for k0 in range(0, nkt, 8)]
                        for cg in range(2):
                            pss = [psn() for _ in range(4)]
                            for gi, (k0, nk) in enumerate(kgs):
                                w3, wb = load_w(wview(w2, k0 * 128, nk, cg * 512, 512))
                                for oc in range(4):
                                    for k in range(nk):
                                        MM(pss[oc][0][:, :], w3[:, k, oc * 128:(oc + 1) * 128], src3[:, k0 + k, :],
                                           gi == 0 and k == 0, gi == len(kgs) - 1 and k == nk - 1, [wb, srcb], [pss[oc][1]])
                            for oc in range(4):
                                ACT(y3[:, cg * 4 + oc, :], pss[oc][0][:, :], AF.Copy, [pss[oc][1]], [yE[1]])
                        sq3 = sqE[0].rearrange("p (k t) -> p k t", k=8)
                        fm_norm(y3, yE[1], gcol, sq3, sqE[1], None, None, rsE[0], rsE[1], tmE[0], tmE[1])
                        for k in range(8):
                            STT(y3[:, k, :], y3[:, k, :], gains[:, gcol + k:gcol + k + 1], rsE[0], ALU.mult, ALU.mult,
                                [yE[1], rsE[1], B_par], [yE[1]])
                            TT("pool", x3[:, k, :], x3[:, k, :], y3[:, k, :], ALU.add, [xE[1], yE[1]], [xE[1]])

                    proj_norm_res(w_o[l], 8, mg3, mgE[1], 8)
                    sq3 = sqE[0].rearrange("p (k t) -> p k t", k=8)
                    xn3 = xnE[0].rearrange("p (k t) -> p k t", k=8)
                    fm_norm(x3, xE[1], 16, sq3, sqE[1], xn3, xnE[1], rsE[0], rsE[1], tmE[0], tmE[1])
                    ac3 = acE[0].rearrange("p (k t) -> p k t", k=22)
                    for i0 in range(0, 22, 4):
                        nci = min(4, 22 - i0)
                        wg3, wgb = load_w(wview(w_f1[l], 0, 8, i0 * 128, nci * 128))
                        wu3, wub = load_w(wview(w_f1[l], 0, 8, FH + i0 * 128, nci * 128))
                        for ii in range(nci):
                            pgt, pgtb = psn()
                            for k in range(8):
                                MM(pgt[:, :], wg3[:, k, ii * 128:(ii + 1) * 128], xn3[:, k, :], k == 0, k == 7,
                                   [wgb, xnE[1]], [pgtb])
                            put, putb = psn()
                            for k in range(8):
                                MM(put[:, :], wu3[:, k, ii * 128:(ii + 1) * 128], xn3[:, k, :], k == 0, k == 7,
                                   [wub, xnE[1]], [putb])
                            ACT(t1E[0], pgt[:, :], AF.Silu, [pgtb], [t1E[1]])
                            TT("dve", ac3[:, i0 + ii, :], put[:, :], t1E[0], ALU.mult, [putb, t1E[1]], [acE[1]])
                    proj_norm_res(w_f2[l], 22, ac3, acE[1], 24)
                    if not last:
                        STO(xT.rearrange("(k p) s -> p k s", p=128)[:, :, t0:t0 + 512], x3, [xE[1]], [B_xT])
                    else:
                        o3_ = yE[0].rearrange("p (j c) -> p j c", j=4)
                        for j in range(4):
                            for kk in range(0, 8, 4):
                                pt, pb = psn()
                                for k in range(kk, kk + 4):
                                    TR(pt[:, (k - kk) * 128:(k - kk + 1) * 128], x3[:, k, j * 128:(j + 1) * 128], ident_f,
                                       [xE[1]], [pb])
                                if kk:
                                    CP("dve", o3_[:, j, kk * 128:(kk + 4) * 128], pt[:, :], [pb], [yE[1]])
                                else:
                                    ACT(o3_[:, j, kk * 128:(kk + 4) * 128], pt[:, :], AF.Copy, [pb], [yE[1]])
                        STO(yout[si][t0:t0 + 512, :].rearrange("(j p) c -> p j c", p=128), o3_, [yE[1]], [B_out])
    try:
        _main_body()
    except _Stop:
        pass
    P.emit()
    st.close()
    return nc


def host_params(norm_mix_pre, norm_mix_post, norm_ffn_pre, norm_ffn_post, b_conv, w_conv):
    L = norm_mix_pre.shape[0]
    fmv = lambda v: np.ascontiguousarray(np.asarray(v, np.float32).reshape(L, -1, 128).transpose(0, 2, 1))
    gains = np.concatenate([fmv(norm_mix_pre), fmv(norm_mix_post), fmv(norm_ffn_pre), fmv(norm_ffn_post)], axis=2)
    bconv = fmv(b_conv)
    wc = np.asarray(w_conv, np.float32)
    wconv = np.ascontiguousarray(wc.reshape(L, 5, 16, 128).transpose(0, 3, 2, 1)).reshape(L, 128, 80)
    return np.ascontiguousarray(gains), bconv, wconv


_CACHE = {}


def run(seq_lists, xs_per_core, depth, dils, weights, dbg=None, bound=None, keeps=None):
    key = (tuple(seq_lists), depth, tuple(dils), dbg, bound)
    if key not in _CACHE:
        _CACHE[key] = build(list(seq_lists), depth, tuple(dils), dbg, bound)
    nc = _CACHE[key]
    gains, bconv, wconv = host_params(weights["norm_mix_pre"], weights["norm_mix_post"], weights["norm_ffn_pre"],
                                      weights["norm_ffn_post"], weights["b_conv"], weights["w_conv"])
    common = {
        "w_in": np.ascontiguousarray(weights["w_in"], np.float32),
        "w_attn_proj": np.ascontiguousarray(weights["w_attn_proj"], np.float32),
        "w_mlstm_proj": np.ascontiguousarray(weights["w_mlstm_proj"], np.float32),
        "w_out": np.ascontiguousarray(weights["w_out"], np.float32),
        "w_ffn_in": np.ascontiguousarray(weights["w_ffn_in"], np.float32),
        "w_ffn_out": np.ascontiguousarray(weights["w_ffn_out"], np.float32),
        "gains": gains, "bconv": bconv, "wconv": wconv,
        "gnorm": np.ascontiguousarray(weights["g_mlstm_norm"], np.float32),
        "gbias": np.ascontiguousarray(weights["b_mlstm_gates"], np.float32),
        "consts": make_consts(dils),
    }
    in_maps = []
    for ci, xs in enumerate(xs_per_core):
        m = dict(common)
        m["keep"] = np.full((128, 1), 1.0 if keeps is None else float(keeps[ci]), np.float32)
        for i, x in enumerate(xs):
            m["x%d" % i] = np.ascontiguousarray(x, np.float32)
        in_maps.append(m)
    res = run_bass_kernel_spmd(nc, in_maps, core_ids=list(range(len(in_maps))))
    return res.results


def kernel(x_prompt, x_sample, norm_mix_pre, norm_mix_post, norm_ffn_pre, norm_ffn_post, w_in, b_mlstm_gates,
           w_conv, b_conv, g_mlstm_norm, w_attn_proj, w_mlstm_proj, w_out, w_ffn_in, w_ffn_out):
    weights = dict(norm_mix_pre=norm_mix_pre, norm_mix_post=norm_mix_post, norm_ffn_pre=norm_ffn_pre,
                   norm_ffn_post=norm_ffn_post, w_in=w_in, b_mlstm_gates=b_mlstm_gates, w_conv=w_conv, b_conv=b_conv,
                   g_mlstm_norm=g_mlstm_norm, w_attn_proj=w_attn_proj, w_mlstm_proj=w_mlstm_proj, w_out=w_out,
                   w_ffn_in=w_ffn_in, w_ffn_out=w_ffn_out)
    weights = {k: np.asarray(v) for k, v in weights.items()}
    x_prompt = np.asarray(x_prompt, np.float32)
    x_sample = np.asarray(x_sample, np.float32)
    depth = w_in.shape[0]
    n = 8
    nsamp, sp = x_sample.shape[0], x_sample.shape[1]
    pl = x_prompt.shape[1]
    per = pl // sp
    ngrp = nsamp // per
    xs, keeps = [[x_prompt[0]]], [1.0]
    for gi in range(ngrp):
        xs.append([x_sample[gi * per:(gi + 1) * per].reshape(pl, -1)])
        keeps.append(0.0)
    while len(xs) < n:
        xs.append([np.zeros((pl, x_prompt.shape[2]), np.float32)])
        keeps.append(0.0)
    res = run([pl], xs, depth, (1, 4, 16), weights, bound=sp, keeps=keeps)
    y_prompt = res[0]["y0"][None].astype(np.float32)
    y_sample = np.concatenate([res[1 + gi]["y0"].reshape(per, sp, -1) for gi in range(ngrp)], axis=0).astype(np.float32)
    return (y_prompt, y_sample)
```

```python
import math
from contextlib import ExitStack

import numpy as np
import concourse.bass as bass
import concourse.mybir as mybir
from concourse.bass_utils import run_bass_kernel_spmd

F32 = mybir.dt.float32
BF16 = mybir.dt.bfloat16
AF = mybir.ActivationFunctionType
ALU = mybir.AluOpType

ENGS = ("pe", "act", "dve", "pool", "sp")
D = 1024
NIN = 9744
FH = 2816
EPS = 1e-6
LN16 = math.log(16.0)


class Buf:
    __slots__ = ("name", "w", "r")

    def __init__(self, name=""):
        self.name = name
        self.w = []
        self.r = []


class Ins:
    __slots__ = ("eng", "fn", "deps", "inc", "val", "dma", "dsem", "dval", "idx")

    def __init__(self, eng, fn, dma):
        self.eng = eng
        self.fn = fn
        self.deps = []
        self.inc = False
        self.val = 0
        self.dma = dma
        self.dsem = -1
        self.dval = 0
        self.idx = -1


def _reduce(lst):
    best = {}
    for d in lst:
        key = (d.eng, d.dsem) if d.dma else (d.eng, -1)
        cur = best.get(key)
        if cur is None or (d.dval > cur.dval if d.dma else d.idx > cur.idx):
            best[key] = d
    return list(best.values())


class Prog:
    NDSEM = 8

    def __init__(self, nc):
        self.nc = nc
        self.lists = {e: [] for e in ENGS}
        self.dma_count = {e: 0 for e in ENGS}

    def op(self, eng, fn, reads=(), writes=(), pwrites=(), dma=False):
        ins = Ins(eng, fn, dma)
        deps = []
        for b in reads:
            deps.extend(b.w)
        for b in writes:
            deps.extend(b.w)
            deps.extend(b.r)
        for b in pwrites:
            deps.extend(b.r)
        lst = self.lists[eng]
        ins.idx = len(lst)
        if dma:
            n = self.dma_count[eng]
            self.dma_count[eng] = n + 1
            ins.dsem = n % self.NDSEM
            ins.dval = 16 * (n // self.NDSEM + 1)
        final = []
        for d in _reduce(deps):
            if (not d.dma) and d.eng == "pe" and eng == "pe" and not dma:
                continue
            if not d.dma:
                d.inc = True
            final.append(d)
        ins.deps = final
        lst.append(ins)
        for b in writes:
            b.w = [ins]
            b.r = []
        for b in pwrites:
            b.w = _reduce(b.w + [ins])
        for b in reads:
            b.r = _reduce(b.r + [ins])
        return ins

    def emit(self):
        nc = self.nc
        with ExitStack() as st:
            esem = {e: st.enter_context(nc.semaphore("s_" + e)) for e in ENGS}
            dsem = {e: [st.enter_context(nc.semaphore("d_%s%d" % (e, i))) for i in range(self.NDSEM)]
                    for e in ("sp", "act", "pool")}
            for e in ENGS:
                c = 0
                for ins in self.lists[e]:
                    if ins.inc and not ins.dma:
                        c += 1
                        ins.val = c
            block = st.enter_context(nc.Block())
            last = {}
            for e in ("sp", "act", "pool"):
                for ins in self.lists[e]:
                    if ins.dma:
                        last[(e, ins.dsem)] = ins.dval

            def run(ename, eng):
                waited = {}

                def wait(key, sem, val):
                    if waited.get(key, 0) < val:
                        eng.wait_ge(sem, val)
                        waited[key] = val

                for ins in self.lists[ename]:
                    for d in ins.deps:
                        if d.dma:
                            wait(("d", d.eng, d.dsem), dsem[d.eng][d.dsem], d.dval)
                        else:
                            wait(("e", d.eng), esem[d.eng], d.val)
                    if ins.dma:
                        if ins.dval > 16:
                            wait(("d", ename, ins.dsem), dsem[ename][ins.dsem], ins.dval - 16)
                        ins.fn(eng).then_inc(dsem[ename][ins.dsem], 16)
                    else:
                        h = ins.fn(eng)
                        if ins.inc:
                            h.then_inc(esem[ename], 1)
                if ename == "sp":
                    for (e, k), v in last.items():
                        wait(("d", e, k), dsem[e][k], v)

            block.tensor(lambda eng: run("pe", eng))
            block.scalar(lambda eng: run("act", eng))
            block.vector(lambda eng: run("dve", eng))
            block.gpsimd(lambda eng: run("pool", eng))
            block.sync(lambda eng: run("sp", eng))


class Arena:
    def __init__(self, t, n):
        self.t = t
        self.n = n
        self.off = 0
        self.bufs = []
        self.pending = []

    def reset(self):
        pend = list(self.pending)
        for b in self.bufs:
            pend.extend(b.w)
            pend.extend(b.r)
        self.pending = _reduce(pend)
        self.bufs = []
        self.off = 0

    def carve(self, n, name=""):
        assert self.off + n <= self.n, (name, self.off, n, self.n)
        ap = self.t[:, self.off:self.off + n]
        self.off += n
        b = Buf(name)
        b.r = list(self.pending)
        self.bufs.append(b)
        return ap, b


def alibi_slopes(ng):
    h = np.arange(1, ng * 8 + 1, dtype=np.float32)
    return (2.0 ** (-8.0 * h / (ng * 8))).astype(np.float32).reshape(ng, 8)


def make_consts(dils):
    ng = len(dils)
    k = np.arange(128)[:, None]
    q = np.arange(128)[None, :]
    ident = (k == q).astype(np.float32)
    mu = (k <= q).astype(np.float32)
    ml = (k >= q).astype(np.float32)
    ones = np.ones((128, 128), np.float32)
    sl = alibi_slopes(ng)
    mt = np.zeros((128, ng, 2, 8, 128), np.float32)
    for g, d in enumerate(dils):
        for j in range(2):
            step = 128 * j - 64 + k - q
            valid = np.abs(step) <= 64
            for hi, h in enumerate([0, 2, 4, 6, 1, 3, 5, 7]):
                mt[:, g, j, hi, :] = np.where(valid, np.exp(-sl[g, h] * d * np.abs(step).astype(np.float32)), 0.0)
    return np.concatenate([ident, mu, ml, ones, mt.reshape(128, -1)], axis=1).astype(np.float32)


def build(seqs, depth, dils, dbg=None, bound=None):
    nc = bass.Bass("TRN2", target_bir_lowering=False)
    P = Prog(nc)
    st = ExitStack()
    ng = len(dils)
    SMAX = max(seqs)
    L = depth

    def dram(name, shape, dt, kind="Internal"):
        return nc.dram_tensor(name, list(shape), dt, kind=kind).ap()

    xin = [dram("x%d" % i, [s, D], F32, "ExternalInput") for i, s in enumerate(seqs)]
    yout = [dram("y%d" % i, [s, D], F32, "ExternalOutput") for i, s in enumerate(seqs)]
    w_in = dram("w_in", [L, D, NIN], F32, "ExternalInput")
    w_ap = dram("w_attn_proj", [L, 512, D], F32, "ExternalInput")
    w_mp = dram("w_mlstm_proj", [L, D, D], F32, "ExternalInput")
    w_o = dram("w_out", [L, D, D], F32, "ExternalInput")
    w_f1 = dram("w_ffn_in", [L, D, 2 * FH], F32, "ExternalInput")
    w_f2 = dram("w_ffn_out", [L, FH, D], F32, "ExternalInput")
    gains_d = dram("gains", [L, 128, 32], F32, "ExternalInput")
    bconv_d = dram("bconv", [L, 128, 16], F32, "ExternalInput")
    wconv_d = dram("wconv", [L, 128, 80], F32, "ExternalInput")
    gnorm_d = dram("gnorm", [L, D], F32, "ExternalInput")
    gbias_d = dram("gbias", [L, 16], F32, "ExternalInput")
    NCONST = 4 * 128 + ng * 2 * 8 * 128
    const_d = dram("consts", [128, NCONST], F32, "ExternalInput")
    keep_d = dram("keep", [128, 1], F32, "ExternalInput")

    xT = dram("xT", [D, SMAX], F32)
    aqkT = dram("aqkT", [2 * ng * 512, SMAX], BF16)
    vaug_d = dram("vaug", [SMAX, 520], BF16)
    mpre = dram("mpre", [2048, SMAX], BF16)
    mqkT = dram("mqkT", [2048, SMAX], BF16)
    mvaug_d = dram("mvaug", [SMAX, 1028], BF16)
    mo_d = dram("mo", [SMAX, D], BF16)
    gates_d = dram("gates", [SMAX, 16], F32)
    bgT = dram("bgT", [2048, SMAX], BF16)
    og_d = [dram("og%d" % g, [SMAX, 520], F32) for g in range(ng)]
    h_d = [dram("hdir%d" % i, [SMAX, D], F32) for i in range(2)]
    B_xT, B_aqk, B_vaug, B_mpre, B_mqk, B_mvaug, B_mo, B_gates, B_bg = [Buf(n) for n in
        "xT aqk vaug mpre mqk mvaug mo gates bg".split()]
    B_og = [Buf("og%d" % g) for g in range(ng)]
    B_h = [Buf("h0"), Buf("h1")]
    B_in = Buf("in")
    B_out = Buf("out")

    def sb(name, shape, dt):
        return st.enter_context(nc.sbuf_tensor(name, list(shape), dt))

    cf = sb("cf", [128, 512], F32)
    cb = sb("cb", [128, NCONST], BF16)
    B_c = Buf("c")
    P.op("sp", lambda e: e.dma_start(out=cf[:], in_=const_d[:, 0:512]), writes=[B_c], dma=True)
    P.op("pool", lambda e: e.dma_start(out=cb[:], in_=const_d[:, :]), pwrites=[B_c], dma=True)
    ident_f, maskU_f, maskL_f, ones_f = (cf[:, i * 128:(i + 1) * 128] for i in range(4))
    keep = sb("keeps", [128, 1], F32)
    P.op("sp", lambda e: e.dma_start(out=keep[:], in_=keep_d[:, :]), pwrites=[B_c], dma=True)
    ident_b = cb[:, 0:128]
    ones_b = cb[:, 384:512]

    def mtab(g, j, h0, nh):
        o = 512 + ((g * 2 + j) * 8 + h0) * 128
        return cb[:, o:o + nh * 128]

    gains = sb("gainss", [128, 32], F32)
    bconv = sb("bconvs", [128, 16], F32)
    wconv = sb("wconvs", [128, 80], F32)
    gnorm = sb("gnorms", [128, D], F32)
    gbias = sb("gbiass", [128, 16], F32)
    B_par = Buf("par")

    NW = 4
    wt = [sb("wt%d" % i, [128, 4096], BF16) for i in range(NW)]
    B_wt = [Buf("wt%d" % i) for i in range(NW)]
    wctr = [0]
    _wsrc_buf = []

    def load_w(src3):
        i = wctr[0] % NW
        wctr[0] += 1
        nk, ncol = src3.shape[1], src3.shape[2]
        dst = wt[i][:, 0:nk * ncol].rearrange("p (k c) -> p k c", k=nk)
        P.op("pool", lambda e: e.dma_start(out=dst, in_=src3), reads=list(_wsrc_buf), writes=[B_wt[i]], dma=True)
        return dst, B_wt[i]

    def wview(w2, r0, nk, c0, ncol):
        return w2[r0:r0 + nk * 128, c0:c0 + ncol].rearrange("(k p) c -> p k c", p=128)


    NF = 16000
    NB = 33280
    fa = Arena(sb("fa", [128, NF], F32), NF)
    ba = Arena(sb("ba", [128, NB], BF16), NB)
    PS = []
    for i in range(8):
        t = st.enter_context(nc.psum_tensor("ps%d" % i, [128, 512], F32))
        PS.append((t, Buf("ps%d" % i)))
    prr = [0]

    def psn(lo=0, hi=8):
        i = lo + prr[0] % (hi - lo)
        prr[0] += 1
        return PS[i]

    def new_phase():
        fa.reset()
        ba.reset()

    def MM(o, l, r, s, t, rd, wr):
        P.op("pe", lambda e: e.matmul(o, lhsT=l, rhs=r, start=s, stop=t), reads=rd, writes=wr)

    def TR(o, i, idn, rd, wr):
        P.op("pe", lambda e: e.transpose(o, i, idn), reads=rd + [B_c], writes=wr)

    def ACT(o, i, f, rd, wr, bias=None, scale=None, accum=None, eng="act"):
        kw = {}
        if bias is not None:
            kw["bias"] = bias
        if scale is not None:
            kw["scale"] = scale
        if accum is not None:
            kw["accum_out"] = accum
        P.op("act", lambda e: e.activation(out=o, in_=i, func=f, **kw), reads=rd, writes=wr)

    def TT(eng, o, a, b, op, rd, wr):
        P.op(eng, lambda e: e.tensor_tensor(out=o, in0=a, in1=b, op=op), reads=rd, writes=wr)

    def TS(eng, o, a, s1, op0, rd, wr, s2=None, op1=None):
        if op1 is None:
            P.op(eng, lambda e: e.tensor_scalar(out=o, in0=a, scalar1=s1, scalar2=None, op0=op0), reads=rd, writes=wr)
        else:
            P.op(eng, lambda e: e.tensor_scalar(out=o, in0=a, scalar1=s1, scalar2=s2, op0=op0, op1=op1),
                 reads=rd, writes=wr)

    def STT(o, a, s, b, op0, op1, rd, wr):
        P.op("dve", lambda e: e.scalar_tensor_tensor(out=o, in0=a, scalar=s, in1=b, op0=op0, op1=op1),
             reads=rd, writes=wr)

    def CP(eng, o, i, rd, wr):
        P.op(eng, lambda e: e.tensor_copy(out=o, in_=i), reads=rd, writes=wr)

    def RCP(o, i, rd, wr):
        P.op("dve", lambda e: e.reciprocal(out=o, in_=i), reads=rd, writes=wr)

    def MS(eng, o, v, wr, partial=False):
        if partial:
            P.op(eng, lambda e: e.memset(o, v), pwrites=wr)
        else:
            P.op(eng, lambda e: e.memset(o, v), writes=wr)

    def srange(base, n, step):
        return slice(base, base + step * (n - 1) + 1, step)

    def LD(o, i, rd, wr, eng="sp"):
        P.op(eng, lambda e: e.dma_start(out=o, in_=i), reads=rd, writes=wr, dma=True)

    def STO(o, i, rd, pw, eng="sp"):
        P.op(eng, lambda e: e.dma_start(out=o, in_=i), reads=rd, pwrites=pw, dma=True)

    def rstd_from_ssq(ps_ap, psb, n, tmp, tb, out, ob, width):
        TS("dve", tmp, ps_ap, 1.0 / n, ALU.mult, [psb], [tb], EPS, ALU.add)
        ACT(tmp, tmp, AF.Sqrt, [tb], [tb])
        RCP(out, tmp, [tb], [ob])

    def fm_norm(x3, xb, gcol, sq3, sqb, xn3, xnb, rs, rsb, tmp, tb, src_for_sq=None):
        for k in range(8):
            ACT(sq3[:, k, :], x3[:, k, :], AF.Square, [xb], [sqb])
        pt, pb = psn()
        for k in range(8):
            MM(pt[:, :], ones_b, sq3[:, k, :], k == 0, k == 7, [sqb, B_c], [pb])
        rstd_from_ssq(pt[:, :], pb, float(D), tmp, tb, rs, rsb, 512)
        if xn3 is not None:
            for k in range(8):
                STT(xn3[:, k, :], x3[:, k, :], gains[:, gcol + k:gcol + k + 1], rs, ALU.mult, ALU.mult,
                    [xb, rsb, B_par], [xnb])

    B_wbf = Buf("wbf")
    wbf = {}
    for (wname, wsrc, K_, N_) in (("in", w_in, D, NIN), ("ap", w_ap, 512, D), ("mp", w_mp, D, D), ("o", w_o, D, D),
                                  ("f1", w_f1, D, 2 * FH), ("f2", w_f2, FH, D)):
        wdst = dram("wbf_" + wname, [L, K_, N_], BF16)
        wbf[wname] = wdst
        for l_ in range(L):
            for k0 in range(0, K_ // 128, 8):
                nk = min(8, K_ // 128 - k0)
                for c0 in range(0, N_, 512):
                    ncol = min(512, N_ - c0)
                    t3, tb = load_w(wview(wsrc[l_], k0 * 128, nk, c0, ncol))
                    STO(wview(wdst[l_], k0 * 128, nk, c0, ncol), t3, [tb], [B_wbf])
    w_in, w_ap, w_mp, w_o, w_f1, w_f2 = (wbf[n_] for n_ in ("in", "ap", "mp", "o", "f1", "f2"))
    _wsrc_buf.append(B_wbf)

    class _Stop(Exception):
        pass

    def chk(name):
        _MARKS.append((name, {e_: len(P.lists[e_]) for e_ in ENGS}))
        if dbg == name:
            raise _Stop()

    def _main_body():
        for si, S in enumerate(seqs):
            NT = S // 512
            new_phase()
            xa2 = [fa.carve(4096, "xa") for _ in range(2)]
            xs2 = [fa.carve(4096, "xs")] * 2
            for it in range(NT):
                t0 = it * 512
                xa, xab = xa2[it % 2]
                xs, xsb = xs2[it % 2]
                xa3 = xa.rearrange("p (j c) -> p j c", j=4)
                xs3 = xs.rearrange("p (k t) -> p k t", k=8)
                LD(xa3, xin[si][t0:t0 + 512, :].rearrange("(j p) c -> p j c", p=128), [B_in], [xab])
                for k in range(8):
                    pt, pb = psn()
                    for j in range(4):
                        TR(pt[:, j * 128:(j + 1) * 128], xa3[:, j, k * 128:(k + 1) * 128], ident_f, [xab], [pb])
                    if k % 2:
                        CP("dve", xs3[:, k, :], pt[:, :], [pb], [xsb])
                    else:
                        ACT(xs3[:, k, :], pt[:, :], AF.Copy, [pb], [xsb])
                STO(xT.rearrange("(k p) s -> p k s", p=128)[:, :, t0:t0 + 512], xs3, [xsb], [B_xT])

            chk('pro')
            for l in range(L):
                LD(gains[:], gains_d[l], [], [B_par])
                LD(bconv[:], bconv_d[l], [], [B_par])
                LD(wconv[:], wconv_d[l], [], [B_par])
                LD(gnorm[:], gnorm_d[l:l + 1, :].partition_broadcast(128), [], [B_par])
                LD(gbias[:], gbias_d[l:l + 1, :].partition_broadcast(128), [], [B_par])
                wl = w_in[l]

                new_phase()
                xaA = [fa.carve(4096, "xaA") for _ in range(2)]
                rsA = fa.carve(512, "rsA")
                tmA = fa.carve(512, "tmA")
                gtA = fa.carve(64, "gtA")
                sqA = ba.carve(4096, "sqA")
                xnA = ba.carve(4096, "xnA")
                fmA = [ba.carve(2048, "fmA") for _ in range(2)]
                vaA = ba.carve(4 * 520, "vaA")
                mvA = ba.carve(4 * 1028, "mvA")
                moA = ba.carve(4096, "moA")
                MS("pool", vaA[0], 1.0, [vaA[1]])
                MS("pool", mvA[0], 1.0, [mvA[1]])
                fm_groups = []
                for c0 in range(0, 2 * ng * 512, 512):
                    fm_groups.append((c0, aqkT, c0, B_aqk))
                for c0 in range(0, 2048, 512):
                    fm_groups.append((3584 + c0, mpre, c0, B_mpre))
                for c0 in range(0, 2048, 512):
                    fm_groups.append((7696 + c0, bgT, c0, B_bg))
                fctr = 0
                for it in range(NT):
                    t0 = it * 512
                    xa, xab = xaA[it % 2]
                    x3 = xa.rearrange("p (k t) -> p k t", k=8)
                    LD(x3, xT.rearrange("(k p) s -> p k s", p=128)[:, :, t0:t0 + 512], [B_xT], [xab])
                    sq3 = sqA[0].rearrange("p (k t) -> p k t", k=8)
                    xn3 = xnA[0].rearrange("p (k t) -> p k t", k=8)
                    fm_norm(x3, xab, 0, sq3, sqA[1], xn3, xnA[1], rsA[0], rsA[1], tmA[0], tmA[1])
                    for (wc, dst, r0, dbuf) in fm_groups:
                        w3, wb = load_w(wview(wl, 0, 8, wc, 512))
                        fm, fmb = fmA[fctr % 2]
                        fctr += 1
                        fm3 = fm.rearrange("p (o t) -> p o t", o=4)
                        for oc in range(4):
                            pt, pb = psn()
                            for k in range(8):
                                MM(pt[:, :], w3[:, k, oc * 128:(oc + 1) * 128], xn3[:, k, :], k == 0, k == 7,
                                   [wb, xnA[1]], [pb])
                            if oc % 2:
                                CP("dve", fm3[:, oc, :], pt[:, :], [pb], [fmb])
                            else:
                                ACT(fm3[:, oc, :], pt[:, :], AF.Copy, [pb], [fmb])
                        STO(dst[r0:r0 + 512, t0:t0 + 512].rearrange("(o p) s -> p o s", p=128), fm3, [fmb], [dbuf])
                    va4 = vaA[0].rearrange("p (j h e) -> p j h e", j=4, h=8)
                    mv4 = mvA[0].rearrange("p (j h e) -> p j h e", j=4, h=4)
                    mo3 = moA[0].rearrange("p (j c) -> p j c", j=4)
                    gt3 = gtA[0].rearrange("p (j c) -> p j c", j=4)
                    tm_groups = [(3072, 512, "av", 0), (5632, 512, "mv", 0), (6144, 512, "mv", 1),
                                 (6656, 512, "mo", 0), (7168, 512, "mo", 1), (7680, 16, "mg", 0)]
                    for (wc, ncol, kind, half) in tm_groups:
                        w3, wb = load_w(wview(wl, 0, 8, wc, ncol))
                        for j in range(4):
                            pt, pb = psn()
                            for k in range(8):
                                MM(pt[:, 0:ncol], xn3[:, k, j * 128:(j + 1) * 128], w3[:, k, :], k == 0, k == 7,
                                   [wb, xnA[1]], [pb])
                            if kind == "av":
                                CP("dve", va4[:, j, :, 0:64], pt[:, :].rearrange("p (h e) -> p h e", h=8), [pb], [vaA[1]])
                            elif kind == "mv":
                                ACT(mv4[:, j, 2 * half:2 * half + 2, 0:256], pt[:, :].rearrange("p (h e) -> p h e", h=2),
                                    AF.Copy, [pb], [mvA[1]])
                            elif kind == "mo":
                                CP("dve", mo3[:, j, half * 512:(half + 1) * 512], pt[:, :], [pb], [moA[1]])
                            else:
                                TT("dve", gt3[:, j, :], pt[:, 0:16], gbias[:, :], ALU.add, [pb, B_par], [gtA[1]])
                    rows = lambda d_, w_: d_[t0:t0 + 512, :].rearrange("(j p) c -> p j c", p=128)
                    STO(rows(vaug_d, 520), vaA[0].rearrange("p (j c) -> p j c", j=4), [vaA[1]], [B_vaug])
                    STO(rows(mvaug_d, 1028), mvA[0].rearrange("p (j c) -> p j c", j=4), [mvA[1]], [B_mvaug])
                    STO(rows(mo_d, D), mo3, [moA[1]], [B_mo])
                    STO(rows(gates_d, 16), gt3, [gtA[1]], [B_gates])

                chk('A')
                new_phase()
                dmax = max(dils)
                qsB = ba.carve(4 * 128 * dmax, "qs")
                ksB = ba.carve(4 * 256 * dmax, "ks")
                vtB = [[ba.carve(520, "vt") for _ in range(2)] for _ in range(2)]
                ptB = [[ba.carve(512, "pt") for _ in range(2)] for _ in range(2)]
                exB = [fa.carve(512, "ex") for _ in range(2)]
                osB = [fa.carve(520, "os") for _ in range(2)]
                bctr = 0
                import os as _os
                _skip = set(_os.environ.get("PB_SKIP", "").split(","))
                for g, d in enumerate(dils):
                    if ("g%d" % g) in _skip:
                        continue
                    sr = S // d
                    nb = sr // 128
                    for b in range(nb):
                        q3 = qsB[0][:, 0:4 * 128 * d].rearrange("p (h t) -> p h t", h=4)
                        k3 = ksB[0][:, 0:4 * 256 * d].rearrange("p (h t) -> p h t", h=4)
                        LD(q3, aqkT[g * 512:(g + 1) * 512, d * 128 * b:d * 128 * (b + 1)].rearrange("(h p) t -> p h t", p=128),
                           [B_aqk], [qsB[1]])
                        klo = d * (128 * b - 64)
                        khi = d * (128 * b + 192)
                        clo, chi = max(klo, 0), min(khi, S)
                        if clo > klo:
                            MS("pool", k3[:, :, 0:clo - klo], 0.0, [ksB[1]], partial=True)
                        if chi < khi:
                            MS("pool", k3[:, :, chi - klo:khi - klo], 0.0, [ksB[1]], partial=True)
                        wrk = {"pwrites": [ksB[1]]}
                        ksrc = aqkT[ng * 512 + g * 512:ng * 512 + (g + 1) * 512, clo:chi].rearrange("(h p) t -> p h t", p=128)
                        kdst = k3[:, :, clo - klo:chi - klo]
                        P.op("sp", lambda e, o=kdst, i=ksrc: e.dma_start(out=o, in_=i), reads=[B_aqk], dma=True, **wrk)
                        for c in range(d):
                            vts = []
                            for j in range(2):
                                m = b + j
                                vt, vtb = vtB[j][bctr % 2]
                                base = c + d * (128 * m - 64)
                                if m == 0:
                                    MS("pool", vt[0:64, :], 0.0, [vtb], partial=True)
                                    STO(vt[64:128, :], vaug_d[srange(c, 64, d), :], [B_vaug], [vtb])
                                elif m == nb:
                                    MS("pool", vt[64:128, :], 0.0, [vtb], partial=True)
                                    STO(vt[0:64, :], vaug_d[srange(base, 64, d), :], [B_vaug], [vtb])
                                else:
                                    STO(vt[:, :], vaug_d[srange(base, 128, d), :], [B_vaug], [vtb])
                                    if bound and (128 * m * d) % bound == 0:
                                        rs_ = slice(0, 64) if j == 0 else slice(64, 128)
                                        TS("pool", vt[rs_, :], vt[rs_, :], keep[rs_, 0:1], ALU.mult, [vtb, B_c], [vtb])
                                vts.append((vt.rearrange("p (h e) -> p h e", h=8), vtb))
                            osb_t, osb = osB[bctr % 2]
                            if "mm" in _skip:
                                MS("pool", osb_t, 0.0, [osb])
                            for hg in range(2 if "mm" not in _skip else 0):
                                pts = []
                                for j in range(2):
                                    pt_, pb = psn(0, 4)
                                    for hh in range(4):
                                        h = 2 * hh + hg
                                        hp, pp = h // 2, (h % 2) * 64
                                        kk = k3[pp:pp + 64, hp, srange(c + 128 * j * d, 128, d)]
                                        qq = q3[pp:pp + 64, hp, srange(c, 128, d)]
                                        MM(pt_[:, hh * 128:(hh + 1) * 128], kk, qq, True, True, [ksB[1], qsB[1]], [pb])
                                    ex, exb = exB[j]
                                    ACT(ex, pt_[:, :], AF.Exp, [pb], [exb], scale=0.125)
                                    pT, pTb = ptB[j][hg]
                                    TT("dve", pT, ex, mtab(g, j, hg * 4, 4), ALU.mult, [exb, B_c], [pTb])
                                    pts.append((pT, pTb))
                                po, pob = psn(4, 8)
                                for hh in range(4):
                                    h = 2 * hh + hg
                                    for j in range(2):
                                        MM(po[:, hh * 65:(hh + 1) * 65], pts[j][0][:, hh * 128:(hh + 1) * 128],
                                           vts[j][0][:, h, :], j == 0, j == 1, [pts[j][1], vts[j][1]], [pob])
                                os3 = osb_t.rearrange("p (h e) -> p h e", h=8)
                                po3 = po[:, 0:260].rearrange("p (h e) -> p h e", h=4)
                                if hg == 0:
                                    ACT(os3[:, 0:8:2, :], po3, AF.Copy, [pob], [osb])
                                else:
                                    CP("dve", os3[:, 1:8:2, :], po3, [pob], [osb])
                            r0 = c + d * 128 * b
                            STO(og_d[g][srange(r0, 128, d), :], osb_t, [osb], [B_og[g]])
                            bctr += 1

                chk('B')
                new_phase()
                dgC = [ba.carve(640, "dg") for _ in range(2)]
                prC = [ba.carve(516, "pr") for _ in range(3)]
                scC = [ba.carve(512, "sc") for _ in range(3)]
                cctr = 0
                for fc in range(16):
                    dg, dgb = dgC[fc % 2]
                    dg3 = dg.rearrange("p (j c) -> p j c", j=5)
                    for j in range(5):
                        TS("dve", dg3[:, j, :], ident_b, wconv[:, fc * 5 + j:fc * 5 + j + 1], ALU.mult, [B_c, B_par], [dgb])
                    for it in range(NT):
                        t0 = it * 512
                        pr, prb = prC[cctr % 3]
                        sc, scb = scC[cctr % 3]
                        cctr += 1
                        lo, hi = t0 - 2, t0 + 514
                        clo, chi = max(lo, 0), min(hi, S)
                        if clo > lo:
                            MS("pool", pr[:, 0:2], 0.0, [prb], partial=True)
                        if chi < hi:
                            MS("pool", pr[:, 514:516], 0.0, [prb], partial=True)
                        STO(pr[:, clo - lo:chi - lo], mpre[fc * 128:(fc + 1) * 128, clo:chi], [B_mpre], [prb])
                        if bound and t0 > 0 and t0 % bound == 0:
                            TS("pool", pr[:, 0:2], pr[:, 0:2], keep[:, 0:1], ALU.mult, [prb, B_c], [prb])
                        if bound and t0 + 512 < S and (t0 + 512) % bound == 0:
                            TS("pool", pr[:, 514:516], pr[:, 514:516], keep[:, 0:1], ALU.mult, [prb, B_c], [prb])
                        pt, pb = psn()
                        for j in range(5):
                            MM(pt[:, :], dg3[:, j, :], pr[:, j:j + 512], j == 0, j == 4, [dgb, prb], [pb])
                        ACT(sc, pt[:, :], AF.Silu, [pb, B_par], [scb], bias=bconv[:, fc:fc + 1])
                        STO(mqkT[fc * 128:(fc + 1) * 128, t0:t0 + 512], sc, [scb], [B_mqk])

                chk('C')
                for dr in range(2):
                    new_phase()
                    maskf = maskU_f if dr == 0 else maskL_f
                    qTD = ba.carve(4096, "qTD")
                    kTD = ba.carve(4096, "kTD")
                    ktm = ba.carve(4096, "ktm")
                    mvD = ba.carve(4 * 1028, "mvD")
                    CbD = ba.carve(8 * 257, "Cb")
                    scD = [ba.carve(128, "scT") for _ in range(4)]
                    vtD = [ba.carve(257, "vtl") for _ in range(4)]
                    vpD = [ba.carve(257, "vpr") for _ in range(4)]
                    hst = fa.carve(4096, "hst")
                    CfD = fa.carve(8 * 257, "Cf")
                    gtD = fa.carve(64, "gtD")
                    gsm = {n: fa.carve(16, n) for n in ["e1", "sp", "tmp", "tmp2", "eb", "ebt", "ecum", "etot"]}
                    dnD = [fa.carve(2, "dn") for _ in range(4)]
                    CfH = [CfD[1]] + [Buf("CfH%d" % h_) for h_ in range(1, 4)]
                    CbH = [CbD[1]] + [Buf("CbH%d" % h_) for h_ in range(1, 4)]
                    for h_ in range(1, 4):
                        CfH[h_].r = list(CfD[1].r)
                        CbH[h_].r = list(CbD[1].r)
                        fa.bufs.append(CfH[h_])
                        ba.bufs.append(CbH[h_])
                    MS("pool", CfD[0], 0.0, CfH)
                    MS("pool", CbD[0], 0.0, CbH)
                    Cf3 = CfD[0].rearrange("p (c e) -> p c e", c=8)
                    Cb3 = CbD[0].rearrange("p (c e) -> p c e", c=8)
                    order = list(range(NT)) if dr == 0 else list(range(NT - 1, -1, -1))
                    for it in order:
                        t0 = it * 512
                        if bound and ((dr == 0 and t0 > 0 and t0 % bound == 0) or
                                      (dr == 1 and t0 + 512 < S and (t0 + 512) % bound == 0)):
                            TS("dve", CfD[0], CfD[0], keep[:, 0:1], ALU.mult, CfH + [B_c], CfH)
                            TS("pool", CbD[0], CbD[0], keep[:, 0:1], ALU.mult, CbH + [B_c], CbH)
                        q3 = qTD[0].rearrange("p (c t) -> p c t", c=8)
                        k3 = kTD[0].rearrange("p (c t) -> p c t", c=8)
                        LD(q3, mqkT[0:1024, t0:t0 + 512].rearrange("(c p) t -> p c t", p=128), [B_mqk], [qTD[1]])
                        LD(k3, mqkT[1024:2048, t0:t0 + 512].rearrange("(c p) t -> p c t", p=128), [B_mqk], [kTD[1]])
                        mv3 = mvD[0].rearrange("p (j c) -> p j c", j=4)
                        LD(mv3, mvaug_d[t0:t0 + 512, :].rearrange("(j p) c -> p j c", p=128), [B_mvaug], [mvD[1]])
                        gt3 = gtD[0].rearrange("p (j c) -> p j c", j=4)
                        LD(gt3, gates_d[t0:t0 + 512, :].rearrange("(j p) c -> p j c", p=128), [B_gates], [gtD[1]])
                        g3 = {n: v[0].rearrange("p (j c) -> p j c", j=4) for n, v in gsm.items()}
                        gb_ = {n: v[1] for n, v in gsm.items()}
                        li = gt3[:, :, dr * 8:dr * 8 + 4]
                        gf = gt3[:, :, dr * 8 + 4:dr * 8 + 8]
                        ACT(g3["e1"], gf, AF.Exp, [gtD[1]], [gb_["e1"]], scale=-1.0)
                        ACT(g3["sp"], g3["e1"], AF.Ln, [gb_["e1"]], [gb_["sp"]], bias=1.0)
                        pg, pgb = PS[7]
                        for j in range(4):
                            MM(pg[:, j * 4:j * 4 + 4], maskf, g3["sp"][:, j, :], True, True, [B_c, gb_["sp"]], [pgb])
                        for j in range(4):
                            MM(pg[:, 16 + j * 4:16 + j * 4 + 4], ones_f, g3["sp"][:, j, :], True, True, [B_c, gb_["sp"]], [pgb])
                        ncum = pg[:, 0:16].rearrange("p (j c) -> p j c", j=4)
                        ntot = pg[:, 16:32].rearrange("p (j c) -> p j c", j=4)
                        TT("dve", g3["tmp"], li, ncum, ALU.add, [gtD[1], pgb], [gb_["tmp"]])
                        TT("dve", g3["tmp2"], g3["tmp"], ntot, ALU.subtract, [gb_["tmp"], pgb], [gb_["tmp2"]])
                        ACT(g3["eb"], g3["tmp"], AF.Exp, [gb_["tmp"]], [gb_["eb"]], bias=-LN16)
                        ACT(g3["ebt"], g3["tmp2"], AF.Exp, [gb_["tmp2"]], [gb_["ebt"]], bias=-LN16)
                        ACT(g3["ecum"], ncum, AF.Exp, [pgb], [gb_["ecum"]])
                        ACT(g3["etot"], ntot, AF.Exp, [pgb], [gb_["etot"]], scale=-1.0)
                        kt3 = ktm[0].rearrange("p (j c) -> p j c", j=4)
                        for j in range(4):
                            pk, pkb = PS[6]
                            pkb16 = pk[:, :].bitcast(BF16)
                            for c in range(8):
                                TR(pkb16[:, c * 128:(c + 1) * 128], k3[:, c, j * 128:(j + 1) * 128], ident_b, [kTD[1]], [pkb])
                            CP("dve", kt3[:, j, :], pkb16, [pkb], [ktm[1]])
                        h3 = hst[0].rearrange("p (j c) -> p j c", j=4)
                        jorder = list(range(4)) if dr == 0 else [3, 2, 1, 0]
                        for j in jorder:
                            tsl = slice(j * 128, (j + 1) * 128)
                            for h in range(4):
                                mvh = mv3[:, j, h * 257:(h + 1) * 257]
                                ACT(vtD[h][0], mvh, AF.Copy, [mvD[1], gb_["eb"]], [vtD[h][1]], scale=g3["eb"][:, j, h:h + 1])
                                TS("pool", vpD[h][0], mvh, g3["ebt"][:, j, h:h + 1], ALU.mult, [mvD[1], gb_["ebt"]],
                                   [vpD[h][1]])
                            pS, pSb = PS[0]
                            for h in range(4):
                                for c in range(2):
                                    MM(pS[:, h * 128:(h + 1) * 128], k3[:, 2 * h + c, tsl], q3[:, 2 * h + c, tsl],
                                       c == 0, c == 1, [kTD[1], qTD[1]], [pSb])
                            for h in range(4):
                                TT("dve", scD[h][0], pS[:, h * 128:(h + 1) * 128], maskf, ALU.mult, [pSb, B_c], [scD[h][1]])
                            pcs = []
                            for h in range(4):
                                for c in range(2):
                                    pc, pcb = PS[5 + (2 * h + c) % 3]
                                    MM(pc[:, 0:257], kt3[:, j, (2 * h + c) * 128:(2 * h + c + 1) * 128], vpD[h][0], True, True,
                                       [ktm[1], vpD[h][1]], [pcb])
                                    STT(Cf3[:, 2 * h + c, :], Cf3[:, 2 * h + c, :], g3["etot"][:, j, h:h + 1], pc[:, 0:257],
                                        ALU.mult, ALU.add, [CfH[h], gb_["etot"], pcb], [CfH[h]])
                            for h in range(4):
                                ph, phb = PS[1 + h]
                                MM(ph[:, 0:257], scD[h][0], vtD[h][0], True, False, [scD[h][1], vtD[h][1]], [phb])
                                for c in range(2):
                                    MM(ph[:, 0:257], q3[:, 2 * h + c, tsl], Cb3[:, 2 * h + c, :], False, c == 1,
                                       [qTD[1], CbH[h]], [phb])
                            for h in range(4):
                                ACT(Cb3[:, 2 * h:2 * h + 2, :], Cf3[:, 2 * h:2 * h + 2, :], AF.Copy, [CfH[h]], [CbH[h]])
                            for h in range(4):
                                ph, phb = PS[1 + h]
                                dn, dnb = dnD[h]
                                ACT(dn[:, 0:1], ph[:, 256:257], AF.Abs, [phb], [dnb])
                                TS("dve", dn[:, 0:1], dn[:, 0:1], g3["ecum"][:, j, h:h + 1], ALU.max, [dnb, gb_["ecum"]], [dnb])
                                RCP(dn[:, 1:2], dn[:, 0:1], [dnb], [dnb])
                                ACT(h3[:, j, h * 256:(h + 1) * 256], ph[:, 0:256], AF.Copy, [phb, dnb], [hst[1]],
                                    scale=dn[:, 1:2])
                        STO(h_d[dr][t0:t0 + 512, :].rearrange("(j p) c -> p j c", p=128), h3, [hst[1]], [B_h[dr]])

                chk('D')
                new_phase()
                xE = fa.carve(4096, "xE")
                yE = fa.carve(4096, "yE")
                hfE = fa.carve(1024, "hf")
                hbE = fa.carve(1024, "hb")
                sgE = fa.carve(1024, "sg")
                ogE = [fa.carve(520, "og%d" % g) for g in range(ng)]
                smE = fa.carve(32, "smE")
                gaE = fa.carve(512, "ga")
                gmE = fa.carve(512, "gm")
                t1E = fa.carve(512, "t1")
                t2E = fa.carve(512, "t2")
                rsE = fa.carve(512, "rsE")
                tmE = fa.carve(512, "tmE")
                moE = ba.carve(1024, "moE")
                hnE = ba.carve(1024, "hnE")
                aoE = ba.carve(512, "aoE")
                mlT = ba.carve(4096, "mlT")
                atT = ba.carve(2048, "atT")
                bgE = [ba.carve(512, "bgE") for _ in range(2)]
                mgE = ba.carve(4096, "mgE")
                sqE = ba.carve(4096, "sqE")
                xnE = ba.carve(4096, "xnE")
                acE = ba.carve(22 * 512, "acE")
                last = (l == L - 1)
                for it in range(NT):
                    t0 = it * 512
                    x3 = xE[0].rearrange("p (k t) -> p k t", k=8)
                    y3 = yE[0].rearrange("p (k t) -> p k t", k=8)
                    LD(x3, xT.rearrange("(k p) s -> p k s", p=128)[:, :, t0:t0 + 512], [B_xT], [xE[1]])
                    ml3 = mlT[0].rearrange("p (k t) -> p k t", k=8)
                    at3 = atT[0].rearrange("p (k t) -> p k t", k=4)
                    for j in range(4):
                        r0 = t0 + j * 128
                        LD(hfE[0], h_d[0][r0:r0 + 128, :], [B_h[0]], [hfE[1]])
                        LD(hbE[0], h_d[1][r0:r0 + 128, :], [B_h[1]], [hbE[1]])
                        LD(moE[0], mo_d[r0:r0 + 128, :], [B_mo], [moE[1]])
                        TT("dve", hfE[0], hfE[0], hbE[0], ALU.add, [hfE[1], hbE[1]], [hfE[1]])
                        sm = smE[0]
                        for h in range(4):
                            ACT(sgE[0][:, h * 256:(h + 1) * 256], hfE[0][:, h * 256:(h + 1) * 256], AF.Square,
                                [hfE[1]], [sgE[1], smE[1]], accum=sm[:, h:h + 1])
                        rstd_from_ssq(sm[:, 0:4], smE[1], 256.0, sm[:, 4:8], smE[1], sm[:, 8:12], smE[1], 4)
                        ACT(sgE[0], moE[0], AF.Sigmoid, [moE[1]], [sgE[1]])
                        TT("dve", sgE[0], sgE[0], gnorm[:, :], ALU.mult, [sgE[1], B_par], [sgE[1]])
                        for h in range(4):
                            hs = slice(h * 256, (h + 1) * 256)
                            STT(hnE[0][:, hs], hfE[0][:, hs], sm[:, 8 + h:9 + h], sgE[0][:, hs], ALU.mult, ALU.mult,
                                [hfE[1], smE[1], sgE[1]], [hnE[1]])
                        pk, pkb = psn()
                        pk16 = pk[:, :].bitcast(BF16)
                        for k in range(8):
                            TR(pk16[:, k * 128:(k + 1) * 128], hnE[0][:, k * 128:(k + 1) * 128], ident_b, [hnE[1]], [pkb])
                        CP("dve", ml3[:, :, j * 128:(j + 1) * 128], pk16.rearrange("p (k t) -> p k t", k=8), [pkb], [mlT[1]])
                        for g in range(ng):
                            LD(ogE[g][0], og_d[g][r0:r0 + 128, :], [B_og[g]], [ogE[g][1]])
                        for g in range(1, ng):
                            TT("dve", ogE[0][0], ogE[0][0], ogE[g][0], ALU.add, [ogE[0][1], ogE[g][1]], [ogE[0][1]])
                        o3 = ogE[0][0].rearrange("p (h e) -> p h e", h=8)
                        RCP(sm[:, 16:24], o3[:, :, 64], [ogE[0][1]], [smE[1]])
                        ao3 = aoE[0].rearrange("p (h e) -> p h e", h=8)
                        for h in range(8):
                            TS("dve", ao3[:, h, :], o3[:, h, 0:64], sm[:, 16 + h:17 + h], ALU.mult,
                               [ogE[0][1], smE[1]], [aoE[1]])
                        pk, pkb = psn()
                        pk16 = pk[:, :].bitcast(BF16)
                        for k in range(4):
                            TR(pk16[:, k * 128:(k + 1) * 128], aoE[0][:, k * 128:(k + 1) * 128], ident_b, [aoE[1]], [pkb])
                        CP("dve", at3[:, :, j * 128:(j + 1) * 128], pk16[:, 0:512].rearrange("p (k t) -> p k t", k=4),
                           [pkb], [atT[1]])
                    mg3 = mgE[0].rearrange("p (k t) -> p k t", k=8)
                    for cg in range(2):
                        wa3, wab = load_w(wview(w_ap[l], 0, 4, cg * 512, 512))
                        wm3, wmb = load_w(wview(w_mp[l], 0, 8, cg * 512, 512))
                        for oc in range(4):
                            fcx = cg * 4 + oc
                            LD(bgE[0][0], bgT[fcx * 128:(fcx + 1) * 128, t0:t0 + 512], [B_bg], [bgE[0][1]])
                            LD(bgE[1][0], bgT[1024 + fcx * 128:1024 + (fcx + 1) * 128, t0:t0 + 512], [B_bg], [bgE[1][1]])
                            pa, pab = psn()
                            for k in range(4):
                                MM(pa[:, :], wa3[:, k, oc * 128:(oc + 1) * 128], at3[:, k, :], k == 0, k == 3,
                                   [wab, atT[1]], [pab])
                            pm, pmb = psn()
                            for k in range(8):
                                MM(pm[:, :], wm3[:, k, oc * 128:(oc + 1) * 128], ml3[:, k, :], k == 0, k == 7,
                                   [wmb, mlT[1]], [pmb])
                            ACT(gaE[0], bgE[0][0], AF.Sigmoid, [bgE[0][1]], [gaE[1]])
                            ACT(gmE[0], bgE[1][0], AF.Sigmoid, [bgE[1][1]], [gmE[1]])
                            TT("dve", t1E[0], pa[:, :], gaE[0], ALU.mult, [pab, gaE[1]], [t1E[1]])
                            TT("dve", t2E[0], pm[:, :], gmE[0], ALU.mult, [pmb, gmE[1]], [t2E[1]])
                            TT("dve", mg3[:, fcx, :], t1E[0], t2E[0], ALU.add, [t1E[1], t2E[1]], [mgE[1]])

                    def proj_norm_res(w2, nkt, src3, srcb, gcol):
                        kgs = [(k0, min(8, nkt - k0)) for k0 in range(0, nkt, 8)]
                        for cg in range(2):
                            pss = [psn() for _ in range(4)]
                            for gi, (k0, nk) in enumerate(kgs):
                                w3, wb = load_w(wview(w2, k0 * 128, nk, cg * 512, 512))
                                for oc in range(4):
                                    for k in range(nk):
                                        MM(pss[oc][0][:, :], w3[:, k, oc * 128:(oc + 1) * 128], src3[:, k0 + k, :],
                                           gi == 0 and k == 0, gi == len(kgs) - 1 and k == nk - 1, [wb, srcb], [pss[oc][1]])
                            for oc in range(4):
                                ACT(y3[:, cg * 4 + oc, :], pss[oc][0][:, :], AF.Copy, [pss[oc][1]], [yE[1]])
                        sq3 = sqE[0].rearrange("p (k t) -> p k t", k=8)
                        fm_norm(y3, yE[1], gcol, sq3, sqE[1], None, None, rsE[0], rsE[1], tmE[0], tmE[1])
                        for k in range(8):
                            STT(y3[:, k, :], y3[:, k, :], gains[:, gcol + k:gcol + k + 1], rsE[0], ALU.mult, ALU.mult,
                                [yE[1], rsE[1], B_par], [yE[1]])
                            TT("dve", x3[:, k, :], x3[:, k, :], y3[:, k, :], ALU.add, [xE[1], yE[1]], [xE[1]])

                    proj_norm_res(w_o[l], 8, mg3, mgE[1], 8)
                    sq3 = sqE[0].rearrange("p (k t) -> p k t", k=8)
                    xn3 = xnE[0].rearrange("p (k t) -> p k t", k=8)
                    fm_norm(x3, xE[1], 16, sq3, sqE[1], xn3, xnE[1], rsE[0], rsE[1], tmE[0], tmE[1])
                    ac3 = acE[0].rearrange("p (k t) -> p k t", k=22)
                    for i0 in range(0, 22, 4):
                        nci = min(4, 22 - i0)
                        wg3, wgb = load_w(wview(w_f1[l], 0, 8, i0 * 128, nci * 128))
                        wu3, wub = load_w(wview(w_f1[l], 0, 8, FH + i0 * 128, nci * 128))
                        for ii in range(nci):
                            pgt, pgtb = psn()
                            for k in range(8):
                                MM(pgt[:, :], wg3[:, k, ii * 128:(ii + 1) * 128], xn3[:, k, :], k == 0, k == 7,
                                   [wgb, xnE[1]], [pgtb])
                            put, putb = psn()
                            for k in range(8):
                                MM(put[:, :], wu3[:, k, ii * 128:(ii + 1) * 128], xn3[:, k, :], k == 0, k == 7,
                                   [wub, xnE[1]], [putb])
                            ACT(t1E[0], pgt[:, :], AF.Silu, [pgtb], [t1E[1]])
                            TT("dve", ac3[:, i0 + ii, :], put[:, :], t1E[0], ALU.mult, [putb, t1E[1]], [acE[1]])
                    proj_norm_res(w_f2[l], 22, ac3, acE[1], 24)
                    if not last:
                        STO(xT.rearrange("(k p) s -> p k s", p=128)[:, :, t0:t0 + 512], x3, [xE[1]], [B_xT])
                    else:
                        o3_ = yE[0].rearrange("p (j c) -> p j c", j=4)
                        for j in range(4):
                            for kk in range(0, 8, 4):
                                pt, pb = psn()
                                for k in range(kk, kk + 4):
                                    TR(pt[:, (k - kk) * 128:(k - kk + 1) * 128], x3[:, k, j * 128:(j + 1) * 128], ident_f,
                                       [xE[1]], [pb])
                                if kk:
                                    CP("dve", o3_[:, j, kk * 128:(kk + 4) * 128], pt[:, :], [pb], [yE[1]])
                                else:
                                    ACT(o3_[:, j, kk * 128:(kk + 4) * 128], pt[:, :], AF.Copy, [pb], [yE[1]])
                        STO(yout[si][t0:t0 + 512, :].rearrange("(j p) c -> p j c", p=128), o3_, [yE[1]], [B_out])
    try:
        _main_body()
    except _Stop:
        pass
    P.emit()
    st.close()
    return nc


def host_params(norm_mix_pre, norm_mix_post, norm_ffn_pre, norm_ffn_post, b_conv, w_conv):
    L = norm_mix_pre.shape[0]
    fmv = lambda v: np.ascontiguousarray(np.asarray(v, np.float32).reshape(L, -1, 128).transpose(0, 2, 1))
    gains = np.concatenate([fmv(norm_mix_pre), fmv(norm_mix_post), fmv(norm_ffn_pre), fmv(norm_ffn_post)], axis=2)
    bconv = fmv(b_conv)
    wc = np.asarray(w_conv, np.float32)
    wconv = np.ascontiguousarray(wc.reshape(L, 5, 16, 128).transpose(0, 3, 2, 1)).reshape(L, 128, 80)
    return np.ascontiguousarray(gains), bconv, wconv


_CACHE = {}
_MARKS = []


def run(seq_lists, xs_per_core, depth, dils, weights, dbg=None, bound=None, keeps=None):
    key = (tuple(seq_lists), depth, tuple(dils), dbg, bound)
    if key not in _CACHE:
        _CACHE[key] = build(list(seq_lists), depth, tuple(dils), dbg, bound)
    nc = _CACHE[key]
    gains, bconv, wconv = host_params(weights["norm_mix_pre"], weights["norm_mix_post"], weights["norm_ffn_pre"],
                                      weights["norm_ffn_post"], weights["b_conv"], weights["w_conv"])
    common = {
        "w_in": np.ascontiguousarray(weights["w_in"], np.float32),
        "w_attn_proj": np.ascontiguousarray(weights["w_attn_proj"], np.float32),
        "w_mlstm_proj": np.ascontiguousarray(weights["w_mlstm_proj"], np.float32),
        "w_out": np.ascontiguousarray(weights["w_out"], np.float32),
        "w_ffn_in": np.ascontiguousarray(weights["w_ffn_in"], np.float32),
        "w_ffn_out": np.ascontiguousarray(weights["w_ffn_out"], np.float32),
        "gains": gains, "bconv": bconv, "wconv": wconv,
        "gnorm": np.ascontiguousarray(weights["g_mlstm_norm"], np.float32),
        "gbias": np.ascontiguousarray(weights["b_mlstm_gates"], np.float32),
        "consts": make_consts(dils),
    }
    in_maps = []
    for ci, xs in enumerate(xs_per_core):
        m = dict(common)
        m["keep"] = np.full((128, 1), 1.0 if keeps is None else float(keeps[ci]), np.float32)
        for i, x in enumerate(xs):
            m["x%d" % i] = np.ascontiguousarray(x, np.float32)
        in_maps.append(m)
    res = run_bass_kernel_spmd(nc, in_maps, core_ids=list(range(len(in_maps))))
    return res.results


def kernel(x_prompt, x_sample, norm_mix_pre, norm_mix_post, norm_ffn_pre, norm_ffn_post, w_in, b_mlstm_gates,
           w_conv, b_conv, g_mlstm_norm, w_attn_proj, w_mlstm_proj, w_out, w_ffn_in, w_ffn_out):
    weights = dict(norm_mix_pre=norm_mix_pre, norm_mix_post=norm_mix_post, norm_ffn_pre=norm_ffn_pre,
                   norm_ffn_post=norm_ffn_post, w_in=w_in, b_mlstm_gates=b_mlstm_gates, w_conv=w_conv, b_conv=b_conv,
                   g_mlstm_norm=g_mlstm_norm, w_attn_proj=w_attn_proj, w_mlstm_proj=w_mlstm_proj, w_out=w_out,
                   w_ffn_in=w_ffn_in, w_ffn_out=w_ffn_out)
    weights = {k: np.asarray(v) for k, v in weights.items()}
    x_prompt = np.asarray(x_prompt, np.float32)
    x_sample = np.asarray(x_sample, np.float32)
    depth = w_in.shape[0]
    n = 8
    nsamp, sp = x_sample.shape[0], x_sample.shape[1]
    pl = x_prompt.shape[1]
    per = pl // sp
    ngrp = nsamp // per
    xs, keeps = [[x_prompt[0]]], [1.0]
    for gi in range(ngrp):
        xs.append([x_sample[gi * per:(gi + 1) * per].reshape(pl, -1)])
        keeps.append(0.0)
    while len(xs) < n:
        xs.append([np.zeros((pl, x_prompt.shape[2]), np.float32)])
        keeps.append(0.0)
    res = run([pl], xs, depth, (1, 4, 16), weights, bound=sp, keeps=keeps)
    y_prompt = res[0]["y0"][None].astype(np.float32)
    y_sample = np.concatenate([res[1 + gi]["y0"].reshape(per, sp, -1) for gi in range(ngrp)], axis=0).astype(np.float32)
    return (y_prompt, y_sample)
```

```python
import math
from contextlib import ExitStack

import numpy as np
import concourse.bass as bass
import concourse.mybir as mybir
from concourse.bass_utils import run_bass_kernel_spmd

F32 = mybir.dt.float32
BF16 = mybir.dt.bfloat16
AF = mybir.ActivationFunctionType
ALU = mybir.AluOpType

ENGS = ("pe", "act", "dve", "pool", "sp")
D = 1024
NIN = 9744
FH = 2816
EPS = 1e-6
LN16 = math.log(16.0)


class Buf:
    __slots__ = ("name", "w", "r")

    def __init__(self, name=""):
        self.name = name
        self.w = []
        self.r = []


class Ins:
    __slots__ = ("eng", "fn", "deps", "inc", "val", "dma", "dsem", "dval", "idx")

    def __init__(self, eng, fn, dma):
        self.eng = eng
        self.fn = fn
        self.deps = []
        self.inc = False
        self.val = 0
        self.dma = dma
        self.dsem = -1
        self.dval = 0
        self.idx = -1


def _reduce(lst):
    best = {}
    for d in lst:
        key = (d.eng, d.dsem) if d.dma else (d.eng, -1)
        cur = best.get(key)
        if cur is None or (d.dval > cur.dval if d.dma else d.idx > cur.idx):
            best[key] = d
    return list(best.values())


class Prog:
    NDSEM = 8

    def __init__(self, nc):
        self.nc = nc
        self.lists = {e: [] for e in ENGS}
        self.dma_count = {e: 0 for e in ENGS}

    def op(self, eng, fn, reads=(), writes=(), pwrites=(), dma=False):
        ins = Ins(eng, fn, dma)
        deps = []
        for b in reads:
            deps.extend(b.w)
        for b in writes:
            deps.extend(b.w)
            deps.extend(b.r)
        for b in pwrites:
            deps.extend(b.r)
        lst = self.lists[eng]
        ins.idx = len(lst)
        if dma:
            n = self.dma_count[eng]
            self.dma_count[eng] = n + 1
            ins.dsem = n % self.NDSEM
            ins.dval = 16 * (n // self.NDSEM + 1)
        final = []
        for d in _reduce(deps):
            if (not d.dma) and d.eng == "pe" and eng == "pe" and not dma:
                continue
            if not d.dma:
                d.inc = True
            final.append(d)
        ins.deps = final
        lst.append(ins)
        for b in writes:
            b.w = [ins]
            b.r = []
        for b in pwrites:
            b.w = _reduce(b.w + [ins])
        for b in reads:
            b.r = _reduce(b.r + [ins])
        return ins

    def emit(self):
        nc = self.nc
        with ExitStack() as st:
            esem = {e: st.enter_context(nc.semaphore("s_" + e)) for e in ENGS}
            dsem = {e: [st.enter_context(nc.semaphore("d_%s%d" % (e, i))) for i in range(self.NDSEM)]
                    for e in ("sp", "act", "pool")}
            for e in ENGS:
                c = 0
                for ins in self.lists[e]:
                    if ins.inc and not ins.dma:
                        c += 1
                        ins.val = c
            block = st.enter_context(nc.Block())
            last = {}
            for e in ("sp", "act", "pool"):
                for ins in self.lists[e]:
                    if ins.dma:
                        last[(e, ins.dsem)] = ins.dval

            def run(ename, eng):
                waited = {}

                def wait(key, sem, val):
                    if waited.get(key, 0) < val:
                        eng.wait_ge(sem, val)
                        waited[key] = val

                for ins in self.lists[ename]:
                    for d in ins.deps:
                        if d.dma:
                            wait(("d", d.eng, d.dsem), dsem[d.eng][d.dsem], d.dval)
                        else:
                            wait(("e", d.eng), esem[d.eng], d.val)
                    if ins.dma:
                        if ins.dval > 16:
                            wait(("d", ename, ins.dsem), dsem[ename][ins.dsem], ins.dval - 16)
                        ins.fn(eng).then_inc(dsem[ename][ins.dsem], 16)
                    else:
                        h = ins.fn(eng)
                        if ins.inc:
                            h.then_inc(esem[ename], 1)
                if ename == "sp":
                    for (e, k), v in last.items():
                        wait(("d", e, k), dsem[e][k], v)

            block.tensor(lambda eng: run("pe", eng))
            block.scalar(lambda eng: run("act", eng))
            block.vector(lambda eng: run("dve", eng))
            block.gpsimd(lambda eng: run("pool", eng))
            block.sync(lambda eng: run("sp", eng))


class Arena:
    def __init__(self, t, n):
        self.t = t
        self.n = n
        self.off = 0
        self.bufs = []
        self.pending = []

    def reset(self):
        pend = list(self.pending)
        for b in self.bufs:
            pend.extend(b.w)
            pend.extend(b.r)
        self.pending = _reduce(pend)
        self.bufs = []
        self.off = 0

    def carve(self, n, name=""):
        assert self.off + n <= self.n, (name, self.off, n, self.n)
        ap = self.t[:, self.off:self.off + n]
        self.off += n
        b = Buf(name)
        b.r = list(self.pending)
        self.bufs.append(b)
        return ap, b


def alibi_slopes(ng):
    h = np.arange(1, ng * 8 + 1, dtype=np.float32)
    return (2.0 ** (-8.0 * h / (ng * 8))).astype(np.float32).reshape(ng, 8)


def make_consts(dils):
    ng = len(dils)
    k = np.arange(128)[:, None]
    q = np.arange(128)[None, :]
    ident = (k == q).astype(np.float32)
    mu = (k <= q).astype(np.float32)
    ml = (k >= q).astype(np.float32)
    ones = np.ones((128, 128), np.float32)
    sl = alibi_slopes(ng)
    mt = np.zeros((128, ng, 2, 8, 128), np.float32)
    for g, d in enumerate(dils):
        for j in range(2):
            step = 128 * j - 64 + k - q
            valid = np.abs(step) <= 64
            for hi, h in enumerate([0, 2, 4, 6, 1, 3, 5, 7]):
                mt[:, g, j, hi, :] = np.where(valid, np.exp(-sl[g, h] * d * np.abs(step).astype(np.float32)), 0.0)
    return np.concatenate([ident, mu, ml, ones, mt.reshape(128, -1)], axis=1).astype(np.float32)


def build(seqs, depth, dils, dbg=None, bound=None):
    nc = bass.Bass("TRN2", target_bir_lowering=False)
    P = Prog(nc)
    st = ExitStack()
    ng = len(dils)
    SMAX = max(seqs)
    L = depth

    def dram(name, shape, dt, kind="Internal"):
        return nc.dram_tensor(name, list(shape), dt, kind=kind).ap()

    xin = [dram("x%d" % i, [s, D], F32, "ExternalInput") for i, s in enumerate(seqs)]
    yout = [dram("y%d" % i, [s, D], F32, "ExternalOutput") for i, s in enumerate(seqs)]
    w_in = dram("w_in", [L, D, NIN], F32, "ExternalInput")
    w_ap = dram("w_attn_proj", [L, 512, D], F32, "ExternalInput")
    w_mp = dram("w_mlstm_proj", [L, D, D], F32, "ExternalInput")
    w_o = dram("w_out", [L, D, D], F32, "ExternalInput")
    w_f1 = dram("w_ffn_in", [L, D, 2 * FH], F32, "ExternalInput")
    w_f2 = dram("w_ffn_out", [L, FH, D], F32, "ExternalInput")
    gains_d = dram("gains", [L, 128, 32], F32, "ExternalInput")
    bconv_d = dram("bconv", [L, 128, 16], F32, "ExternalInput")
    wconv_d = dram("wconv", [L, 128, 80], F32, "ExternalInput")
    gnorm_d = dram("gnorm", [L, D], F32, "ExternalInput")
    gbias_d = dram("gbias", [L, 16], F32, "ExternalInput")
    NCONST = 4 * 128 + ng * 2 * 8 * 128
    const_d = dram("consts", [128, NCONST], F32, "ExternalInput")
    keep_d = dram("keep", [128, 1], F32, "ExternalInput")

    xT = dram("xT", [D, SMAX], F32)
    aqkT = dram("aqkT", [2 * ng * 512, SMAX], BF16)
    vaug_d = dram("vaug", [SMAX, 520], BF16)
    mpre = dram("mpre", [2048, SMAX], BF16)
    mqkT = dram("mqkT", [2048, SMAX], BF16)
    mvaug_d = dram("mvaug", [SMAX, 1028], BF16)
    mo_d = dram("mo", [SMAX, D], BF16)
    gates_d = dram("gates", [SMAX, 16], F32)
    bgT = dram("bgT", [2048, SMAX], BF16)
    og_d = [dram("og%d" % g, [SMAX, 520], F32) for g in range(ng)]
    h_d = [dram("hdir%d" % i, [SMAX, D], F32) for i in range(2)]
    B_xT, B_aqk, B_vaug, B_mpre, B_mqk, B_mvaug, B_mo, B_gates, B_bg = [Buf(n) for n in
        "xT aqk vaug mpre mqk mvaug mo gates bg".split()]
    B_og = [Buf("og%d" % g) for g in range(ng)]
    B_h = [Buf("h0"), Buf("h1")]
    B_in = Buf("in")
    B_out = Buf("out")

    def sb(name, shape, dt):
        return st.enter_context(nc.sbuf_tensor(name, list(shape), dt))

    cf = sb("cf", [128, 512], F32)
    cb = sb("cb", [128, NCONST], BF16)
    B_c = Buf("c")
    P.op("sp", lambda e: e.dma_start(out=cf[:], in_=const_d[:, 0:512]), writes=[B_c], dma=True)
    P.op("pool", lambda e: e.dma_start(out=cb[:], in_=const_d[:, :]), pwrites=[B_c], dma=True)
    ident_f, maskU_f, maskL_f, ones_f = (cf[:, i * 128:(i + 1) * 128] for i in range(4))
    keep = sb("keeps", [128, 1], F32)
    P.op("sp", lambda e: e.dma_start(out=keep[:], in_=keep_d[:, :]), pwrites=[B_c], dma=True)
    ident_b = cb[:, 0:128]
    ones_b = cb[:, 384:512]

    def mtab(g, j, h0, nh):
        o = 512 + ((g * 2 + j) * 8 + h0) * 128
        return cb[:, o:o + nh * 128]

    gains = sb("gainss", [128, 32], F32)
    bconv = sb("bconvs", [128, 16], F32)
    wconv = sb("wconvs", [128, 80], F32)
    gnorm = sb("gnorms", [128, D], F32)
    gbias = sb("gbiass", [128, 16], F32)
    B_par = Buf("par")

    NW = 4
    wt = [sb("wt%d" % i, [128, 4096], BF16) for i in range(NW)]
    B_wt = [Buf("wt%d" % i) for i in range(NW)]
    wctr = [0]
    _wsrc_buf = []

    def load_w(src3):
        i = wctr[0] % NW
        wctr[0] += 1
        nk, ncol = src3.shape[1], src3.shape[2]
        dst = wt[i][:, 0:nk * ncol].rearrange("p (k c) -> p k c", k=nk)
        P.op("pool", lambda e: e.dma_start(out=dst, in_=src3), reads=list(_wsrc_buf), writes=[B_wt[i]], dma=True)
        return dst, B_wt[i]

    def wview(w2, r0, nk, c0, ncol):
        return w2[r0:r0 + nk * 128, c0:c0 + ncol].rearrange("(k p) c -> p k c", p=128)


    NF = 16000
    NB = 33536
    fa = Arena(sb("fa", [128, NF], F32), NF)
    ba = Arena(sb("ba", [128, NB], BF16), NB)
    PS = []
    for i in range(8):
        t = st.enter_context(nc.psum_tensor("ps%d" % i, [128, 512], F32))
        PS.append((t, Buf("ps%d" % i)))
    prr = [0]

    def psn(lo=0, hi=8):
        i = lo + prr[0] % (hi - lo)
        prr[0] += 1
        return PS[i]

    def new_phase():
        fa.reset()
        ba.reset()

    def MM(o, l, r, s, t, rd, wr):
        P.op("pe", lambda e: e.matmul(o, lhsT=l, rhs=r, start=s, stop=t), reads=rd, writes=wr)

    def TR(o, i, idn, rd, wr):
        P.op("pe", lambda e: e.transpose(o, i, idn), reads=rd + [B_c], writes=wr)

    def ACT(o, i, f, rd, wr, bias=None, scale=None, accum=None, eng="act"):
        kw = {}
        if bias is not None:
            kw["bias"] = bias
        if scale is not None:
            kw["scale"] = scale
        if accum is not None:
            kw["accum_out"] = accum
        P.op("act", lambda e: e.activation(out=o, in_=i, func=f, **kw), reads=rd, writes=wr)

    def TT(eng, o, a, b, op, rd, wr):
        P.op(eng, lambda e: e.tensor_tensor(out=o, in0=a, in1=b, op=op), reads=rd, writes=wr)

    def TS(eng, o, a, s1, op0, rd, wr, s2=None, op1=None):
        if op1 is None:
            P.op(eng, lambda e: e.tensor_scalar(out=o, in0=a, scalar1=s1, scalar2=None, op0=op0), reads=rd, writes=wr)
        else:
            P.op(eng, lambda e: e.tensor_scalar(out=o, in0=a, scalar1=s1, scalar2=s2, op0=op0, op1=op1),
                 reads=rd, writes=wr)

    def STT(o, a, s, b, op0, op1, rd, wr):
        P.op("dve", lambda e: e.scalar_tensor_tensor(out=o, in0=a, scalar=s, in1=b, op0=op0, op1=op1),
             reads=rd, writes=wr)

    def CP(eng, o, i, rd, wr):
        P.op(eng, lambda e: e.tensor_copy(out=o, in_=i), reads=rd, writes=wr)

    def RCP(o, i, rd, wr):
        P.op("dve", lambda e: e.reciprocal(out=o, in_=i), reads=rd, writes=wr)

    def MS(eng, o, v, wr, partial=False):
        if partial:
            P.op(eng, lambda e: e.memset(o, v), pwrites=wr)
        else:
            P.op(eng, lambda e: e.memset(o, v), writes=wr)

    def srange(base, n, step):
        return slice(base, base + step * (n - 1) + 1, step)

    def LD(o, i, rd, wr, eng="sp"):
        P.op(eng, lambda e: e.dma_start(out=o, in_=i), reads=rd, writes=wr, dma=True)

    sto_eng = ["act"]

    def STO(o, i, rd, pw, eng=None):
        P.op(eng or sto_eng[0], lambda e: e.dma_start(out=o, in_=i), reads=rd, pwrites=pw, dma=True)

    def LDP(o, i, rd, pw, eng="sp"):
        P.op(eng, lambda e: e.dma_start(out=o, in_=i), reads=rd, pwrites=pw, dma=True)

    def rstd_from_ssq(ps_ap, psb, n, tmp, tb, out, ob, width):
        TS("dve", tmp, ps_ap, 1.0 / n, ALU.mult, [psb], [tb], EPS, ALU.add)
        ACT(tmp, tmp, AF.Sqrt, [tb], [tb])
        RCP(out, tmp, [tb], [ob])

    def fm_norm(x3, xb, gcol, sq3, sqb, xn3, xnb, rs, rsb, tmp, tb, src_for_sq=None):
        for k in range(8):
            ACT(sq3[:, k, :], x3[:, k, :], AF.Square, [xb], [sqb])
        pt, pb = psn()
        for k in range(8):
            MM(pt[:, :], ones_b, sq3[:, k, :], k == 0, k == 7, [sqb, B_c], [pb])
        rstd_from_ssq(pt[:, :], pb, float(D), tmp, tb, rs, rsb, 512)
        if xn3 is not None:
            for k in range(8):
                STT(xn3[:, k, :], x3[:, k, :], gains[:, gcol + k:gcol + k + 1], rs, ALU.mult, ALU.mult,
                    [xb, rsb, B_par], [xnb])

    B_wbf = Buf("wbf")
    wbf = {}
    for (wname, wsrc, K_, N_) in (("in", w_in, D, NIN), ("ap", w_ap, 512, D), ("mp", w_mp, D, D), ("o", w_o, D, D),
                                  ("f1", w_f1, D, 2 * FH), ("f2", w_f2, FH, D)):
        wdst = dram("wbf_" + wname, [L, K_, N_], BF16)
        wbf[wname] = wdst
        for l_ in range(L):
            for k0 in range(0, K_ // 128, 8):
                nk = min(8, K_ // 128 - k0)
                for c0 in range(0, N_, 512):
                    ncol = min(512, N_ - c0)
                    t3, tb = load_w(wview(wsrc[l_], k0 * 128, nk, c0, ncol))
                    STO(wview(wdst[l_], k0 * 128, nk, c0, ncol), t3, [tb], [B_wbf])
    w_in, w_ap, w_mp, w_o, w_f1, w_f2 = (wbf[n_] for n_ in ("in", "ap", "mp", "o", "f1", "f2"))
    _wsrc_buf.append(B_wbf)

    class _Stop(Exception):
        pass

    def chk(name):
        _MARKS.append((name, {e_: len(P.lists[e_]) for e_ in ENGS}))
        if dbg == name:
            raise _Stop()

    def _main_body():
        for si, S in enumerate(seqs):
            NT = S // 512
            new_phase()
            xa2 = [fa.carve(4096, "xa") for _ in range(2)]
            xs2 = [fa.carve(4096, "xs")] * 2
            for it in range(NT):
                t0 = it * 512
                xa, xab = xa2[it % 2]
                xs, xsb = xs2[it % 2]
                xa3 = xa.rearrange("p (j c) -> p j c", j=4)
                xs3 = xs.rearrange("p (k t) -> p k t", k=8)
                LD(xa3, xin[si][t0:t0 + 512, :].rearrange("(j p) c -> p j c", p=128), [B_in], [xab])
                for k in range(8):
                    pt, pb = psn()
                    for j in range(4):
                        TR(pt[:, j * 128:(j + 1) * 128], xa3[:, j, k * 128:(k + 1) * 128], ident_f, [xab], [pb])
                    if k % 2:
                        CP("dve", xs3[:, k, :], pt[:, :], [pb], [xsb])
                    else:
                        ACT(xs3[:, k, :], pt[:, :], AF.Copy, [pb], [xsb])
                STO(xT.rearrange("(k p) s -> p k s", p=128)[:, :, t0:t0 + 512], xs3, [xsb], [B_xT])

            chk('pro')
            for l in range(L):
                LD(gains[:], gains_d[l], [], [B_par])
                LD(bconv[:], bconv_d[l], [], [B_par])
                LD(wconv[:], wconv_d[l], [], [B_par])
                LD(gnorm[:], gnorm_d[l:l + 1, :].partition_broadcast(128), [], [B_par])
                LD(gbias[:], gbias_d[l:l + 1, :].partition_broadcast(128), [], [B_par])
                wl = w_in[l]

                sto_eng[0] = 'act'
                new_phase()
                xaA = [fa.carve(4096, "xaA") for _ in range(2)]
                rsA = fa.carve(512, "rsA")
                tmA = fa.carve(512, "tmA")
                gtA = fa.carve(64, "gtA")
                sqA = ba.carve(4096, "sqA")
                xnA = ba.carve(4096, "xnA")
                fmA = [ba.carve(2048, "fmA") for _ in range(2)]
                vaA = ba.carve(4 * 520, "vaA")
                mvA = ba.carve(4 * 1028, "mvA")
                moA = ba.carve(4096, "moA")
                MS("pool", vaA[0], 1.0, [vaA[1]])
                MS("pool", mvA[0], 1.0, [mvA[1]])
                fm_groups = []
                for c0 in range(0, 2 * ng * 512, 512):
                    fm_groups.append((c0, aqkT, c0, B_aqk))
                for c0 in range(0, 2048, 512):
                    fm_groups.append((3584 + c0, mpre, c0, B_mpre))
                for c0 in range(0, 2048, 512):
                    fm_groups.append((7696 + c0, bgT, c0, B_bg))
                fctr = 0
                for it in range(NT):
                    t0 = it * 512
                    xa, xab = xaA[it % 2]
                    x3 = xa.rearrange("p (k t) -> p k t", k=8)
                    LD(x3, xT.rearrange("(k p) s -> p k s", p=128)[:, :, t0:t0 + 512], [B_xT], [xab])
                    sq3 = sqA[0].rearrange("p (k t) -> p k t", k=8)
                    xn3 = xnA[0].rearrange("p (k t) -> p k t", k=8)
                    fm_norm(x3, xab, 0, sq3, sqA[1], xn3, xnA[1], rsA[0], rsA[1], tmA[0], tmA[1])
                    for (wc, dst, r0, dbuf) in fm_groups:
                        w3, wb = load_w(wview(wl, 0, 8, wc, 512))
                        fm, fmb = fmA[fctr % 2]
                        fctr += 1
                        fm3 = fm.rearrange("p (o t) -> p o t", o=4)
                        for oc in range(4):
                            pt, pb = psn()
                            for k in range(8):
                                MM(pt[:, :], w3[:, k, oc * 128:(oc + 1) * 128], xn3[:, k, :], k == 0, k == 7,
                                   [wb, xnA[1]], [pb])
                            if oc % 2:
                                CP("dve", fm3[:, oc, :], pt[:, :], [pb], [fmb])
                            else:
                                ACT(fm3[:, oc, :], pt[:, :], AF.Copy, [pb], [fmb])
                        STO(dst[r0:r0 + 512, t0:t0 + 512].rearrange("(o p) s -> p o s", p=128), fm3, [fmb], [dbuf])
                    va4 = vaA[0].rearrange("p (j h e) -> p j h e", j=4, h=8)
                    mv4 = mvA[0].rearrange("p (j h e) -> p j h e", j=4, h=4)
                    mo3 = moA[0].rearrange("p (j c) -> p j c", j=4)
                    gt3 = gtA[0].rearrange("p (j c) -> p j c", j=4)
                    tm_groups = [(3072, 512, "av", 0), (5632, 512, "mv", 0), (6144, 512, "mv", 1),
                                 (6656, 512, "mo", 0), (7168, 512, "mo", 1), (7680, 16, "mg", 0)]
                    for (wc, ncol, kind, half) in tm_groups:
                        w3, wb = load_w(wview(wl, 0, 8, wc, ncol))
                        for j in range(4):
                            pt, pb = psn()
                            for k in range(8):
                                MM(pt[:, 0:ncol], xn3[:, k, j * 128:(j + 1) * 128], w3[:, k, :], k == 0, k == 7,
                                   [wb, xnA[1]], [pb])
                            if kind == "av":
                                CP("dve", va4[:, j, :, 0:64], pt[:, :].rearrange("p (h e) -> p h e", h=8), [pb], [vaA[1]])
                            elif kind == "mv":
                                ACT(mv4[:, j, 2 * half:2 * half + 2, 0:256], pt[:, :].rearrange("p (h e) -> p h e", h=2),
                                    AF.Copy, [pb], [mvA[1]])
                            elif kind == "mo":
                                CP("dve", mo3[:, j, half * 512:(half + 1) * 512], pt[:, :], [pb], [moA[1]])
                            else:
                                TT("dve", gt3[:, j, :], pt[:, 0:16], gbias[:, :], ALU.add, [pb, B_par], [gtA[1]])
                    rows = lambda d_, w_: d_[t0:t0 + 512, :].rearrange("(j p) c -> p j c", p=128)
                    STO(rows(vaug_d, 520), vaA[0].rearrange("p (j c) -> p j c", j=4), [vaA[1]], [B_vaug])
                    STO(rows(mvaug_d, 1028), mvA[0].rearrange("p (j c) -> p j c", j=4), [mvA[1]], [B_mvaug])
                    STO(rows(mo_d, D), mo3, [moA[1]], [B_mo])
                    STO(rows(gates_d, 16), gt3, [gtA[1]], [B_gates])

                chk('A')
                sto_eng[0] = 'pool'
                new_phase()
                dmax = max(dils)
                qsB = ba.carve(4 * 128 * dmax, "qs")
                ksB = ba.carve(4 * 256 * dmax, "ks")
                vtB = [[ba.carve(520, "vt") for _ in range(2)] for _ in range(2)]
                ptB = [[ba.carve(512, "pt") for _ in range(2)] for _ in range(2)]
                exB = [fa.carve(512, "ex") for _ in range(4)]
                osB = [fa.carve(520, "os") for _ in range(2)]
                bctr = 0
                import os as _os
                _skip = set(_os.environ.get("PB_SKIP", "").split(","))
                for g, d in enumerate(dils):
                    if ("g%d" % g) in _skip:
                        continue
                    sr = S // d
                    nb = sr // 128
                    for b in range(nb):
                        q3 = qsB[0][:, 0:4 * 128 * d].rearrange("p (h t) -> p h t", h=4)
                        k3 = ksB[0][:, 0:4 * 256 * d].rearrange("p (h t) -> p h t", h=4)
                        LD(q3, aqkT[g * 512:(g + 1) * 512, d * 128 * b:d * 128 * (b + 1)].rearrange("(h p) t -> p h t", p=128),
                           [B_aqk], [qsB[1]])
                        klo = d * (128 * b - 64)
                        khi = d * (128 * b + 192)
                        clo, chi = max(klo, 0), min(khi, S)
                        if clo > klo:
                            MS("pool", k3[:, :, 0:clo - klo], 0.0, [ksB[1]], partial=True)
                        if chi < khi:
                            MS("pool", k3[:, :, chi - klo:khi - klo], 0.0, [ksB[1]], partial=True)
                        wrk = {"pwrites": [ksB[1]]}
                        ksrc = aqkT[ng * 512 + g * 512:ng * 512 + (g + 1) * 512, clo:chi].rearrange("(h p) t -> p h t", p=128)
                        kdst = k3[:, :, clo - klo:chi - klo]
                        P.op("sp", lambda e, o=kdst, i=ksrc: e.dma_start(out=o, in_=i), reads=[B_aqk], dma=True, **wrk)
                        for c in range(d):
                            vts = []
                            for j in range(2):
                                m = b + j
                                vt, vtb = vtB[j][bctr % 2]
                                base = c + d * (128 * m - 64)
                                if m == 0:
                                    MS("pool", vt[0:64, :], 0.0, [vtb], partial=True)
                                    LDP(vt[64:128, :], vaug_d[srange(c, 64, d), :], [B_vaug], [vtb])
                                elif m == nb:
                                    MS("pool", vt[64:128, :], 0.0, [vtb], partial=True)
                                    LDP(vt[0:64, :], vaug_d[srange(base, 64, d), :], [B_vaug], [vtb])
                                else:
                                    LDP(vt[:, :], vaug_d[srange(base, 128, d), :], [B_vaug], [vtb])
                                    if bound and (128 * m * d) % bound == 0:
                                        rs_ = slice(0, 64) if j == 0 else slice(64, 128)
                                        TS("pool", vt[rs_, :], vt[rs_, :], keep[rs_, 0:1], ALU.mult, [vtb, B_c], [vtb])
                                vts.append((vt.rearrange("p (h e) -> p h e", h=8), vtb))
                            osb_t, osb = osB[bctr % 2]
                            ptsall = []
                            for hg in range(2):
                                pts = []
                                for j in range(2):
                                    pt_, pb = psn(0, 4)
                                    for hh in range(4):
                                        h = 2 * hh + hg
                                        hp, pp = h // 2, (h % 2) * 64
                                        kk = k3[pp:pp + 64, hp, srange(c + 128 * j * d, 128, d)]
                                        qq = q3[pp:pp + 64, hp, srange(c, 128, d)]
                                        MM(pt_[:, hh * 128:(hh + 1) * 128], kk, qq, True, True, [ksB[1], qsB[1]], [pb])
                                    ex, exb = exB[hg * 2 + j]
                                    ACT(ex, pt_[:, :], AF.Exp, [pb], [exb], scale=0.125)
                                    pT, pTb = ptB[j][hg]
                                    TT("dve", pT, ex, mtab(g, j, hg * 4, 4), ALU.mult, [exb, B_c], [pTb])
                                    pts.append((pT, pTb))
                                ptsall.append(pts)
                            for hg in range(2):
                                pts = ptsall[hg]
                                po, pob = psn(4, 8)
                                for hh in range(4):
                                    h = 2 * hh + hg
                                    for j in range(2):
                                        MM(po[:, hh * 65:(hh + 1) * 65], pts[j][0][:, hh * 128:(hh + 1) * 128],
                                           vts[j][0][:, h, :], j == 0, j == 1, [pts[j][1], vts[j][1]], [pob])
                                os3 = osb_t.rearrange("p (h e) -> p h e", h=8)
                                po3 = po[:, 0:260].rearrange("p (h e) -> p h e", h=4)
                                if hg == 0:
                                    ACT(os3[:, 0:8:2, :], po3, AF.Copy, [pob], [osb])
                                else:
                                    CP("dve", os3[:, 1:8:2, :], po3, [pob], [osb])
                            r0 = c + d * 128 * b
                            STO(og_d[g][srange(r0, 128, d), :], osb_t, [osb], [B_og[g]])
                            bctr += 1

                chk('B')
                new_phase()
                dgC = [ba.carve(640, "dg") for _ in range(2)]
                prC = [ba.carve(516, "pr") for _ in range(3)]
                scC = [ba.carve(512, "sc") for _ in range(3)]
                cctr = 0
                for fc in range(16):
                    dg, dgb = dgC[fc % 2]
                    dg3 = dg.rearrange("p (j c) -> p j c", j=5)
                    for j in range(5):
                        TS("dve", dg3[:, j, :], ident_b, wconv[:, fc * 5 + j:fc * 5 + j + 1], ALU.mult, [B_c, B_par], [dgb])
                    for it in range(NT):
                        t0 = it * 512
                        pr, prb = prC[cctr % 3]
                        sc, scb = scC[cctr % 3]
                        cctr += 1
                        lo, hi = t0 - 2, t0 + 514
                        clo, chi = max(lo, 0), min(hi, S)
                        if clo > lo:
                            MS("pool", pr[:, 0:2], 0.0, [prb], partial=True)
                        if chi < hi:
                            MS("pool", pr[:, 514:516], 0.0, [prb], partial=True)
                        LDP(pr[:, clo - lo:chi - lo], mpre[fc * 128:(fc + 1) * 128, clo:chi], [B_mpre], [prb])
                        if bound and t0 > 0 and t0 % bound == 0:
                            TS("pool", pr[:, 0:2], pr[:, 0:2], keep[:, 0:1], ALU.mult, [prb, B_c], [prb])
                        if bound and t0 + 512 < S and (t0 + 512) % bound == 0:
                            TS("pool", pr[:, 514:516], pr[:, 514:516], keep[:, 0:1], ALU.mult, [prb, B_c], [prb])
                        pt, pb = psn()
                        for j in range(5):
                            MM(pt[:, :], dg3[:, j, :], pr[:, j:j + 512], j == 0, j == 4, [dgb, prb], [pb])
                        ACT(sc, pt[:, :], AF.Silu, [pb, B_par], [scb], bias=bconv[:, fc:fc + 1])
                        STO(mqkT[fc * 128:(fc + 1) * 128, t0:t0 + 512], sc, [scb], [B_mqk])

                chk('C')
                for dr in range(2):
                    new_phase()
                    maskf = maskU_f if dr == 0 else maskL_f
                    qTD2 = [ba.carve(4096, "qTD") for _ in range(2)]
                    kTD2 = [ba.carve(4096, "kTD") for _ in range(2)]
                    ktm = ba.carve(4096, "ktm")
                    mvD2 = [ba.carve(4 * 1028, "mvD") for _ in range(2)]
                    CbD = ba.carve(8 * 257, "Cb")
                    scD = [ba.carve(128, "scT") for _ in range(4)]
                    vtD = [ba.carve(257, "vtl") for _ in range(4)]
                    vpD = [ba.carve(257, "vpr") for _ in range(4)]
                    hst = fa.carve(4096, "hst")
                    CfD = fa.carve(8 * 257, "Cf")
                    gtD2 = [fa.carve(64, "gtD") for _ in range(2)]
                    gsm = {n: fa.carve(16, n) for n in ["e1", "sp", "tmp", "tmp2", "eb", "ebt", "ecum", "etot"]}
                    dnD = [fa.carve(2, "dn") for _ in range(4)]
                    CfH = [CfD[1]] + [Buf("CfH%d" % h_) for h_ in range(1, 4)]
                    CbH = [CbD[1]] + [Buf("CbH%d" % h_) for h_ in range(1, 4)]
                    for h_ in range(1, 4):
                        CfH[h_].r = list(CfD[1].r)
                        CbH[h_].r = list(CbD[1].r)
                        fa.bufs.append(CfH[h_])
                        ba.bufs.append(CbH[h_])
                    MS("pool", CfD[0], 0.0, CfH)
                    MS("pool", CbD[0], 0.0, CbH)
                    Cf3 = CfD[0].rearrange("p (c e) -> p c e", c=8)
                    Cb3 = CbD[0].rearrange("p (c e) -> p c e", c=8)
                    order = list(range(NT)) if dr == 0 else list(range(NT - 1, -1, -1))
                    for it in order:
                        t0 = it * 512
                        if bound and ((dr == 0 and t0 > 0 and t0 % bound == 0) or
                                      (dr == 1 and t0 + 512 < S and (t0 + 512) % bound == 0)):
                            TS("dve", CfD[0], CfD[0], keep[:, 0:1], ALU.mult, CfH + [B_c], CfH)
                            TS("pool", CbD[0], CbD[0], keep[:, 0:1], ALU.mult, CbH + [B_c], CbH)
                        qTD, kTD, mvD, gtD = qTD2[it % 2], kTD2[it % 2], mvD2[it % 2], gtD2[it % 2]
                        q3 = qTD[0].rearrange("p (c t) -> p c t", c=8)
                        k3 = kTD[0].rearrange("p (c t) -> p c t", c=8)
                        LD(q3, mqkT[0:1024, t0:t0 + 512].rearrange("(c p) t -> p c t", p=128), [B_mqk], [qTD[1]])
                        LD(k3, mqkT[1024:2048, t0:t0 + 512].rearrange("(c p) t -> p c t", p=128), [B_mqk], [kTD[1]])
                        mv3 = mvD[0].rearrange("p (j c) -> p j c", j=4)
                        LD(mv3, mvaug_d[t0:t0 + 512, :].rearrange("(j p) c -> p j c", p=128), [B_mvaug], [mvD[1]])
                        gt3 = gtD[0].rearrange("p (j c) -> p j c", j=4)
                        LD(gt3, gates_d[t0:t0 + 512, :].rearrange("(j p) c -> p j c", p=128), [B_gates], [gtD[1]])
                        g3 = {n: v[0].rearrange("p (j c) -> p j c", j=4) for n, v in gsm.items()}
                        gb_ = {n: v[1] for n, v in gsm.items()}
                        li = gt3[:, :, dr * 8:dr * 8 + 4]
                        gf = gt3[:, :, dr * 8 + 4:dr * 8 + 8]
                        ACT(g3["e1"], gf, AF.Exp, [gtD[1]], [gb_["e1"]], scale=-1.0)
                        ACT(g3["sp"], g3["e1"], AF.Ln, [gb_["e1"]], [gb_["sp"]], bias=1.0)
                        pg, pgb = PS[7]
                        for j in range(4):
                            MM(pg[:, j * 4:j * 4 + 4], maskf, g3["sp"][:, j, :], True, True, [B_c, gb_["sp"]], [pgb])
                        for j in range(4):
                            MM(pg[:, 16 + j * 4:16 + j * 4 + 4], ones_f, g3["sp"][:, j, :], True, True, [B_c, gb_["sp"]], [pgb])
                        ncum = pg[:, 0:16].rearrange("p (j c) -> p j c", j=4)
                        ntot = pg[:, 16:32].rearrange("p (j c) -> p j c", j=4)
                        TT("dve", g3["tmp"], li, ncum, ALU.add, [gtD[1], pgb], [gb_["tmp"]])
                        TT("dve", g3["tmp2"], g3["tmp"], ntot, ALU.subtract, [gb_["tmp"], pgb], [gb_["tmp2"]])
                        ACT(g3["eb"], g3["tmp"], AF.Exp, [gb_["tmp"]], [gb_["eb"]], bias=-LN16)
                        ACT(g3["ebt"], g3["tmp2"], AF.Exp, [gb_["tmp2"]], [gb_["ebt"]], bias=-LN16)
                        ACT(g3["ecum"], ncum, AF.Exp, [pgb], [gb_["ecum"]])
                        ACT(g3["etot"], ntot, AF.Exp, [pgb], [gb_["etot"]], scale=-1.0)
                        kt3 = ktm[0].rearrange("p (j c) -> p j c", j=4)
                        for j in range(4):
                            pk, pkb = PS[6]
                            pkb16 = pk[:, :].bitcast(BF16)
                            for c in range(8):
                                TR(pkb16[:, c * 128:(c + 1) * 128], k3[:, c, j * 128:(j + 1) * 128], ident_b, [kTD[1]], [pkb])
                            CP("dve", kt3[:, j, :], pkb16, [pkb], [ktm[1]])
                        h3 = hst[0].rearrange("p (j c) -> p j c", j=4)
                        jorder = list(range(4)) if dr == 0 else [3, 2, 1, 0]
                        for j in jorder:
                            tsl = slice(j * 128, (j + 1) * 128)
                            for h in range(4):
                                mvh = mv3[:, j, h * 257:(h + 1) * 257]
                                ACT(vtD[h][0], mvh, AF.Copy, [mvD[1], gb_["eb"]], [vtD[h][1]], scale=g3["eb"][:, j, h:h + 1])
                                TS("pool", vpD[h][0], mvh, g3["ebt"][:, j, h:h + 1], ALU.mult, [mvD[1], gb_["ebt"]],
                                   [vpD[h][1]])
                            pS, pSb = PS[0]
                            for h in range(4):
                                for c in range(2):
                                    MM(pS[:, h * 128:(h + 1) * 128], k3[:, 2 * h + c, tsl], q3[:, 2 * h + c, tsl],
                                       c == 0, c == 1, [kTD[1], qTD[1]], [pSb])
                            for h in range(4):
                                TT("dve", scD[h][0], pS[:, h * 128:(h + 1) * 128], maskf, ALU.mult, [pSb, B_c], [scD[h][1]])
                            pcs = []
                            for h in range(4):
                                for c in range(2):
                                    pc, pcb = PS[5 + (2 * h + c) % 3]
                                    MM(pc[:, 0:257], kt3[:, j, (2 * h + c) * 128:(2 * h + c + 1) * 128], vpD[h][0], True, True,
                                       [ktm[1], vpD[h][1]], [pcb])
                                    STT(Cf3[:, 2 * h + c, :], Cf3[:, 2 * h + c, :], g3["etot"][:, j, h:h + 1], pc[:, 0:257],
                                        ALU.mult, ALU.add, [CfH[h], gb_["etot"], pcb], [CfH[h]])
                            for h in range(4):
                                ph, phb = PS[1 + h]
                                MM(ph[:, 0:257], scD[h][0], vtD[h][0], True, False, [scD[h][1], vtD[h][1]], [phb])
                                for c in range(2):
                                    MM(ph[:, 0:257], q3[:, 2 * h + c, tsl], Cb3[:, 2 * h + c, :], False, c == 1,
                                       [qTD[1], CbH[h]], [phb])
                            for h in range(4):
                                ACT(Cb3[:, 2 * h:2 * h + 2, :], Cf3[:, 2 * h:2 * h + 2, :], AF.Copy, [CfH[h]], [CbH[h]])
                            for h in range(4):
                                ph, phb = PS[1 + h]
                                dn, dnb = dnD[h]
                                ACT(dn[:, 0:1], ph[:, 256:257], AF.Abs, [phb], [dnb])
                                TS("dve", dn[:, 0:1], dn[:, 0:1], g3["ecum"][:, j, h:h + 1], ALU.max, [dnb, gb_["ecum"]], [dnb])
                                RCP(dn[:, 1:2], dn[:, 0:1], [dnb], [dnb])
                                ACT(h3[:, j, h * 256:(h + 1) * 256], ph[:, 0:256], AF.Copy, [phb, dnb], [hst[1]],
                                    scale=dn[:, 1:2])
                        STO(h_d[dr][t0:t0 + 512, :].rearrange("(j p) c -> p j c", p=128), h3, [hst[1]], [B_h[dr]])

                chk('D')
                sto_eng[0] = 'act'
                new_phase()
                xE = fa.carve(4096, "xE")
                yE = fa.carve(4096, "yE")
                hfE = fa.carve(1024, "hf")
                hbE = fa.carve(1024, "hb")
                sgE = fa.carve(1024, "sg")
                ogE = [fa.carve(520, "og%d" % g) for g in range(ng)]
                smE = fa.carve(32, "smE")
                gaE = fa.carve(512, "ga")
                gmE = fa.carve(512, "gm")
                t1E = fa.carve(512, "t1")
                t2E = fa.carve(512, "t2")
                rsE = fa.carve(512, "rsE")
                tmE = fa.carve(512, "tmE")
                moE = ba.carve(1024, "moE")
                hnE = ba.carve(1024, "hnE")
                aoE = ba.carve(512, "aoE")
                mlT = ba.carve(4096, "mlT")
                atT = ba.carve(2048, "atT")
                bgE = [ba.carve(512, "bgE") for _ in range(2)]
                mgE = ba.carve(4096, "mgE")
                sqE = ba.carve(4096, "sqE")
                xnE = ba.carve(4096, "xnE")
                acE = ba.carve(22 * 512, "acE")
                last = (l == L - 1)
                for it in range(NT):
                    t0 = it * 512
                    x3 = xE[0].rearrange("p (k t) -> p k t", k=8)
                    y3 = yE[0].rearrange("p (k t) -> p k t", k=8)
                    LD(x3, xT.rearrange("(k p) s -> p k s", p=128)[:, :, t0:t0 + 512], [B_xT], [xE[1]])
                    ml3 = mlT[0].rearrange("p (k t) -> p k t", k=8)
                    at3 = atT[0].rearrange("p (k t) -> p k t", k=4)
                    for j in range(4):
                        r0 = t0 + j * 128
                        LD(hfE[0], h_d[0][r0:r0 + 128, :], [B_h[0]], [hfE[1]])
                        LD(hbE[0], h_d[1][r0:r0 + 128, :], [B_h[1]], [hbE[1]])
                        LD(moE[0], mo_d[r0:r0 + 128, :], [B_mo], [moE[1]])
                        TT("dve", hfE[0], hfE[0], hbE[0], ALU.add, [hfE[1], hbE[1]], [hfE[1]])
                        sm = smE[0]
                        for h in range(4):
                            ACT(sgE[0][:, h * 256:(h + 1) * 256], hfE[0][:, h * 256:(h + 1) * 256], AF.Square,
                                [hfE[1]], [sgE[1], smE[1]], accum=sm[:, h:h + 1])
                        rstd_from_ssq(sm[:, 0:4], smE[1], 256.0, sm[:, 4:8], smE[1], sm[:, 8:12], smE[1], 4)
                        ACT(sgE[0], moE[0], AF.Sigmoid, [moE[1]], [sgE[1]])
                        TT("dve", sgE[0], sgE[0], gnorm[:, :], ALU.mult, [sgE[1], B_par], [sgE[1]])
                        for h in range(4):
                            hs = slice(h * 256, (h + 1) * 256)
                            STT(hnE[0][:, hs], hfE[0][:, hs], sm[:, 8 + h:9 + h], sgE[0][:, hs], ALU.mult, ALU.mult,
                                [hfE[1], smE[1], sgE[1]], [hnE[1]])
                        pk, pkb = psn()
                        pk16 = pk[:, :].bitcast(BF16)
                        for k in range(8):
                            TR(pk16[:, k * 128:(k + 1) * 128], hnE[0][:, k * 128:(k + 1) * 128], ident_b, [hnE[1]], [pkb])
                        CP("dve", ml3[:, :, j * 128:(j + 1) * 128], pk16.rearrange("p (k t) -> p k t", k=8), [pkb], [mlT[1]])
                        for g in range(ng):
                            LD(ogE[g][0], og_d[g][r0:r0 + 128, :], [B_og[g]], [ogE[g][1]])
                        for g in range(1, ng):
                            TT("dve", ogE[0][0], ogE[0][0], ogE[g][0], ALU.add, [ogE[0][1], ogE[g][1]], [ogE[0][1]])
                        o3 = ogE[0][0].rearrange("p (h e) -> p h e", h=8)
                        RCP(sm[:, 16:24], o3[:, :, 64], [ogE[0][1]], [smE[1]])
                        ao3 = aoE[0].rearrange("p (h e) -> p h e", h=8)
                        for h in range(8):
                            TS("dve", ao3[:, h, :], o3[:, h, 0:64], sm[:, 16 + h:17 + h], ALU.mult,
                               [ogE[0][1], smE[1]], [aoE[1]])
                        pk, pkb = psn()
                        pk16 = pk[:, :].bitcast(BF16)
                        for k in range(4):
                            TR(pk16[:, k * 128:(k + 1) * 128], aoE[0][:, k * 128:(k + 1) * 128], ident_b, [aoE[1]], [pkb])
                        CP("dve", at3[:, :, j * 128:(j + 1) * 128], pk16[:, 0:512].rearrange("p (k t) -> p k t", k=4),
                           [pkb], [atT[1]])
                    mg3 = mgE[0].rearrange("p (k t) -> p k t", k=8)
                    for cg in range(2):
                        wa3, wab = load_w(wview(w_ap[l], 0, 4, cg * 512, 512))
                        wm3, wmb = load_w(wview(w_mp[l], 0, 8, cg * 512, 512))
                        for oc in range(4):
                            fcx = cg * 4 + oc
                            LD(bgE[0][0], bgT[fcx * 128:(fcx + 1) * 128, t0:t0 + 512], [B_bg], [bgE[0][1]])
                            LD(bgE[1][0], bgT[1024 + fcx * 128:1024 + (fcx + 1) * 128, t0:t0 + 512], [B_bg], [bgE[1][1]])
                            pa, pab = psn()
                            for k in range(4):
                                MM(pa[:, :], wa3[:, k, oc * 128:(oc + 1) * 128], at3[:, k, :], k == 0, k == 3,
                                   [wab, atT[1]], [pab])
                            pm, pmb = psn()
                            for k in range(8):
                                MM(pm[:, :], wm3[:, k, oc * 128:(oc + 1) * 128], ml3[:, k, :], k == 0, k == 7,
                                   [wmb, mlT[1]], [pmb])
                            ACT(gaE[0], bgE[0][0], AF.Sigmoid, [bgE[0][1]], [gaE[1]])
                            ACT(gmE[0], bgE[1][0], AF.Sigmoid, [bgE[1][1]], [gmE[1]])
                            TT("dve", t1E[0], pa[:, :], gaE[0], ALU.mult, [pab, gaE[1]], [t1E[1]])
                            TT("dve", t2E[0], pm[:, :], gmE[0], ALU.mult, [pmb, gmE[1]], [t2E[1]])
                            TT("dve", mg3[:, fcx, :], t1E[0], t2E[0], ALU.add, [t1E[1], t2E[1]], [mgE[1]])

                    def proj_norm_res(w2, nkt, src3, srcb, gcol):
                        kgs = [(k0, min(8, nkt - k0)) for k0 in range(0, nkt, 8)]
                        for cg in range(2):
                            pss = [psn() for _ in range(4)]
                            for gi, (k0, nk) in enumerate(kgs):
                                w3, wb = load_w(wview(w2, k0 * 128, nk, cg * 512, 512))
                                for oc in range(4):
                                    for k in range(nk):
                                        MM(pss[oc][0][:, :], w3[:, k, oc * 128:(oc + 1) * 128], src3[:, k0 + k, :],
                                           gi == 0 and k == 0, gi == len(kgs) - 1 and k == nk - 1, [wb, srcb], [pss[oc][1]])
                            for oc in range(4):
                                ACT(y3[:, cg * 4 + oc, :], pss[oc][0][:, :], AF.Copy, [pss[oc][1]], [yE[1]])
                        sq3 = sqE[0].rearrange("p (k t) -> p k t", k=8)
                        fm_norm(y3, yE[1], gcol, sq3, sqE[1], None, None, rsE[0], rsE[1], tmE[0], tmE[1])
                        for k in range(8):
                            STT(y3[:, k, :], y3[:, k, :], gains[:, gcol + k:gcol + k + 1], rsE[0], ALU.mult, ALU.mult,
                                [yE[1], rsE[1], B_par], [yE[1]])
                            TT("dve", x3[:, k, :], x3[:, k, :], y3[:, k, :], ALU.add, [xE[1], yE[1]], [xE[1]])

                    proj_norm_res(w_o[l], 8, mg3, mgE[1], 8)
                    sq3 = sqE[0].rearrange("p (k t) -> p k t", k=8)
                    xn3 = xnE[0].rearrange("p (k t) -> p k t", k=8)
                    fm_norm(x3, xE[1], 16, sq3, sqE[1], xn3, xnE[1], rsE[0], rsE[1], tmE[0], tmE[1])
                    ac3 = acE[0].rearrange("p (k t) -> p k t", k=22)
                    for i0 in range(0, 22, 4):
                        nci = min(4, 22 - i0)
                        wg3, wgb = load_w(wview(w_f1[l], 0, 8, i0 * 128, nci * 128))
                        wu3, wub = load_w(wview(w_f1[l], 0, 8, FH + i0 * 128, nci * 128))
                        for ii in range(nci):
                            pgt, pgtb = psn()
                            for k in range(8):
                                MM(pgt[:, :], wg3[:, k, ii * 128:(ii + 1) * 128], xn3[:, k, :], k == 0, k == 7,
                                   [wgb, xnE[1]], [pgtb])
                            put, putb = psn()
                            for k in range(8):
                                MM(put[:, :], wu3[:, k, ii * 128:(ii + 1) * 128], xn3[:, k, :], k == 0, k == 7,
                                   [wub, xnE[1]], [putb])
                            ACT(t1E[0], pgt[:, :], AF.Silu, [pgtb], [t1E[1]])
                            TT("dve", ac3[:, i0 + ii, :], put[:, :], t1E[0], ALU.mult, [putb, t1E[1]], [acE[1]])
                    proj_norm_res(w_f2[l], 22, ac3, acE[1], 24)
                    if not last:
                        STO(xT.rearrange("(k p) s -> p k s", p=128)[:, :, t0:t0 + 512], x3, [xE[1]], [B_xT])
                    else:
                        o3_ = yE[0].rearrange("p (j c) -> p j c", j=4)
                        for j in range(4):
                            for kk in range(0, 8, 4):
                                pt, pb = psn()
                                for k in range(kk, kk + 4):
                                    TR(pt[:, (k - kk) * 128:(k - kk + 1) * 128], x3[:, k, j * 128:(j + 1) * 128], ident_f,
                                       [xE[1]], [pb])
                                if kk:
                                    CP("dve", o3_[:, j, kk * 128:(kk + 4) * 128], pt[:, :], [pb], [yE[1]])
                                else:
                                    ACT(o3_[:, j, kk * 128:(kk + 4) * 128], pt[:, :], AF.Copy, [pb], [yE[1]])
                        STO(yout[si][t0:t0 + 512, :].rearrange("(j p) c -> p j c", p=128), o3_, [yE[1]], [B_out])
    try:
        _main_body()
    except _Stop:
        pass
    P.emit()
    st.close()
    return nc


def host_params(norm_mix_pre, norm_mix_post, norm_ffn_pre, norm_ffn_post, b_conv, w_conv):
    L = norm_mix_pre.shape[0]
    fmv = lambda v: np.ascontiguousarray(np.asarray(v, np.float32).reshape(L, -1, 128).transpose(0, 2, 1))
    gains = np.concatenate([fmv(norm_mix_pre), fmv(norm_mix_post), fmv(norm_ffn_pre), fmv(norm_ffn_post)], axis=2)
    bconv = fmv(b_conv)
    wc = np.asarray(w_conv, np.float32)
    wconv = np.ascontiguousarray(wc.reshape(L, 5, 16, 128).transpose(0, 3, 2, 1)).reshape(L, 128, 80)
    return np.ascontiguousarray(gains), bconv, wconv


_CACHE = {}
_MARKS = []


def run(seq_lists, xs_per_core, depth, dils, weights, dbg=None, bound=None, keeps=None):
    key = (tuple(seq_lists), depth, tuple(dils), dbg, bound)
    if key not in _CACHE:
        _CACHE[key] = build(list(seq_lists), depth, tuple(dils), dbg, bound)
    nc = _CACHE[key]
    gains, bconv, wconv = host_params(weights["norm_mix_pre"], weights["norm_mix_post"], weights["norm_ffn_pre"],
                                      weights["norm_ffn_post"], weights["b_conv"], weights["w_conv"])
    common = {
        "w_in": np.ascontiguousarray(weights["w_in"], np.float32),
        "w_attn_proj": np.ascontiguousarray(weights["w_attn_proj"], np.float32),
        "w_mlstm_proj": np.ascontiguousarray(weights["w_mlstm_proj"], np.float32),
        "w_out": np.ascontiguousarray(weights["w_out"], np.float32),
        "w_ffn_in": np.ascontiguousarray(weights["w_ffn_in"], np.float32),
        "w_ffn_out": np.ascontiguousarray(weights["w_ffn_out"], np.float32),
        "gains": gains, "bconv": bconv, "wconv": wconv,
        "gnorm": np.ascontiguousarray(weights["g_mlstm_norm"], np.float32),
        "gbias": np.ascontiguousarray(weights["b_mlstm_gates"], np.float32),
        "consts": make_consts(dils),
    }
    in_maps = []
    for ci, xs in enumerate(xs_per_core):
        m = dict(common)
        m["keep"] = np.full((128, 1), 1.0 if keeps is None else float(keeps[ci]), np.float32)
        for i, x in enumerate(xs):
            m["x%d" % i] = np.ascontiguousarray(x, np.float32)
        in_maps.append(m)
    res = run_bass_kernel_spmd(nc, in_maps, core_ids=list(range(len(in_maps))))
    return res.results


def kernel(x_prompt, x_sample, norm_mix_pre, norm_mix_post, norm_ffn_pre, norm_ffn_post, w_in, b_mlstm_gates,
           w_conv, b_conv, g_mlstm_norm, w_attn_proj, w_mlstm_proj, w_out, w_ffn_in, w_ffn_out):
    weights = dict(norm_mix_pre=norm_mix_pre, norm_mix_post=norm_mix_post, norm_ffn_pre=norm_ffn_pre,
                   norm_ffn_post=norm_ffn_post, w_in=w_in, b_mlstm_gates=b_mlstm_gates, w_conv=w_conv, b_conv=b_conv,
                   g_mlstm_norm=g_mlstm_norm, w_attn_proj=w_attn_proj, w_mlstm_proj=w_mlstm_proj, w_out=w_out,
                   w_ffn_in=w_ffn_in, w_ffn_out=w_ffn_out)
    weights = {k: np.asarray(v) for k, v in weights.items()}
    x_prompt = np.asarray(x_prompt, np.float32)
    x_sample = np.asarray(x_sample, np.float32)
    depth = w_in.shape[0]
    n = 8
    nsamp, sp = x_sample.shape[0], x_sample.shape[1]
    pl = x_prompt.shape[1]
    per = pl // sp
    ngrp = nsamp // per
    xs, keeps = [[x_prompt[0]]], [1.0]
    for gi in range(ngrp):
        xs.append([x_sample[gi * per:(gi + 1) * per].reshape(pl, -1)])
        keeps.append(0.0)
    while len(xs) < n:
        xs.append([np.zeros((pl, x_prompt.shape[2]), np.float32)])
        keeps.append(0.0)
    res = run([pl], xs, depth, (1, 4, 16), weights, bound=sp, keeps=keeps)
    y_prompt = res[0]["y0"][None].astype(np.float32)
    y_sample = np.concatenate([res[1 + gi]["y0"].reshape(per, sp, -1) for gi in range(ngrp)], axis=0).astype(np.float32)
    return (y_prompt, y_sample)
```

```python
import math
from contextlib import ExitStack

import numpy as np
import concourse.bass as bass
import concourse.mybir as mybir
from concourse.bass_utils import run_bass_kernel_spmd

F32 = mybir.dt.float32
BF16 = mybir.dt.bfloat16
AF = mybir.ActivationFunctionType
ALU = mybir.AluOpType

ENGS = ("pe", "act", "dve", "pool", "sp")
D = 1024
NIN = 9744
FH = 2816
EPS = 1e-6
LN16 = math.log(16.0)


class Buf:
    __slots__ = ("name", "w", "r")

    def __init__(self, name=""):
        self.name = name
        self.w = []
        self.r = []


class Ins:
    __slots__ = ("eng", "fn", "deps", "inc", "val", "dma", "dsem", "dval", "idx")

    def __init__(self, eng, fn, dma):
        self.eng = eng
        self.fn = fn
        self.deps = []
        self.inc = False
        self.val = 0
        self.dma = dma
        self.dsem = -1
        self.dval = 0
        self.idx = -1


def _reduce(lst):
    best = {}
    for d in lst:
        key = (d.eng, d.dsem) if d.dma else (d.eng, -1)
        cur = best.get(key)
        if cur is None or (d.dval > cur.dval if d.dma else d.idx > cur.idx):
            best[key] = d
    return list(best.values())


class Prog:
    NDSEM = 8

    def __init__(self, nc):
        self.nc = nc
        self.lists = {e: [] for e in ENGS}
        self.dma_count = {e: 0 for e in ENGS}

    def op(self, eng, fn, reads=(), writes=(), pwrites=(), dma=False):
        ins = Ins(eng, fn, dma)
        deps = []
        for b in reads:
            deps.extend(b.w)
        for b in writes:
            deps.extend(b.w)
            deps.extend(b.r)
        for b in pwrites:
            deps.extend(b.r)
        lst = self.lists[eng]
        ins.idx = len(lst)
        if dma:
            n = self.dma_count[eng]
            self.dma_count[eng] = n + 1
            ins.dsem = n % self.NDSEM
            ins.dval = 16 * (n // self.NDSEM + 1)
        final = []
        for d in _reduce(deps):
            if (not d.dma) and d.eng == "pe" and eng == "pe" and not dma:
                continue
            if not d.dma:
                d.inc = True
            final.append(d)
        ins.deps = final
        lst.append(ins)
        for b in writes:
            b.w = [ins]
            b.r = []
        for b in pwrites:
            b.w = _reduce(b.w + [ins])
        for b in reads:
            b.r = _reduce(b.r + [ins])
        return ins

    def emit(self):
        nc = self.nc
        with ExitStack() as st:
            esem = {e: st.enter_context(nc.semaphore("s_" + e)) for e in ENGS}
            dsem = {e: [st.enter_context(nc.semaphore("d_%s%d" % (e, i))) for i in range(self.NDSEM)]
                    for e in ("sp", "act", "pool")}
            for e in ENGS:
                c = 0
                for ins in self.lists[e]:
                    if ins.inc and not ins.dma:
                        c += 1
                        ins.val = c
            block = st.enter_context(nc.Block())
            last = {}
            for e in ("sp", "act", "pool"):
                for ins in self.lists[e]:
                    if ins.dma:
                        last[(e, ins.dsem)] = ins.dval

            def run(ename, eng):
                waited = {}

                def wait(key, sem, val):
                    if waited.get(key, 0) < val:
                        eng.wait_ge(sem, val)
                        waited[key] = val

                for ins in self.lists[ename]:
                    for d in ins.deps:
                        if d.dma:
                            wait(("d", d.eng, d.dsem), dsem[d.eng][d.dsem], d.dval)
                        else:
                            wait(("e", d.eng), esem[d.eng], d.val)
                    if ins.dma:
                        if ins.dval > 16:
                            wait(("d", ename, ins.dsem), dsem[ename][ins.dsem], ins.dval - 16)
                        ins.fn(eng).then_inc(dsem[ename][ins.dsem], 16)
                    else:
                        h = ins.fn(eng)
                        if ins.inc:
                            h.then_inc(esem[ename], 1)
                if ename == "sp":
                    for (e, k), v in last.items():
                        wait(("d", e, k), dsem[e][k], v)

            block.tensor(lambda eng: run("pe", eng))
            block.scalar(lambda eng: run("act", eng))
            block.vector(lambda eng: run("dve", eng))
            block.gpsimd(lambda eng: run("pool", eng))
            block.sync(lambda eng: run("sp", eng))


class Arena:
    def __init__(self, t, n):
        self.t = t
        self.n = n
        self.off = 0
        self.bufs = []
        self.pending = []

    def reset(self):
        pend = list(self.pending)
        for b in self.bufs:
            pend.extend(b.w)
            pend.extend(b.r)
        self.pending = _reduce(pend)
        self.bufs = []
        self.off = 0

    def carve(self, n, name=""):
        assert self.off + n <= self.n, (name, self.off, n, self.n)
        ap = self.t[:, self.off:self.off + n]
        self.off += n
        b = Buf(name)
        b.r = list(self.pending)
        self.bufs.append(b)
        return ap, b


def alibi_slopes(ng):
    h = np.arange(1, ng * 8 + 1, dtype=np.float32)
    return (2.0 ** (-8.0 * h / (ng * 8))).astype(np.float32).reshape(ng, 8)


def make_consts(dils):
    ng = len(dils)
    k = np.arange(128)[:, None]
    q = np.arange(128)[None, :]
    ident = (k == q).astype(np.float32)
    mu = (k <= q).astype(np.float32)
    ml = (k >= q).astype(np.float32)
    ones = np.ones((128, 128), np.float32)
    sl = alibi_slopes(ng)
    mt = np.zeros((128, ng, 2, 8, 128), np.float32)
    for g, d in enumerate(dils):
        for j in range(2):
            step = 128 * j - 64 + k - q
            valid = np.abs(step) <= 64
            for hi, h in enumerate([0, 2, 4, 6, 1, 3, 5, 7]):
                mt[:, g, j, hi, :] = np.where(valid, np.exp(-sl[g, h] * d * np.abs(step).astype(np.float32)), 0.0)
    return np.concatenate([ident, mu, ml, ones, mt.reshape(128, -1)], axis=1).astype(np.float32)


def build(seqs, depth, dils, dbg=None, bound=None):
    nc = bass.Bass("TRN2", target_bir_lowering=False)
    P = Prog(nc)
    st = ExitStack()
    ng = len(dils)
    SMAX = max(seqs)
    L = depth

    def dram(name, shape, dt, kind="Internal"):
        return nc.dram_tensor(name, list(shape), dt, kind=kind).ap()

    xin = [dram("x%d" % i, [s, D], F32, "ExternalInput") for i, s in enumerate(seqs)]
    yout = [dram("y%d" % i, [s, D], F32, "ExternalOutput") for i, s in enumerate(seqs)]
    w_in = dram("w_in", [L, D, NIN], F32, "ExternalInput")
    w_ap = dram("w_attn_proj", [L, 512, D], F32, "ExternalInput")
    w_mp = dram("w_mlstm_proj", [L, D, D], F32, "ExternalInput")
    w_o = dram("w_out", [L, D, D], F32, "ExternalInput")
    w_f1 = dram("w_ffn_in", [L, D, 2 * FH], F32, "ExternalInput")
    w_f2 = dram("w_ffn_out", [L, FH, D], F32, "ExternalInput")
    gains_d = dram("gains", [L, 128, 32], F32, "ExternalInput")
    bconv_d = dram("bconv", [L, 128, 16], F32, "ExternalInput")
    wconv_d = dram("wconv", [L, 128, 80], F32, "ExternalInput")
    gnorm_d = dram("gnorm", [L, D], F32, "ExternalInput")
    gbias_d = dram("gbias", [L, 16], F32, "ExternalInput")
    NCONST = 4 * 128 + ng * 2 * 8 * 128
    const_d = dram("consts", [128, NCONST], F32, "ExternalInput")
    keep_d = dram("keep", [128, 1], F32, "ExternalInput")

    xT = dram("xT", [D, SMAX], F32)
    aqkT = dram("aqkT", [2 * ng * 512, SMAX], BF16)
    vaug_d = dram("vaug", [SMAX, 520], BF16)
    mpre = dram("mpre", [2048, SMAX], BF16)
    mqkT = dram("mqkT", [2048, SMAX], BF16)
    mvaug_d = dram("mvaug", [SMAX, 1028], BF16)
    mo_d = dram("mo", [SMAX, D], BF16)
    gates_d = dram("gates", [SMAX, 16], F32)
    bgT = dram("bgT", [2048, SMAX], BF16)
    og_d = [dram("og%d" % g, [SMAX, 520], F32) for g in range(ng)]
    h_d = [dram("hdir%d" % i, [SMAX, D], F32) for i in range(2)]
    B_xT, B_aqk, B_vaug, B_mpre, B_mqk, B_mvaug, B_mo, B_gates, B_bg = [Buf(n) for n in
        "xT aqk vaug mpre mqk mvaug mo gates bg".split()]
    B_og = [Buf("og%d" % g) for g in range(ng)]
    B_h = [Buf("h0"), Buf("h1")]
    B_in = Buf("in")
    B_out = Buf("out")

    def sb(name, shape, dt):
        return st.enter_context(nc.sbuf_tensor(name, list(shape), dt))

    cf = sb("cf", [128, 512], F32)
    cb = sb("cb", [128, NCONST], BF16)
    B_c = Buf("c")
    P.op("sp", lambda e: e.dma_start(out=cf[:], in_=const_d[:, 0:512]), writes=[B_c], dma=True)
    P.op("pool", lambda e: e.dma_start(out=cb[:], in_=const_d[:, :]), pwrites=[B_c], dma=True)
    ident_f, maskU_f, maskL_f, ones_f = (cf[:, i * 128:(i + 1) * 128] for i in range(4))
    keep = sb("keeps", [128, 1], F32)
    P.op("sp", lambda e: e.dma_start(out=keep[:], in_=keep_d[:, :]), pwrites=[B_c], dma=True)
    ident_b = cb[:, 0:128]
    ones_b = cb[:, 384:512]

    def mtab(g, j, h0, nh):
        o = 512 + ((g * 2 + j) * 8 + h0) * 128
        return cb[:, o:o + nh * 128]

    gains = sb("gainss", [128, 32], F32)
    bconv = sb("bconvs", [128, 16], F32)
    wconv = sb("wconvs", [128, 80], F32)
    gnorm = sb("gnorms", [128, D], F32)
    gbias = sb("gbiass", [128, 16], F32)
    B_par = Buf("par")

    NW = 4
    wt = [sb("wt%d" % i, [128, 4096], BF16) for i in range(NW)]
    B_wt = [Buf("wt%d" % i) for i in range(NW)]
    wctr = [0]
    _wsrc_buf = []

    def load_w(src3):
        i = wctr[0] % NW
        wctr[0] += 1
        nk, ncol = src3.shape[1], src3.shape[2]
        dst = wt[i][:, 0:nk * ncol].rearrange("p (k c) -> p k c", k=nk)
        P.op("pool", lambda e: e.dma_start(out=dst, in_=src3), reads=list(_wsrc_buf), writes=[B_wt[i]], dma=True)
        return dst, B_wt[i]

    def wview(w2, r0, nk, c0, ncol):
        return w2[r0:r0 + nk * 128, c0:c0 + ncol].rearrange("(k p) c -> p k c", p=128)


    NF = 16000
    NB = 33536
    fa = Arena(sb("fa", [128, NF], F32), NF)
    ba = Arena(sb("ba", [128, NB], BF16), NB)
    PS = []
    for i in range(8):
        t = st.enter_context(nc.psum_tensor("ps%d" % i, [128, 512], F32))
        PS.append((t, Buf("ps%d" % i)))
    prr = [0]

    def psn(lo=0, hi=8):
        i = lo + prr[0] % (hi - lo)
        prr[0] += 1
        return PS[i]

    def new_phase():
        fa.reset()
        ba.reset()

    def MM(o, l, r, s, t, rd, wr):
        P.op("pe", lambda e: e.matmul(o, lhsT=l, rhs=r, start=s, stop=t), reads=rd, writes=wr)

    def TR(o, i, idn, rd, wr):
        P.op("pe", lambda e: e.transpose(o, i, idn), reads=rd + [B_c], writes=wr)

    def ACT(o, i, f, rd, wr, bias=None, scale=None, accum=None, eng="act"):
        kw = {}
        if bias is not None:
            kw["bias"] = bias
        if scale is not None:
            kw["scale"] = scale
        if accum is not None:
            kw["accum_out"] = accum
        P.op("act", lambda e: e.activation(out=o, in_=i, func=f, **kw), reads=rd, writes=wr)

    def TT(eng, o, a, b, op, rd, wr):
        P.op(eng, lambda e: e.tensor_tensor(out=o, in0=a, in1=b, op=op), reads=rd, writes=wr)

    def TS(eng, o, a, s1, op0, rd, wr, s2=None, op1=None):
        if op1 is None:
            P.op(eng, lambda e: e.tensor_scalar(out=o, in0=a, scalar1=s1, scalar2=None, op0=op0), reads=rd, writes=wr)
        else:
            P.op(eng, lambda e: e.tensor_scalar(out=o, in0=a, scalar1=s1, scalar2=s2, op0=op0, op1=op1),
                 reads=rd, writes=wr)

    def STT(o, a, s, b, op0, op1, rd, wr):
        P.op("dve", lambda e: e.scalar_tensor_tensor(out=o, in0=a, scalar=s, in1=b, op0=op0, op1=op1),
             reads=rd, writes=wr)

    def CP(eng, o, i, rd, wr):
        P.op(eng, lambda e: e.tensor_copy(out=o, in_=i), reads=rd, writes=wr)

    def RCP(o, i, rd, wr):
        P.op("dve", lambda e: e.reciprocal(out=o, in_=i), reads=rd, writes=wr)

    def MS(eng, o, v, wr, partial=False):
        if partial:
            P.op(eng, lambda e: e.memset(o, v), pwrites=wr)
        else:
            P.op(eng, lambda e: e.memset(o, v), writes=wr)

    def srange(base, n, step):
        return slice(base, base + step * (n - 1) + 1, step)

    def LD(o, i, rd, wr, eng="sp"):
        P.op(eng, lambda e: e.dma_start(out=o, in_=i), reads=rd, writes=wr, dma=True)

    sto_eng = ["act"]

    def STO(o, i, rd, pw, eng=None):
        P.op(eng or sto_eng[0], lambda e: e.dma_start(out=o, in_=i), reads=rd, pwrites=pw, dma=True)

    def LDP(o, i, rd, pw, eng="sp"):
        P.op(eng, lambda e: e.dma_start(out=o, in_=i), reads=rd, pwrites=pw, dma=True)

    def rstd_from_ssq(ps_ap, psb, n, tmp, tb, out, ob, width):
        TS("dve", tmp, ps_ap, 1.0 / n, ALU.mult, [psb], [tb], EPS, ALU.add)
        ACT(tmp, tmp, AF.Sqrt, [tb], [tb])
        RCP(out, tmp, [tb], [ob])

    def fm_norm(x3, xb, gcol, sq3, sqb, xn3, xnb, rs, rsb, tmp, tb, src_for_sq=None):
        for k in range(8):
            ACT(sq3[:, k, :], x3[:, k, :], AF.Square, [xb], [sqb])
        pt, pb = psn()
        for k in range(8):
            MM(pt[:, :], ones_b, sq3[:, k, :], k == 0, k == 7, [sqb, B_c], [pb])
        rstd_from_ssq(pt[:, :], pb, float(D), tmp, tb, rs, rsb, 512)
        if xn3 is not None:
            for k in range(8):
                STT(xn3[:, k, :], x3[:, k, :], gains[:, gcol + k:gcol + k + 1], rs, ALU.mult, ALU.mult,
                    [xb, rsb, B_par], [xnb])

    B_wbf = Buf("wbf")
    wbf = {}
    for (wname, wsrc, K_, N_) in (("in", w_in, D, NIN), ("ap", w_ap, 512, D), ("mp", w_mp, D, D), ("o", w_o, D, D),
                                  ("f1", w_f1, D, 2 * FH), ("f2", w_f2, FH, D)):
        wdst = dram("wbf_" + wname, [L, K_, N_], BF16)
        wbf[wname] = wdst
        for l_ in range(L):
            for k0 in range(0, K_ // 128, 8):
                nk = min(8, K_ // 128 - k0)
                for c0 in range(0, N_, 512):
                    ncol = min(512, N_ - c0)
                    t3, tb = load_w(wview(wsrc[l_], k0 * 128, nk, c0, ncol))
                    STO(wview(wdst[l_], k0 * 128, nk, c0, ncol), t3, [tb], [B_wbf])
    w_in, w_ap, w_mp, w_o, w_f1, w_f2 = (wbf[n_] for n_ in ("in", "ap", "mp", "o", "f1", "f2"))
    _wsrc_buf.append(B_wbf)

    class _Stop(Exception):
        pass

    def chk(name):
        _MARKS.append((name, {e_: len(P.lists[e_]) for e_ in ENGS}))
        if dbg == name:
            raise _Stop()

    def _main_body():
        for si, S in enumerate(seqs):
            NT = S // 512
            new_phase()
            xa2 = [fa.carve(4096, "xa") for _ in range(2)]
            xs2 = [fa.carve(4096, "xs")] * 2
            for it in range(NT):
                t0 = it * 512
                xa, xab = xa2[it % 2]
                xs, xsb = xs2[it % 2]
                xa3 = xa.rearrange("p (j c) -> p j c", j=4)
                xs3 = xs.rearrange("p (k t) -> p k t", k=8)
                LD(xa3, xin[si][t0:t0 + 512, :].rearrange("(j p) c -> p j c", p=128), [B_in], [xab])
                for k in range(8):
                    pt, pb = psn()
                    for j in range(4):
                        TR(pt[:, j * 128:(j + 1) * 128], xa3[:, j, k * 128:(k + 1) * 128], ident_f, [xab], [pb])
                    if k % 2:
                        CP("dve", xs3[:, k, :], pt[:, :], [pb], [xsb])
                    else:
                        ACT(xs3[:, k, :], pt[:, :], AF.Copy, [pb], [xsb])
                STO(xT.rearrange("(k p) s -> p k s", p=128)[:, :, t0:t0 + 512], xs3, [xsb], [B_xT])

            chk('pro')
            for l in range(L):
                LD(gains[:], gains_d[l], [], [B_par])
                LD(bconv[:], bconv_d[l], [], [B_par])
                LD(wconv[:], wconv_d[l], [], [B_par])
                LD(gnorm[:], gnorm_d[l:l + 1, :].partition_broadcast(128), [], [B_par])
                LD(gbias[:], gbias_d[l:l + 1, :].partition_broadcast(128), [], [B_par])
                wl = w_in[l]

                sto_eng[0] = 'act'
                new_phase()
                xaA = [fa.carve(4096, "xaA") for _ in range(2)]
                rsA = fa.carve(512, "rsA")
                tmA = fa.carve(512, "tmA")
                gtA = fa.carve(64, "gtA")
                sqA = ba.carve(4096, "sqA")
                xnA = ba.carve(4096, "xnA")
                fmA = [ba.carve(2048, "fmA") for _ in range(2)]
                vaA = ba.carve(4 * 520, "vaA")
                mvA = ba.carve(4 * 1028, "mvA")
                moA = ba.carve(4096, "moA")
                MS("pool", vaA[0], 1.0, [vaA[1]])
                MS("pool", mvA[0], 1.0, [mvA[1]])
                fm_groups = []
                for c0 in range(0, 2 * ng * 512, 512):
                    fm_groups.append((c0, aqkT, c0, B_aqk))
                for c0 in range(0, 2048, 512):
                    fm_groups.append((3584 + c0, mpre, c0, B_mpre))
                for c0 in range(0, 2048, 512):
                    fm_groups.append((7696 + c0, bgT, c0, B_bg))
                fctr = 0
                for it in range(NT):
                    t0 = it * 512
                    xa, xab = xaA[it % 2]
                    x3 = xa.rearrange("p (k t) -> p k t", k=8)
                    LD(x3, xT.rearrange("(k p) s -> p k s", p=128)[:, :, t0:t0 + 512], [B_xT], [xab])
                    sq3 = sqA[0].rearrange("p (k t) -> p k t", k=8)
                    xn3 = xnA[0].rearrange("p (k t) -> p k t", k=8)
                    fm_norm(x3, xab, 0, sq3, sqA[1], xn3, xnA[1], rsA[0], rsA[1], tmA[0], tmA[1])
                    for (wc, dst, r0, dbuf) in fm_groups:
                        w3, wb = load_w(wview(wl, 0, 8, wc, 512))
                        fm, fmb = fmA[fctr % 2]
                        fctr += 1
                        fm3 = fm.rearrange("p (o t) -> p o t", o=4)
                        for oc in range(4):
                            pt, pb = psn()
                            for k in range(8):
                                MM(pt[:, :], w3[:, k, oc * 128:(oc + 1) * 128], xn3[:, k, :], k == 0, k == 7,
                                   [wb, xnA[1]], [pb])
                            if oc % 2:
                                CP("dve", fm3[:, oc, :], pt[:, :], [pb], [fmb])
                            else:
                                ACT(fm3[:, oc, :], pt[:, :], AF.Copy, [pb], [fmb])
                        STO(dst[r0:r0 + 512, t0:t0 + 512].rearrange("(o p) s -> p o s", p=128), fm3, [fmb], [dbuf])
                    va4 = vaA[0].rearrange("p (j h e) -> p j h e", j=4, h=8)
                    mv4 = mvA[0].rearrange("p (j h e) -> p j h e", j=4, h=4)
                    mo3 = moA[0].rearrange("p (j c) -> p j c", j=4)
                    gt3 = gtA[0].rearrange("p (j c) -> p j c", j=4)
                    tm_groups = [(3072, 512, "av", 0), (5632, 512, "mv", 0), (6144, 512, "mv", 1),
                                 (6656, 512, "mo", 0), (7168, 512, "mo", 1), (7680, 16, "mg", 0)]
                    for (wc, ncol, kind, half) in tm_groups:
                        w3, wb = load_w(wview(wl, 0, 8, wc, ncol))
                        for j in range(4):
                            pt, pb = psn()
                            for k in range(8):
                                MM(pt[:, 0:ncol], xn3[:, k, j * 128:(j + 1) * 128], w3[:, k, :], k == 0, k == 7,
                                   [wb, xnA[1]], [pb])
                            if kind == "av":
                                CP("dve", va4[:, j, :, 0:64], pt[:, :].rearrange("p (h e) -> p h e", h=8), [pb], [vaA[1]])
                            elif kind == "mv":
                                ACT(mv4[:, j, 2 * half:2 * half + 2, 0:256], pt[:, :].rearrange("p (h e) -> p h e", h=2),
                                    AF.Copy, [pb], [mvA[1]])
                            elif kind == "mo":
                                CP("dve", mo3[:, j, half * 512:(half + 1) * 512], pt[:, :], [pb], [moA[1]])
                            else:
                                TT("dve", gt3[:, j, :], pt[:, 0:16], gbias[:, :], ALU.add, [pb, B_par], [gtA[1]])
                    rows = lambda d_, w_: d_[t0:t0 + 512, :].rearrange("(j p) c -> p j c", p=128)
                    STO(rows(vaug_d, 520), vaA[0].rearrange("p (j c) -> p j c", j=4), [vaA[1]], [B_vaug])
                    STO(rows(mvaug_d, 1028), mvA[0].rearrange("p (j c) -> p j c", j=4), [mvA[1]], [B_mvaug])
                    STO(rows(mo_d, D), mo3, [moA[1]], [B_mo])
                    STO(rows(gates_d, 16), gt3, [gtA[1]], [B_gates])

                chk('A')
                sto_eng[0] = 'pool'
                new_phase()
                dmax = max(dils)
                qsB = ba.carve(4 * 128 * dmax, "qs")
                ksB = ba.carve(4 * 256 * dmax, "ks")
                vtB = [[ba.carve(520, "vt") for _ in range(2)] for _ in range(2)]
                ptB = [[ba.carve(512, "pt") for _ in range(2)] for _ in range(2)]
                exB = [fa.carve(512, "ex") for _ in range(4)]
                osB = [fa.carve(520, "os") for _ in range(2)]
                bctr = 0
                import os as _os
                _skip = set(_os.environ.get("PB_SKIP", "").split(","))
                for g, d in enumerate(dils):
                    if ("g%d" % g) in _skip:
                        continue
                    sr = S // d
                    nb = sr // 128
                    for b in range(nb):
                        q3 = qsB[0][:, 0:4 * 128 * d].rearrange("p (h t) -> p h t", h=4)
                        k3 = ksB[0][:, 0:4 * 256 * d].rearrange("p (h t) -> p h t", h=4)
                        LD(q3, aqkT[g * 512:(g + 1) * 512, d * 128 * b:d * 128 * (b + 1)].rearrange("(h p) t -> p h t", p=128),
                           [B_aqk], [qsB[1]])
                        klo = d * (128 * b - 64)
                        khi = d * (128 * b + 192)
                        clo, chi = max(klo, 0), min(khi, S)
                        if clo > klo:
                            MS("pool", k3[:, :, 0:clo - klo], 0.0, [ksB[1]], partial=True)
                        if chi < khi:
                            MS("pool", k3[:, :, chi - klo:khi - klo], 0.0, [ksB[1]], partial=True)
                        wrk = {"pwrites": [ksB[1]]}
                        ksrc = aqkT[ng * 512 + g * 512:ng * 512 + (g + 1) * 512, clo:chi].rearrange("(h p) t -> p h t", p=128)
                        kdst = k3[:, :, clo - klo:chi - klo]
                        P.op("sp", lambda e, o=kdst, i=ksrc: e.dma_start(out=o, in_=i), reads=[B_aqk], dma=True, **wrk)
                        for c in range(d):
                            vts = []
                            for j in range(2):
                                m = b + j
                                vt, vtb = vtB[j][bctr % 2]
                                base = c + d * (128 * m - 64)
                                if m == 0:
                                    MS("pool", vt[0:64, :], 0.0, [vtb], partial=True)
                                    LDP(vt[64:128, :], vaug_d[srange(c, 64, d), :], [B_vaug], [vtb])
                                elif m == nb:
                                    MS("pool", vt[64:128, :], 0.0, [vtb], partial=True)
                                    LDP(vt[0:64, :], vaug_d[srange(base, 64, d), :], [B_vaug], [vtb])
                                else:
                                    LDP(vt[:, :], vaug_d[srange(base, 128, d), :], [B_vaug], [vtb])
                                    if bound and (128 * m * d) % bound == 0:
                                        rs_ = slice(0, 64) if j == 0 else slice(64, 128)
                                        TS("pool", vt[rs_, :], vt[rs_, :], keep[rs_, 0:1], ALU.mult, [vtb, B_c], [vtb])
                                vts.append((vt.rearrange("p (h e) -> p h e", h=8), vtb))
                            osb_t, osb = osB[bctr % 2]
                            ptsall = []
                            for hg in range(2):
                                pts = []
                                for j in range(2):
                                    pt_, pb = psn(0, 4)
                                    for hh in range(4):
                                        h = 2 * hh + hg
                                        hp, pp = h // 2, (h % 2) * 64
                                        kk = k3[pp:pp + 64, hp, srange(c + 128 * j * d, 128, d)]
                                        qq = q3[pp:pp + 64, hp, srange(c, 128, d)]
                                        MM(pt_[:, hh * 128:(hh + 1) * 128], kk, qq, True, True, [ksB[1], qsB[1]], [pb])
                                    ex, exb = exB[hg * 2 + j]
                                    ACT(ex, pt_[:, :], AF.Exp, [pb], [exb], scale=0.125)
                                    pT, pTb = ptB[j][hg]
                                    TT("dve", pT, ex, mtab(g, j, hg * 4, 4), ALU.mult, [exb, B_c], [pTb])
                                    pts.append((pT, pTb))
                                ptsall.append(pts)
                            for hg in range(2):
                                pts = ptsall[hg]
                                po, pob = psn(4, 8)
                                for hh in range(4):
                                    h = 2 * hh + hg
                                    for j in range(2):
                                        MM(po[:, hh * 65:(hh + 1) * 65], pts[j][0][:, hh * 128:(hh + 1) * 128],
                                           vts[j][0][:, h, :], j == 0, j == 1, [pts[j][1], vts[j][1]], [pob])
                                os3 = osb_t.rearrange("p (h e) -> p h e", h=8)
                                po3 = po[:, 0:260].rearrange("p (h e) -> p h e", h=4)
                                if hg == 0:
                                    ACT(os3[:, 0:8:2, :], po3, AF.Copy, [pob], [osb])
                                else:
                                    CP("dve", os3[:, 1:8:2, :], po3, [pob], [osb])
                            r0 = c + d * 128 * b
                            STO(og_d[g][srange(r0, 128, d), :], osb_t, [osb], [B_og[g]])
                            bctr += 1

                chk('B')
                new_phase()
                dgC = [ba.carve(640, "dg") for _ in range(2)]
                prC = [ba.carve(516, "pr") for _ in range(3)]
                scC = [ba.carve(512, "sc") for _ in range(3)]
                cctr = 0
                for fc in range(16):
                    dg, dgb = dgC[fc % 2]
                    dg3 = dg.rearrange("p (j c) -> p j c", j=5)
                    for j in range(5):
                        TS("dve", dg3[:, j, :], ident_b, wconv[:, fc * 5 + j:fc * 5 + j + 1], ALU.mult, [B_c, B_par], [dgb])
                    for it in range(NT):
                        t0 = it * 512
                        pr, prb = prC[cctr % 3]
                        sc, scb = scC[cctr % 3]
                        cctr += 1
                        lo, hi = t0 - 2, t0 + 514
                        clo, chi = max(lo, 0), min(hi, S)
                        if clo > lo:
                            MS("pool", pr[:, 0:2], 0.0, [prb], partial=True)
                        if chi < hi:
                            MS("pool", pr[:, 514:516], 0.0, [prb], partial=True)
                        LDP(pr[:, clo - lo:chi - lo], mpre[fc * 128:(fc + 1) * 128, clo:chi], [B_mpre], [prb])
                        if bound and t0 > 0 and t0 % bound == 0:
                            TS("pool", pr[:, 0:2], pr[:, 0:2], keep[:, 0:1], ALU.mult, [prb, B_c], [prb])
                        if bound and t0 + 512 < S and (t0 + 512) % bound == 0:
                            TS("pool", pr[:, 514:516], pr[:, 514:516], keep[:, 0:1], ALU.mult, [prb, B_c], [prb])
                        pt, pb = psn()
                        for j in range(5):
                            MM(pt[:, :], dg3[:, j, :], pr[:, j:j + 512], j == 0, j == 4, [dgb, prb], [pb])
                        ACT(sc, pt[:, :], AF.Silu, [pb, B_par], [scb], bias=bconv[:, fc:fc + 1])
                        STO(mqkT[fc * 128:(fc + 1) * 128, t0:t0 + 512], sc, [scb], [B_mqk])

                chk('C')
                for dr in range(2):
                    new_phase()
                    maskf = maskU_f if dr == 0 else maskL_f
                    qTD2 = [ba.carve(4096, "qTD") for _ in range(2)]
                    kTD2 = [ba.carve(4096, "kTD") for _ in range(2)]
                    ktm = ba.carve(4096, "ktm")
                    mvD2 = [ba.carve(4 * 1028, "mvD") for _ in range(2)]
                    CbD = ba.carve(8 * 257, "Cb")
                    scD = [ba.carve(128, "scT") for _ in range(4)]
                    vtD = [ba.carve(257, "vtl") for _ in range(4)]
                    vpD = [ba.carve(257, "vpr") for _ in range(4)]
                    hst = fa.carve(4096, "hst")
                    CfD = fa.carve(8 * 257, "Cf")
                    gtD2 = [fa.carve(64, "gtD") for _ in range(2)]
                    gsm = {n: fa.carve(16, n) for n in ["e1", "sp", "tmp", "tmp2", "eb", "ebt", "ecum", "etot"]}
                    dnD = [fa.carve(2, "dn") for _ in range(4)]
                    CfH = [CfD[1]] + [Buf("CfH%d" % h_) for h_ in range(1, 4)]
                    CbH = [CbD[1]] + [Buf("CbH%d" % h_) for h_ in range(1, 4)]
                    for h_ in range(1, 4):
                        CfH[h_].r = list(CfD[1].r)
                        CbH[h_].r = list(CbD[1].r)
                        fa.bufs.append(CfH[h_])
                        ba.bufs.append(CbH[h_])
                    MS("pool", CfD[0], 0.0, CfH)
                    MS("pool", CbD[0], 0.0, CbH)
                    Cf3 = CfD[0].rearrange("p (c e) -> p c e", c=8)
                    Cb3 = CbD[0].rearrange("p (c e) -> p c e", c=8)
                    order = list(range(NT)) if dr == 0 else list(range(NT - 1, -1, -1))
                    for it in order:
                        t0 = it * 512
                        if bound and ((dr == 0 and t0 > 0 and t0 % bound == 0) or
                                      (dr == 1 and t0 + 512 < S and (t0 + 512) % bound == 0)):
                            TS("dve", CfD[0], CfD[0], keep[:, 0:1], ALU.mult, CfH + [B_c], CfH)
                            TS("pool", CbD[0], CbD[0], keep[:, 0:1], ALU.mult, CbH + [B_c], CbH)
                        qTD, kTD, mvD, gtD = qTD2[it % 2], kTD2[it % 2], mvD2[it % 2], gtD2[it % 2]
                        q3 = qTD[0].rearrange("p (c t) -> p c t", c=8)
                        k3 = kTD[0].rearrange("p (c t) -> p c t", c=8)
                        LD(q3, mqkT[0:1024, t0:t0 + 512].rearrange("(c p) t -> p c t", p=128), [B_mqk], [qTD[1]])
                        LD(k3, mqkT[1024:2048, t0:t0 + 512].rearrange("(c p) t -> p c t", p=128), [B_mqk], [kTD[1]])
                        mv3 = mvD[0].rearrange("p (j c) -> p j c", j=4)
                        LD(mv3, mvaug_d[t0:t0 + 512, :].rearrange("(j p) c -> p j c", p=128), [B_mvaug], [mvD[1]])
                        gt3 = gtD[0].rearrange("p (j c) -> p j c", j=4)
                        LD(gt3, gates_d[t0:t0 + 512, :].rearrange("(j p) c -> p j c", p=128), [B_gates], [gtD[1]])
                        g3 = {n: v[0].rearrange("p (j c) -> p j c", j=4) for n, v in gsm.items()}
                        gb_ = {n: v[1] for n, v in gsm.items()}
                        li = gt3[:, :, dr * 8:dr * 8 + 4]
                        gf = gt3[:, :, dr * 8 + 4:dr * 8 + 8]
                        ACT(g3["e1"], gf, AF.Exp, [gtD[1]], [gb_["e1"]], scale=-1.0)
                        ACT(g3["sp"], g3["e1"], AF.Ln, [gb_["e1"]], [gb_["sp"]], bias=1.0)
                        pg, pgb = PS[7]
                        for j in range(4):
                            MM(pg[:, j * 4:j * 4 + 4], maskf, g3["sp"][:, j, :], True, True, [B_c, gb_["sp"]], [pgb])
                        for j in range(4):
                            MM(pg[:, 16 + j * 4:16 + j * 4 + 4], ones_f, g3["sp"][:, j, :], True, True, [B_c, gb_["sp"]], [pgb])
                        ncum = pg[:, 0:16].rearrange("p (j c) -> p j c", j=4)
                        ntot = pg[:, 16:32].rearrange("p (j c) -> p j c", j=4)
                        TT("dve", g3["tmp"], li, ncum, ALU.add, [gtD[1], pgb], [gb_["tmp"]])
                        TT("dve", g3["tmp2"], g3["tmp"], ntot, ALU.subtract, [gb_["tmp"], pgb], [gb_["tmp2"]])
                        ACT(g3["eb"], g3["tmp"], AF.Exp, [gb_["tmp"]], [gb_["eb"]], bias=-LN16)
                        ACT(g3["ebt"], g3["tmp2"], AF.Exp, [gb_["tmp2"]], [gb_["ebt"]], bias=-LN16)
                        ACT(g3["ecum"], ncum, AF.Exp, [pgb], [gb_["ecum"]])
                        ACT(g3["etot"], ntot, AF.Exp, [pgb], [gb_["etot"]], scale=-1.0)
                        kt3 = ktm[0].rearrange("p (j c) -> p j c", j=4)
                        for j in range(4):
                            pk, pkb = PS[6]
                            pkb16 = pk[:, :].bitcast(BF16)
                            for c in range(8):
                                TR(pkb16[:, c * 128:(c + 1) * 128], k3[:, c, j * 128:(j + 1) * 128], ident_b, [kTD[1]], [pkb])
                            CP("dve", kt3[:, j, :], pkb16, [pkb], [ktm[1]])
                        h3 = hst[0].rearrange("p (j c) -> p j c", j=4)
                        jorder = list(range(4)) if dr == 0 else [3, 2, 1, 0]
                        for j in jorder:
                            tsl = slice(j * 128, (j + 1) * 128)
                            for h in range(4):
                                mvh = mv3[:, j, h * 257:(h + 1) * 257]
                                ACT(vtD[h][0], mvh, AF.Copy, [mvD[1], gb_["eb"]], [vtD[h][1]], scale=g3["eb"][:, j, h:h + 1])
                                TS("dve", vpD[h][0], mvh, g3["ebt"][:, j, h:h + 1], ALU.mult, [mvD[1], gb_["ebt"]],
                                   [vpD[h][1]])
                            pS, pSb = PS[0]
                            for h in range(4):
                                for c in range(2):
                                    MM(pS[:, h * 128:(h + 1) * 128], k3[:, 2 * h + c, tsl], q3[:, 2 * h + c, tsl],
                                       c == 0, c == 1, [kTD[1], qTD[1]], [pSb])
                            for h in range(4):
                                TT("dve", scD[h][0], pS[:, h * 128:(h + 1) * 128], maskf, ALU.mult, [pSb, B_c], [scD[h][1]])
                            pcs = []
                            for h in range(4):
                                for c in range(2):
                                    pc, pcb = PS[5 + (2 * h + c) % 3]
                                    MM(pc[:, 0:257], kt3[:, j, (2 * h + c) * 128:(2 * h + c + 1) * 128], vpD[h][0], True, True,
                                       [ktm[1], vpD[h][1]], [pcb])
                                    STT(Cf3[:, 2 * h + c, :], Cf3[:, 2 * h + c, :], g3["etot"][:, j, h:h + 1], pc[:, 0:257],
                                        ALU.mult, ALU.add, [CfH[h], gb_["etot"], pcb], [CfH[h]])
                            for h in range(4):
                                ph, phb = PS[1 + h]
                                MM(ph[:, 0:257], scD[h][0], vtD[h][0], True, False, [scD[h][1], vtD[h][1]], [phb])
                                for c in range(2):
                                    MM(ph[:, 0:257], q3[:, 2 * h + c, tsl], Cb3[:, 2 * h + c, :], False, c == 1,
                                       [qTD[1], CbH[h]], [phb])
                            for h in range(4):
                                ACT(Cb3[:, 2 * h:2 * h + 2, :], Cf3[:, 2 * h:2 * h + 2, :], AF.Copy, [CfH[h]], [CbH[h]])
                            for h in range(4):
                                ph, phb = PS[1 + h]
                                dn, dnb = dnD[h]
                                ACT(dn[:, 0:1], ph[:, 256:257], AF.Abs, [phb], [dnb])
                                TS("dve", dn[:, 0:1], dn[:, 0:1], g3["ecum"][:, j, h:h + 1], ALU.max, [dnb, gb_["ecum"]], [dnb])
                                RCP(dn[:, 1:2], dn[:, 0:1], [dnb], [dnb])
                                ACT(h3[:, j, h * 256:(h + 1) * 256], ph[:, 0:256], AF.Copy, [phb, dnb], [hst[1]],
                                    scale=dn[:, 1:2])
                        STO(h_d[dr][t0:t0 + 512, :].rearrange("(j p) c -> p j c", p=128), h3, [hst[1]], [B_h[dr]])

                chk('D')
                sto_eng[0] = 'act'
                new_phase()
                xE = fa.carve(4096, "xE")
                yE = fa.carve(4096, "yE")
                hfE = fa.carve(1024, "hf")
                hbE = fa.carve(1024, "hb")
                sgE = fa.carve(1024, "sg")
                ogE = [fa.carve(520, "og%d" % g) for g in range(ng)]
                smE = fa.carve(32, "smE")
                gaE = fa.carve(512, "ga")
                gmE = fa.carve(512, "gm")
                t1E = fa.carve(512, "t1")
                t2E = fa.carve(512, "t2")
                rsE = fa.carve(512, "rsE")
                tmE = fa.carve(512, "tmE")
                moE = ba.carve(1024, "moE")
                hnE = ba.carve(1024, "hnE")
                aoE = ba.carve(512, "aoE")
                mlT = ba.carve(4096, "mlT")
                atT = ba.carve(2048, "atT")
                bgE = [ba.carve(512, "bgE") for _ in range(2)]
                mgE = ba.carve(4096, "mgE")
                sqE = ba.carve(4096, "sqE")
                xnE = ba.carve(4096, "xnE")
                acE = ba.carve(22 * 512, "acE")
                last = (l == L - 1)
                for it in range(NT):
                    t0 = it * 512
                    x3 = xE[0].rearrange("p (k t) -> p k t", k=8)
                    y3 = yE[0].rearrange("p (k t) -> p k t", k=8)
                    LD(x3, xT.rearrange("(k p) s -> p k s", p=128)[:, :, t0:t0 + 512], [B_xT], [xE[1]])
                    ml3 = mlT[0].rearrange("p (k t) -> p k t", k=8)
                    at3 = atT[0].rearrange("p (k t) -> p k t", k=4)
                    for j in range(4):
                        r0 = t0 + j * 128
                        LD(hfE[0], h_d[0][r0:r0 + 128, :], [B_h[0]], [hfE[1]])
                        LD(hbE[0], h_d[1][r0:r0 + 128, :], [B_h[1]], [hbE[1]])
                        LD(moE[0], mo_d[r0:r0 + 128, :], [B_mo], [moE[1]])
                        TT("dve", hfE[0], hfE[0], hbE[0], ALU.add, [hfE[1], hbE[1]], [hfE[1]])
                        sm = smE[0]
                        for h in range(4):
                            ACT(sgE[0][:, h * 256:(h + 1) * 256], hfE[0][:, h * 256:(h + 1) * 256], AF.Square,
                                [hfE[1]], [sgE[1], smE[1]], accum=sm[:, h:h + 1])
                        rstd_from_ssq(sm[:, 0:4], smE[1], 256.0, sm[:, 4:8], smE[1], sm[:, 8:12], smE[1], 4)
                        ACT(sgE[0], moE[0], AF.Sigmoid, [moE[1]], [sgE[1]])
                        TT("dve", sgE[0], sgE[0], gnorm[:, :], ALU.mult, [sgE[1], B_par], [sgE[1]])
                        for h in range(4):
                            hs = slice(h * 256, (h + 1) * 256)
                            STT(hnE[0][:, hs], hfE[0][:, hs], sm[:, 8 + h:9 + h], sgE[0][:, hs], ALU.mult, ALU.mult,
                                [hfE[1], smE[1], sgE[1]], [hnE[1]])
                        pk, pkb = psn()
                        pk16 = pk[:, :].bitcast(BF16)
                        for k in range(8):
                            TR(pk16[:, k * 128:(k + 1) * 128], hnE[0][:, k * 128:(k + 1) * 128], ident_b, [hnE[1]], [pkb])
                        CP("dve", ml3[:, :, j * 128:(j + 1) * 128], pk16.rearrange("p (k t) -> p k t", k=8), [pkb], [mlT[1]])
                        for g in range(ng):
                            LD(ogE[g][0], og_d[g][r0:r0 + 128, :], [B_og[g]], [ogE[g][1]])
                        for g in range(1, ng):
                            TT("dve", ogE[0][0], ogE[0][0], ogE[g][0], ALU.add, [ogE[0][1], ogE[g][1]], [ogE[0][1]])
                        o3 = ogE[0][0].rearrange("p (h e) -> p h e", h=8)
                        RCP(sm[:, 16:24], o3[:, :, 64], [ogE[0][1]], [smE[1]])
                        ao3 = aoE[0].rearrange("p (h e) -> p h e", h=8)
                        for h in range(8):
                            TS("dve", ao3[:, h, :], o3[:, h, 0:64], sm[:, 16 + h:17 + h], ALU.mult,
                               [ogE[0][1], smE[1]], [aoE[1]])
                        pk, pkb = psn()
                        pk16 = pk[:, :].bitcast(BF16)
                        for k in range(4):
                            TR(pk16[:, k * 128:(k + 1) * 128], aoE[0][:, k * 128:(k + 1) * 128], ident_b, [aoE[1]], [pkb])
                        CP("dve", at3[:, :, j * 128:(j + 1) * 128], pk16[:, 0:512].rearrange("p (k t) -> p k t", k=4),
                           [pkb], [atT[1]])
                    mg3 = mgE[0].rearrange("p (k t) -> p k t", k=8)
                    for cg in range(2):
                        wa3, wab = load_w(wview(w_ap[l], 0, 4, cg * 512, 512))
                        wm3, wmb = load_w(wview(w_mp[l], 0, 8, cg * 512, 512))
                        for oc in range(4):
                            fcx = cg * 4 + oc
                            LD(bgE[0][0], bgT[fcx * 128:(fcx + 1) * 128, t0:t0 + 512], [B_bg], [bgE[0][1]])
                            LD(bgE[1][0], bgT[1024 + fcx * 128:1024 + (fcx + 1) * 128, t0:t0 + 512], [B_bg], [bgE[1][1]])
                            pa, pab = psn()
                            for k in range(4):
                                MM(pa[:, :], wa3[:, k, oc * 128:(oc + 1) * 128], at3[:, k, :], k == 0, k == 3,
                                   [wab, atT[1]], [pab])
                            pm, pmb = psn()
                            for k in range(8):
                                MM(pm[:, :], wm3[:, k, oc * 128:(oc + 1) * 128], ml3[:, k, :], k == 0, k == 7,
                                   [wmb, mlT[1]], [pmb])
                            ACT(gaE[0], bgE[0][0], AF.Sigmoid, [bgE[0][1]], [gaE[1]])
                            ACT(gmE[0], bgE[1][0], AF.Sigmoid, [bgE[1][1]], [gmE[1]])
                            TT("dve", t1E[0], pa[:, :], gaE[0], ALU.mult, [pab, gaE[1]], [t1E[1]])
                            TT("dve", t2E[0], pm[:, :], gmE[0], ALU.mult, [pmb, gmE[1]], [t2E[1]])
                            TT("dve", mg3[:, fcx, :], t1E[0], t2E[0], ALU.add, [t1E[1], t2E[1]], [mgE[1]])

                    def proj_norm_res(w2, nkt, src3, srcb, gcol):
                        kgs = [(k0, min(8, nkt - k0)) for k0 in range(0, nkt, 8)]
                        for cg in range(2):
                            pss = [psn() for _ in range(4)]
                            for gi, (k0, nk) in enumerate(kgs):
                                w3, wb = load_w(wview(w2, k0 * 128, nk, cg * 512, 512))
                                for oc in range(4):
                                    for k in range(nk):
                                        MM(pss[oc][0][:, :], w3[:, k, oc * 128:(oc + 1) * 128], src3[:, k0 + k, :],
                                           gi == 0 and k == 0, gi == len(kgs) - 1 and k == nk - 1, [wb, srcb], [pss[oc][1]])
                            for oc in range(4):
                                ACT(y3[:, cg * 4 + oc, :], pss[oc][0][:, :], AF.Copy, [pss[oc][1]], [yE[1]])
                        sq3 = sqE[0].rearrange("p (k t) -> p k t", k=8)
                        fm_norm(y3, yE[1], gcol, sq3, sqE[1], None, None, rsE[0], rsE[1], tmE[0], tmE[1])
                        for k in range(8):
                            STT(y3[:, k, :], y3[:, k, :], gains[:, gcol + k:gcol + k + 1], rsE[0], ALU.mult, ALU.mult,
                                [yE[1], rsE[1], B_par], [yE[1]])
                            TT("dve", x3[:, k, :], x3[:, k, :], y3[:, k, :], ALU.add, [xE[1], yE[1]], [xE[1]])

                    proj_norm_res(w_o[l], 8, mg3, mgE[1], 8)
                    sq3 = sqE[0].rearrange("p (k t) -> p k t", k=8)
                    xn3 = xnE[0].rearrange("p (k t) -> p k t", k=8)
                    fm_norm(x3, xE[1], 16, sq3, sqE[1], xn3, xnE[1], rsE[0], rsE[1], tmE[0], tmE[1])
                    ac3 = acE[0].rearrange("p (k t) -> p k t", k=22)
                    for i0 in range(0, 22, 4):
                        nci = min(4, 22 - i0)
                        wg3, wgb = load_w(wview(w_f1[l], 0, 8, i0 * 128, nci * 128))
                        wu3, wub = load_w(wview(w_f1[l], 0, 8, FH + i0 * 128, nci * 128))
                        for ii in range(nci):
                            pgt, pgtb = psn()
                            for k in range(8):
                                MM(pgt[:, :], wg3[:, k, ii * 128:(ii + 1) * 128], xn3[:, k, :], k == 0, k == 7,
                                   [wgb, xnE[1]], [pgtb])
                            put, putb = psn()
                            for k in range(8):
                                MM(put[:, :], wu3[:, k, ii * 128:(ii + 1) * 128], xn3[:, k, :], k == 0, k == 7,
                                   [wub, xnE[1]], [putb])
                            ACT(t1E[0], pgt[:, :], AF.Silu, [pgtb], [t1E[1]])
                            TT("dve", ac3[:, i0 + ii, :], put[:, :], t1E[0], ALU.mult, [putb, t1E[1]], [acE[1]])
                    proj_norm_res(w_f2[l], 22, ac3, acE[1], 24)
                    if not last:
                        STO(xT.rearrange("(k p) s -> p k s", p=128)[:, :, t0:t0 + 512], x3, [xE[1]], [B_xT])
                    else:
                        o3_ = yE[0].rearrange("p (j c) -> p j c", j=4)
                        for j in range(4):
                            for kk in range(0, 8, 4):
                                pt, pb = psn()
                                for k in range(kk, kk + 4):
                                    TR(pt[:, (k - kk) * 128:(k - kk + 1) * 128], x3[:, k, j * 128:(j + 1) * 128], ident_f,
                                       [xE[1]], [pb])
                                if kk:
                                    CP("dve", o3_[:, j, kk * 128:(kk + 4) * 128], pt[:, :], [pb], [yE[1]])
                                else:
                                    ACT(o3_[:, j, kk * 128:(kk + 4) * 128], pt[:, :], AF.Copy, [pb], [yE[1]])
                        STO(yout[si][t0:t0 + 512, :].rearrange("(j p) c -> p j c", p=128), o3_, [yE[1]], [B_out])
    try:
        _main_body()
    except _Stop:
        pass
    P.emit()
    st.close()
    return nc


def host_params(norm_mix_pre, norm_mix_post, norm_ffn_pre, norm_ffn_post, b_conv, w_conv):
    L = norm_mix_pre.shape[0]
    fmv = lambda v: np.ascontiguousarray(np.asarray(v, np.float32).reshape(L, -1, 128).transpose(0, 2, 1))
    gains = np.concatenate([fmv(norm_mix_pre), fmv(norm_mix_post), fmv(norm_ffn_pre), fmv(norm_ffn_post)], axis=2)
    bconv = fmv(b_conv)
    wc = np.asarray(w_conv, np.float32)
    wconv = np.ascontiguousarray(wc.reshape(L, 5, 16, 128).transpose(0, 3, 2, 1)).reshape(L, 128, 80)
    return np.ascontiguousarray(gains), bconv, wconv


_CACHE = {}
_MARKS = []


def run(seq_lists, xs_per_core, depth, dils, weights, dbg=None, bound=None, keeps=None):
    key = (tuple(seq_lists), depth, tuple(dils), dbg, bound)
    if key not in _CACHE:
        _CACHE[key] = build(list(seq_lists), depth, tuple(dils), dbg, bound)
    nc = _CACHE[key]
    gains, bconv, wconv = host_params(weights["norm_mix_pre"], weights["norm_mix_post"], weights["norm_ffn_pre"],
                                      weights["norm_ffn_post"], weights["b_conv"], weights["w_conv"])
    common = {
        "w_in": np.ascontiguousarray(weights["w_in"], np.float32),
        "w_attn_proj": np.ascontiguousarray(weights["w_attn_proj"], np.float32),
        "w_mlstm_proj": np.ascontiguousarray(weights["w_mlstm_proj"], np.float32),
        "w_out": np.ascontiguousarray(weights["w_out"], np.float32),
        "w_ffn_in": np.ascontiguousarray(weights["w_ffn_in"], np.float32),
        "w_ffn_out": np.ascontiguousarray(weights["w_ffn_out"], np.float32),
        "gains": gains, "bconv": bconv, "wconv": wconv,
        "gnorm": np.ascontiguousarray(weights["g_mlstm_norm"], np.float32),
        "gbias": np.ascontiguousarray(weights["b_mlstm_gates"], np.float32),
        "consts": make_consts(dils),
    }
    in_maps = []
    for ci, xs in enumerate(xs_per_core):
        m = dict(common)
        m["keep"] = np.full((128, 1), 1.0 if keeps is None else float(keeps[ci]), np.float32)
        for i, x in enumerate(xs):
            m["x%d" % i] = np.ascontiguousarray(x, np.float32)
        in_maps.append(m)
    res = run_bass_kernel_spmd(nc, in_maps, core_ids=list(range(len(in_maps))))
    return res.results


def kernel(x_prompt, x_sample, norm_mix_pre, norm_mix_post, norm_ffn_pre, norm_ffn_post, w_in, b_mlstm_gates,
           w_conv, b_conv, g_mlstm_norm, w_attn_proj, w_mlstm_proj, w_out, w_ffn_in, w_ffn_out):
    weights = dict(norm_mix_pre=norm_mix_pre, norm_mix_post=norm_mix_post, norm_ffn_pre=norm_ffn_pre,
                   norm_ffn_post=norm_ffn_post, w_in=w_in, b_mlstm_gates=b_mlstm_gates, w_conv=w_conv, b_conv=b_conv,
                   g_mlstm_norm=g_mlstm_norm, w_attn_proj=w_attn_proj, w_mlstm_proj=w_mlstm_proj, w_out=w_out,
                   w_ffn_in=w_ffn_in, w_ffn_out=w_ffn_out)
    weights = {k: np.asarray(v) for k, v in weights.items()}
    x_prompt = np.asarray(x_prompt, np.float32)
    x_sample = np.asarray(x_sample, np.float32)
    depth = w_in.shape[0]
    n = 8
    nsamp, sp = x_sample.shape[0], x_sample.shape[1]
    pl = x_prompt.shape[1]
    per = pl // sp
    ngrp = nsamp // per
    xs, keeps = [[x_prompt[0]]], [1.0]
    for gi in range(ngrp):
        xs.append([x_sample[gi * per:(gi + 1) * per].reshape(pl, -1)])
        keeps.append(0.0)
    while len(xs) < n:
        xs.append([np.zeros((pl, x_prompt.shape[2]), np.float32)])
        keeps.append(0.0)
    res = run([pl], xs, depth, (1, 4, 16), weights, bound=sp, keeps=keeps)
    y_prompt = res[0]["y0"][None].astype(np.float32)
    y_sample = np.concatenate([res[1 + gi]["y0"].reshape(per, sp, -1) for gi in range(ngrp)], axis=0).astype(np.float32)
    return (y_prompt, y_sample)
```
